# Optimizing a Trainium2 kernel written in Bass

```python
import math
import jax, jax.numpy as jnp
from jax import lax
import numpy as np

D_MODEL = 1024
BATCH = 32
SEQ = 256
DEPTH = 4
DEC_BATCH = 8
DEC_SEQ = 2048
PAST_LEN = 256

GRID_W = 64
N_MIXERS = 3
N_A_LAYERS = (DEPTH + 2) // 3
N_B_LAYERS = (DEPTH + 1) // 3
N_C_LAYERS = DEPTH // 3

DEEPNORM_ALPHA = (2 * DEPTH) ** 0.25
DEEPNORM_BETA = (8 * DEPTH) ** -0.25
LN_EPS = 1e-5
RMS_EPS = 1e-6
FFN_RES = 0.5
N_MOD = 9

D_FF = 2816

A_D_INNER = 2 * D_MODEL
A_HEAD_DIM = 64
A_N_HEADS = A_D_INNER // A_HEAD_DIM
A_N_GROUPS = 4
A_D_STATE = 128
A_CONV_W = 5
A_CHUNK = 128
A_CONV_DIM = A_D_INNER + 2 * A_N_GROUPS * A_D_STATE
A_IN_DIM = A_D_INNER + A_CONV_DIM + 2 * A_N_HEADS

B_N_HEADS = 8
B_HEAD_K = 128
B_HEAD_V = 256
B_CONV_W = 5
B_CHUNK = 64
B_QK_DIM = B_N_HEADS * B_HEAD_K
B_V_DIM = B_N_HEADS * B_HEAD_V
B_CONV_DIM = 2 * B_QK_DIM + B_V_DIM
B_IN_DIM = B_CONV_DIM + B_V_DIM + 4 * B_N_HEADS

C_N_HEADS = 16
C_N_KV = 4
C_GROUP = C_N_HEADS // C_N_KV
C_HEAD_DIM = 64
C_WINDOW = 128
C_BLOCK = 128
C_IN_DIM = (C_N_HEADS + 2 * C_N_KV) * C_HEAD_DIM
C_SCALE = C_HEAD_DIM ** -0.5
ROPE_BASE = 10000.0

kernel_name = "hybrid_diffusion_prefix_trunk_step"


def layer_norm(x, g, b):
    xf = x.astype(jnp.float32)
    mu = jnp.mean(xf, axis=-1, keepdims=True)
    var = jnp.mean(jnp.square(xf - mu), axis=-1, keepdims=True)
    return ((xf - mu) * lax.rsqrt(var + LN_EPS) * g + b).astype(x.dtype)


def rms_norm(x, g):
    xf = x.astype(jnp.float32)
    return xf * lax.rsqrt(jnp.mean(jnp.square(xf), axis=-1, keepdims=True) + RMS_EPS) * g


def l2_normalize(x):
    xf = x.astype(jnp.float32)
    return xf * lax.rsqrt(jnp.sum(jnp.square(xf), axis=-1, keepdims=True) + RMS_EPS)


def adaln(cond, w, b):
    return (jax.nn.silu(cond) @ w + b).reshape(cond.shape[0], N_MOD, D_MODEL)


def modulate(x, shift, scale):
    return x * (1.0 + scale[:, None]) + shift[:, None]


def swiglu(h, w_gate, w_up, w_down):
    return (jax.nn.silu(h @ w_gate) * (h @ w_up)) @ w_down


def ffn_half(y, mod, s, g, b, w_gate, w_up, w_down):
    h = modulate(y, mod[:, 3 * s], mod[:, 3 * s + 1])
    f = swiglu(h, w_gate, w_up, w_down)
    return layer_norm(DEEPNORM_ALPHA * y + FFN_RES * mod[:, 3 * s + 2][:, None] * f, g, b)


def dwconv_centred(x, w, b):
    k = w.shape[0]
    y = lax.conv_general_dilated(x, w[:, None, :], window_strides=(1,), padding=[(k // 2, k // 2)],
                                 dimension_numbers=('NWC', 'WIO', 'NWC'), feature_group_count=x.shape[-1])
    return y + b


def axial_rope(x):
    f32 = jnp.float32
    L, d = x.shape[1], x.shape[-1]
    half = d // 2
    n_freq = half // 2
    t = jnp.arange(L)
    row = (t // GRID_W).astype(f32)
    col = (t % GRID_W).astype(f32)
    inv_freq = ROPE_BASE ** (-jnp.arange(n_freq, dtype=f32) / n_freq)

    def rot(xa, pos):
        ang = pos[:, None] * inv_freq
        cos = jnp.cos(ang)[None, :, None]
        sin = jnp.sin(ang)[None, :, None]
        x1, x2 = xa[..., :n_freq], xa[..., n_freq:]
        return jnp.concatenate([x1 * cos - x2 * sin, x1 * sin + x2 * cos], axis=-1)

    xf = x.astype(f32)
    return jnp.concatenate([rot(xf[..., :half], row), rot(xf[..., half:], col)], axis=-1).astype(x.dtype)


def ssd_chunked(x, dt, a, bm, cm, h0):
    f32 = jnp.float32
    n, L, H, P = x.shape
    G, N = bm.shape[-2:]
    R = H // G
    Q = A_CHUNK
    nc = L // Q
    x = x.astype(f32).reshape(n, nc, Q, G, R, P)
    dt = dt.astype(f32).reshape(n, nc, Q, G, R)
    bc = bm.astype(f32).reshape(n, nc, Q, G, N)
    cc = cm.astype(f32).reshape(n, nc, Q, G, N)
    acs = jnp.cumsum(dt * a.astype(f32).reshape(G, R), axis=2)
    xdt = x * dt[..., None]
    idx = jnp.arange(Q)
    causal = (idx[:, None] >= idx[None, :])[:, :, None, None]
    seg = acs[:, :, :, None] - acs[:, :, None, :]
    lmat = jnp.exp(jnp.where(causal, seg, -jnp.inf))
    cb = jnp.einsum('bcqgn,bcsgn->bcqsg', cc, bc)
    y_diag = jnp.einsum('bcqsg,bcqsgr,bcsgrp->bcqgrp', cb, lmat, xdt)
    decay_end = jnp.exp(acs[:, :, -1:] - acs)
    states = jnp.einsum('bcsgn,bcsgr,bcsgrp->bcgrpn', bc, decay_end, xdt)
    chunk_decay = jnp.exp(acs[:, :, -1])

    def step(h, inp):
        st, dec = inp
        return h * dec[..., None, None] + st, h

    h_fin, h_prev = lax.scan(step, h0.astype(f32).reshape(n, G, R, P, N),
                             (jnp.moveaxis(states, 1, 0), jnp.moveaxis(chunk_decay, 1, 0)))
    h_prev = jnp.moveaxis(h_prev, 0, 1)
    y_off = jnp.einsum('bcqgn,bcgrpn,bcqgr->bcqgrp', cc, h_prev, jnp.exp(acs))
    return (y_diag + y_off).reshape(n, L, H, P), h_fin.reshape(n, H, P, N)


def ssd_mixer(h, h0, w_in, conv_w, conv_b, dt_bias, a_log, d_skip, norm_g, w_out):
    f32 = jnp.float32
    n, L, _ = h.shape
    proj = h @ w_in
    z, xbc, dt = jnp.split(proj, [A_D_INNER, A_D_INNER + A_CONV_DIM], axis=-1)
    xbc = jax.nn.silu(dwconv_centred(xbc, conv_w, conv_b))
    xs, bm, cm = jnp.split(xbc, [A_D_INNER, A_D_INNER + A_N_GROUPS * A_D_STATE], axis=-1)
    xs = xs.reshape(n, L, A_N_HEADS, A_HEAD_DIM)
    bm = bm.reshape(n, L, A_N_GROUPS, A_D_STATE)
    cm = cm.reshape(n, L, A_N_GROUPS, A_D_STATE)
    dt = jax.nn.softplus(dt.reshape(n, L, 2, A_N_HEADS).astype(f32) + dt_bias)
    a = -jnp.exp(a_log.astype(f32))
    y_f, hf_f = ssd_chunked(xs, dt[:, :, 0], a[0], bm, cm, h0[:, 0])
    y_b, hf_b = ssd_chunked(xs[:, ::-1], dt[:, ::-1, 1], a[1], bm[:, ::-1], cm[:, ::-1], h0[:, 1])
    y = y_f + y_b[:, ::-1] + d_skip[:, None] * xs
    y = rms_norm(y.reshape(n, L, A_D_INNER) * jax.nn.silu(z.astype(f32)), norm_g)
    return y.astype(h.dtype) @ w_out, jnp.stack([hf_f, hf_b], axis=1)


def gdn_chunked(q, k, v, g, beta, s0):
    f32 = jnp.float32
    n, L, H, K = q.shape
    V = v.shape[-1]
    Q = B_CHUNK
    nc = L // Q

    def blocks(t):
        return jnp.moveaxis(t.astype(f32).reshape((n, nc, Q) + t.shape[2:]), 3, 2)

    q, k, v, g, beta = blocks(q), blocks(k), blocks(v), blocks(g), blocks(beta)
    gcs = jnp.cumsum(g, axis=-1)
    idx = jnp.arange(Q)
    incl = idx[:, None] >= idx[None, :]
    strict = idx[:, None] > idx[None, :]
    decay = jnp.exp(jnp.where(incl, gcs[..., :, None] - gcs[..., None, :], -jnp.inf))
    kk = jnp.einsum('bchik,bchjk->bchij', k, k)
    a_mat = jnp.where(strict, beta[..., :, None] * kk * decay, 0.0)
    rhs = jnp.concatenate([v * beta[..., None], k * (beta * jnp.exp(gcs))[..., None]], axis=-1)
    sol = lax.linalg.triangular_solve(a_mat + jnp.eye(Q, dtype=f32), rhs, left_side=True,
                                      lower=True, unit_diagonal=True)
    u, w = sol[..., :V], sol[..., V:]
    qk = jnp.einsum('bchik,bchjk->bchij', q, k) * decay
    q_dec = q * jnp.exp(gcs)[..., None]
    k_dec = k * jnp.exp(gcs[..., -1:] - gcs)[..., None]
    tot = jnp.exp(gcs[..., -1])

    def step(s, inp):
        u_c, w_c, q_c, k_c, t_c = inp
        delta = u_c - jnp.einsum('bhqk,bhkv->bhqv', w_c, s)
        o_inter = jnp.einsum('bhqk,bhkv->bhqv', q_c, s)
        s_new = s * t_c[..., None, None] + jnp.einsum('bhqk,bhqv->bhkv', k_c, delta)
        return s_new, (delta, o_inter)

    xs = (jnp.moveaxis(u, 1, 0), jnp.moveaxis(w, 1, 0), jnp.moveaxis(q_dec, 1, 0),
          jnp.moveaxis(k_dec, 1, 0), jnp.moveaxis(tot, 1, 0))
    s_fin, (delta, o_inter) = lax.scan(step, s0.astype(f32), xs)
    delta = jnp.moveaxis(delta, 0, 1)
    o = jnp.moveaxis(o_inter, 0, 1) + jnp.einsum('bchij,bchjv->bchiv', qk, delta)
    return jnp.moveaxis(o, 2, 3).reshape(n, L, H, V), s_fin


def gdn_mixer(h, s0, w_in, conv_w, conv_b, dt_bias, a_log, norm_g, w_out):
    f32 = jnp.float32
    n, L, _ = h.shape
    proj = h @ w_in
    qkv, z, ab = jnp.split(proj, [B_CONV_DIM, B_CONV_DIM + B_V_DIM], axis=-1)
    qkv = jax.nn.silu(dwconv_centred(qkv, conv_w, conv_b))
    q, k, v = jnp.split(qkv, [B_QK_DIM, 2 * B_QK_DIM], axis=-1)
    q = l2_normalize(q.reshape(n, L, B_N_HEADS, B_HEAD_K)) * (B_HEAD_K ** -0.5)
    k = l2_normalize(k.reshape(n, L, B_N_HEADS, B_HEAD_K))
    v = v.reshape(n, L, B_N_HEADS, B_HEAD_V)
    ab = ab.reshape(n, L, 2, 2, B_N_HEADS).astype(f32)
    beta = jax.nn.sigmoid(ab[:, :, 0])
    g = -jnp.exp(a_log.astype(f32)) * jax.nn.softplus(ab[:, :, 1] + dt_bias)
    o_f, s_f = gdn_chunked(q, k, v, g[:, :, 0], beta[:, :, 0], s0[:, 0])
    o_b, s_b = gdn_chunked(q[:, ::-1], k[:, ::-1], v[:, ::-1], g[:, ::-1, 1], beta[:, ::-1, 1], s0[:, 1])
    o = rms_norm(o_f + o_b[:, ::-1], norm_g) * jax.nn.silu(z.reshape(n, L, B_N_HEADS, B_HEAD_V).astype(f32))
    return o.reshape(n, L, B_V_DIM).astype(h.dtype) @ w_out, jnp.stack([s_f, s_b], axis=1)


def attn_qkv(h, w_in):
    n, L, _ = h.shape
    q, k, v = jnp.split(h @ w_in, [C_N_HEADS * C_HEAD_DIM, (C_N_HEADS + C_N_KV) * C_HEAD_DIM], axis=-1)
    return (q.reshape(n, L, C_N_HEADS, C_HEAD_DIM), k.reshape(n, L, C_N_KV, C_HEAD_DIM),
            v.reshape(n, L, C_N_KV, C_HEAD_DIM))


def sink_logits(sink, n, nq):
    s = sink.astype(jnp.float32).reshape(C_N_KV, C_GROUP)
    return jnp.broadcast_to(s[None, :, :, None, None], (n, C_N_KV, C_GROUP, nq, 1))


def attn_ctx(h, w_in, sink, w_out):
    n, L, _ = h.shape
    q, k, v = attn_qkv(h, w_in)
    q = q.reshape(n, L, C_N_KV, C_GROUP, C_HEAD_DIM)
    sinks = sink_logits(sink, n, C_BLOCK)

    def one_block(i):
        q_blk = lax.dynamic_slice_in_dim(q, i * C_BLOCK, C_BLOCK, axis=1)
        s = jnp.einsum('bqgrd,bkgd->bgrqk', q_blk, k).astype(jnp.float32) * C_SCALE
        p = jax.nn.softmax(jnp.concatenate([s, sinks], axis=-1), axis=-1)
        return jnp.einsum('bgrqk,bkgd->bqgrd', p[..., :L].astype(v.dtype), v)

    o = lax.map(one_block, jnp.arange(L // C_BLOCK))
    o = jnp.moveaxis(o, 0, 1).reshape(n, L, C_N_HEADS * C_HEAD_DIM)
    return o @ w_out, k, v


def attn_lat(h, k_ctx, v_ctx, w_in, sink, w_out):
    n, L, _ = h.shape
    lc = k_ctx.shape[1]
    q, k, v = attn_qkv(h, w_in)
    q = axial_rope(q).reshape(n, L, C_N_KV, C_GROUP, C_HEAD_DIM)
    k = axial_rope(k)
    pad = ((0, 0), (C_WINDOW, C_WINDOW), (0, 0), (0, 0))
    kp, vp = jnp.pad(k, pad), jnp.pad(v, pad)
    span = C_BLOCK + 2 * C_WINDOW
    sinks = sink_logits(sink, n, C_BLOCK)
    k_ctx = k_ctx.astype(k.dtype)
    v_ctx = v_ctx.astype(v.dtype)

    def one_block(i):
        start = i * C_BLOCK
        q_blk = lax.dynamic_slice_in_dim(q, start, C_BLOCK, axis=1)
        k_blk = lax.dynamic_slice_in_dim(kp, start, span, axis=1)
        v_blk = lax.dynamic_slice_in_dim(vp, start, span, axis=1)
        qpos = start + jnp.arange(C_BLOCK)
        kpos = start - C_WINDOW + jnp.arange(span)
        ok = (jnp.abs(qpos[:, None] - kpos[None, :]) <= C_WINDOW) & (kpos >= 0)[None, :] & (kpos < L)[None, :]
        s_loc = jnp.einsum('bqgrd,bkgd->bgrqk', q_blk, k_blk).astype(jnp.float32) * C_SCALE
        s_loc = jnp.where(ok, s_loc, -jnp.inf)
        s_ctx = jnp.einsum('bqgrd,bkgd->bgrqk', q_blk, k_ctx).astype(jnp.float32) * C_SCALE
        p = jax.nn.softmax(jnp.concatenate([s_loc, s_ctx, sinks], axis=-1), axis=-1)
        o = jnp.einsum('bgrqk,bkgd->bqgrd', p[..., :span].astype(v.dtype), v_blk)
        return o + jnp.einsum('bgrqk,bkgd->bqgrd', p[..., span:span + lc].astype(v.dtype), v_ctx)

    o = lax.map(one_block, jnp.arange(L // C_BLOCK))
    o = jnp.moveaxis(o, 0, 1).reshape(n, L, C_N_HEADS * C_HEAD_DIM)
    return o @ w_out


def setup_inputs(seed: int = 0) -> dict:
    key = jax.random.key(seed)
    ks = jax.random.split(key, 40)
    f32 = jnp.float32
    D = D_MODEL

    def nrm(k, shape, scale):
        return jax.random.normal(k, shape, f32) * scale

    def dt_bias_init(k, shape):
        dt = jnp.exp(jax.random.uniform(k, shape, f32) * (math.log(0.1) - math.log(0.001)) + math.log(0.001))
        return dt + jnp.log(-jnp.expm1(-dt))

    def a_log_init(k, shape):
        return jnp.log(jax.random.uniform(k, shape, f32, 1.0, 16.0))

    return {
        "x_prompt": nrm(ks[0], (BATCH, SEQ, D), 1.0),
        "x_sample": nrm(ks[1], (DEC_BATCH, DEC_SEQ, D), 1.0),
        "state_ssd": nrm(ks[2], (DEC_BATCH, N_A_LAYERS, 2, A_N_HEADS, A_HEAD_DIM, A_D_STATE), 0.1),
        "state_delta": nrm(ks[3], (DEC_BATCH, N_B_LAYERS, 2, B_N_HEADS, B_HEAD_K, B_HEAD_V), 0.3),
        "cache_k": nrm(ks[4], (DEC_BATCH, N_C_LAYERS, PAST_LEN, C_N_KV, C_HEAD_DIM), 1.0),
        "cache_v": nrm(ks[5], (DEC_BATCH, N_C_LAYERS, PAST_LEN, C_N_KV, C_HEAD_DIM), 1.0),
        "c": nrm(ks[6], (DEC_BATCH, D), 1.0),
        "c_ctx": nrm(ks[7], (D,), 1.0),
        "w_mod": nrm(ks[8], (DEPTH, D, N_MOD * D), 0.5 * D ** -0.5),
        "b_mod": nrm(ks[9], (DEPTH, N_MOD * D), 0.02),
        "ln_g": 1.0 + nrm(ks[10], (DEPTH, 3, D), 0.02),
        "ln_b": nrm(ks[11], (DEPTH, 3, D), 0.02),
        "ffn_w_gate": nrm(ks[12], (DEPTH, 2, D, D_FF), D ** -0.5),
        "ffn_w_up": nrm(ks[13], (DEPTH, 2, D, D_FF), D ** -0.5),
        "ffn_w_down": nrm(ks[14], (DEPTH, 2, D_FF, D), D_FF ** -0.5 * DEEPNORM_BETA),
        "ssd_w_in": nrm(ks[15], (N_A_LAYERS, D, A_IN_DIM), D ** -0.5),
        "ssd_conv_w": nrm(ks[16], (N_A_LAYERS, A_CONV_W, A_CONV_DIM), A_CONV_W ** -0.5),
        "ssd_conv_b": nrm(ks[17], (N_A_LAYERS, A_CONV_DIM), 0.02),
        "ssd_dt_bias": dt_bias_init(ks[18], (N_A_LAYERS, 2, A_N_HEADS)),
        "ssd_a_log": a_log_init(ks[19], (N_A_LAYERS, 2, A_N_HEADS)),
        "ssd_d": 1.0 + nrm(ks[20], (N_A_LAYERS, A_N_HEADS), 0.02),
        "ssd_norm": 1.0 + nrm(ks[21], (N_A_LAYERS, A_D_INNER), 0.02),
        "ssd_w_out": nrm(ks[22], (N_A_LAYERS, A_D_INNER, D), A_D_INNER ** -0.5 * DEEPNORM_BETA),
        "gdn_w_in": nrm(ks[23], (N_B_LAYERS, D, B_IN_DIM), D ** -0.5),
        "gdn_conv_w": nrm(ks[24], (N_B_LAYERS, B_CONV_W, B_CONV_DIM), B_CONV_W ** -0.5),
        "gdn_conv_b": nrm(ks[25], (N_B_LAYERS, B_CONV_DIM), 0.02),
        "gdn_dt_bias": dt_bias_init(ks[26], (N_B_LAYERS, 2, B_N_HEADS)),
        "gdn_a_log": a_log_init(ks[27], (N_B_LAYERS, 2, B_N_HEADS)),
        "gdn_norm": 1.0 + nrm(ks[28], (N_B_LAYERS, B_HEAD_V), 0.02),
        "gdn_w_out": nrm(ks[29], (N_B_LAYERS, B_V_DIM, D), B_V_DIM ** -0.5 * DEEPNORM_BETA),
        "attn_w_in": nrm(ks[30], (N_C_LAYERS, D, C_IN_DIM), D ** -0.5),
        "attn_sink": nrm(ks[31], (N_C_LAYERS, C_N_HEADS), 1.0),
        "attn_w_out": nrm(ks[32], (N_C_LAYERS, C_N_HEADS * C_HEAD_DIM, D), (C_N_HEADS * C_HEAD_DIM) ** -0.5 * DEEPNORM_BETA),
    }


def reference(x_prompt, x_sample, state_ssd, state_delta, cache_k, cache_v, c, c_ctx,
              w_mod, b_mod, ln_g, ln_b, ffn_w_gate, ffn_w_up, ffn_w_down,
              ssd_w_in, ssd_conv_w, ssd_conv_b, ssd_dt_bias, ssd_a_log, ssd_d, ssd_norm, ssd_w_out,
              gdn_w_in, gdn_conv_w, gdn_conv_b, gdn_dt_bias, gdn_a_log, gdn_norm, gdn_w_out,
              attn_w_in, attn_sink, attn_w_out):
    f32 = jnp.float32

    y = x_prompt
    n_p = x_prompt.shape[0]
    ssd_states, gdn_states, k_list, v_list = [], [], [], []
    for i in range(DEPTH):
        mod = adaln(c_ctx[None], w_mod[i], b_mod[i])
        y = ffn_half(y, mod, 0, ln_g[i, 0], ln_b[i, 0], ffn_w_gate[i, 0], ffn_w_up[i, 0], ffn_w_down[i, 0])
        h = modulate(y, mod[:, 3], mod[:, 4])
        kind, j = i % N_MIXERS, i // N_MIXERS
        if kind == 0:
            h0 = jnp.zeros((n_p, 2, A_N_HEADS, A_HEAD_DIM, A_D_STATE), f32)
            m, st = ssd_mixer(h, h0, ssd_w_in[j], ssd_conv_w[j], ssd_conv_b[j], ssd_dt_bias[j],
                              ssd_a_log[j], ssd_d[j], ssd_norm[j], ssd_w_out[j])
            ssd_states.append(st)
        elif kind == 1:
            s0 = jnp.zeros((n_p, 2, B_N_HEADS, B_HEAD_K, B_HEAD_V), f32)
            m, st = gdn_mixer(h, s0, gdn_w_in[j], gdn_conv_w[j], gdn_conv_b[j], gdn_dt_bias[j],
                              gdn_a_log[j], gdn_norm[j], gdn_w_out[j])
            gdn_states.append(st)
        else:
            m, kc, vc = attn_ctx(h, attn_w_in[j], attn_sink[j], attn_w_out[j])
            k_list.append(kc)
            v_list.append(vc)
        y = layer_norm(DEEPNORM_ALPHA * y + mod[:, 5][:, None] * m, ln_g[i, 1], ln_b[i, 1])
        y = ffn_half(y, mod, 2, ln_g[i, 2], ln_b[i, 2], ffn_w_gate[i, 1], ffn_w_up[i, 1], ffn_w_down[i, 1])
    y_prompt = y
    new_state_ssd = jnp.stack(ssd_states, axis=1)
    new_state_delta = jnp.stack(gdn_states, axis=1)
    new_cache_k = jnp.stack(k_list, axis=1)
    new_cache_v = jnp.stack(v_list, axis=1)

    y = x_sample
    for i in range(DEPTH):
        mod = adaln(c, w_mod[i], b_mod[i])
        y = ffn_half(y, mod, 0, ln_g[i, 0], ln_b[i, 0], ffn_w_gate[i, 0], ffn_w_up[i, 0], ffn_w_down[i, 0])
        h = modulate(y, mod[:, 3], mod[:, 4])
        kind, j = i % N_MIXERS, i // N_MIXERS
        if kind == 0:
            m, _ = ssd_mixer(h, state_ssd[:, j], ssd_w_in[j], ssd_conv_w[j], ssd_conv_b[j], ssd_dt_bias[j],
                             ssd_a_log[j], ssd_d[j], ssd_norm[j], ssd_w_out[j])
        elif kind == 1:
            m, _ = gdn_mixer(h, state_delta[:, j], gdn_w_in[j], gdn_conv_w[j], gdn_conv_b[j], gdn_dt_bias[j],
                             gdn_a_log[j], gdn_norm[j], gdn_w_out[j])
        else:
            m = attn_lat(h, cache_k[:, j], cache_v[:, j], attn_w_in[j], attn_sink[j], attn_w_out[j])
        y = layer_norm(DEEPNORM_ALPHA * y + mod[:, 5][:, None] * m, ln_g[i, 1], ln_b[i, 1])
        y = ffn_half(y, mod, 2, ln_g[i, 2], ln_b[i, 2], ffn_w_gate[i, 1], ffn_w_up[i, 1], ffn_w_down[i, 1])
    y_sample = y

    return (y_prompt, y_sample, new_state_ssd, new_state_delta, new_cache_k, new_cache_v)
```

```python
import numpy as np
import concourse.bass as bass
import concourse.mybir as mybir
from concourse.bass_utils import run_bass_kernel_spmd
from contextlib import ExitStack

F32 = mybir.dt.float32
BF16 = mybir.dt.bfloat16
AF = mybir.ActivationFunctionType
ALU = mybir.AluOpType

D = 1024
KC = 8
DFF = 2816
FC = 22
DEPTH = 4
ALPHA = (2 * DEPTH) ** 0.25
LN_EPS = 1e-5 / (ALPHA * ALPHA)
NCORES = 8
TP = 1024
TS = 2048


class Buf:
    __slots__ = ("w", "r", "name", "excl")

    def __init__(self, name="", excl=False):
        self.w = None
        self.r = {}
        self.name = name
        self.excl = excl


class Sy:
    SAME = {"pe": False, "dve": True, "act": True, "pool": True, "sp": False}

    def __init__(self, nc, es, ndma=12):
        self.nc = nc
        self.E = {"pe": nc.tensor, "dve": nc.vector, "act": nc.scalar, "pool": nc.gpsimd, "sp": nc.sync}
        self.sem = {k: es.enter_context(nc.semaphore("s_" + k)) for k in self.E}
        self.cnt = {k: 0 for k in self.E}
        self.waited = {k: {} for k in self.E}
        self.dsem = {}
        for q in ("sp", "pool"):
            self.dsem[q] = [[es.enter_context(nc.semaphore("d_%s%d" % (q, i))), 0] for i in range(ndma)]
        self.drr = {q: 0 for q in self.dsem}
        self.ninst = 0

    def _wait(self, e, tok):
        sem, val, src = tok
        if src == e and not self.SAME[e]:
            return
        k = sem.num
        if self.waited[e].get(k, 0) >= val:
            return
        self.E[e].wait_ge(sem, val)
        self.waited[e][k] = val

    def _deps(self, e, reads, writes):
        for b in reads:
            if b.w is not None:
                self._wait(e, b.w)
        for b in writes:
            if b.w is not None:
                self._wait(e, b.w)
            for t in b.r.values():
                self._wait(e, t)

    def _commit(self, tok, key, reads, writes):
        for b in writes:
            b.w = tok
            b.r = {}
        for b in reads:
            if b not in writes:
                b.r[key] = tok

    def op(self, e, fn, r=(), w=()):
        ex = [b for b in r if b.excl]
        if ex:
            w = list(w) + [b for b in ex if b not in w]
        self._deps(e, r, w)
        ins = fn(self.E[e])
        self.cnt[e] += 1
        ins.then_inc(self.sem[e], 1)
        self._commit((self.sem[e], self.cnt[e], e), e, r, w)
        self.ninst += 1
        return ins

    def dma(self, q, out, in_, r=(), w=()):
        slot = self.drr[q] % len(self.dsem[q])
        self.drr[q] += 1
        ent = self.dsem[q][slot]
        sem = ent[0]
        if ent[1] > 0:
            self._wait(q, (sem, ent[1], "dma"))
        self._deps(q, r, w)
        ins = self.E[q].dma_start(out=out, in_=in_)
        ent[1] += 16
        ins.then_inc(sem, 16)
        self._commit((sem, ent[1], "dma"), (q, slot), r, w)
        self.ninst += 1
        return ins

    def mark(self, name):
        if not hasattr(self, "marks"):
            self.marks = []
        d = dict(self.cnt)
        d["pe_slices"] = getattr(self, "pe_slices", 0)
        self.marks.append((name, d))

    def barrier(self):
        for e in self.E:
            for e2 in self.E:
                if self.cnt[e2] > 0 and (e2 != e or self.SAME[e]):
                    self._wait(e, (self.sem[e2], self.cnt[e2], e2))
            for q in self.dsem:
                for ent in self.dsem[q]:
                    if ent[1] > 0:
                        self._wait(e, (ent[0], ent[1], "dma"))

    def final_wait(self):
        e = "sp"
        for e2 in self.E:
            if e2 != e and self.cnt[e2] > 0:
                self._wait(e, (self.sem[e2], self.cnt[e2], e2))
        for q in self.dsem:
            for ent in self.dsem[q]:
                if ent[1] > 0:
                    self._wait(e, (ent[0], ent[1], "dma"))


class K:
    def __init__(self, nc, es, S):
        self.nc = nc
        self.es = es
        self.S = S
        self.ps = []
        for i in range(8):
            t = es.enter_context(nc.psum_tensor("psb%d" % i, [128, 512], F32))
            self.ps.append((t, Buf("ps%d" % i, excl=True)))
        self.prr = 0

    def psum(self):
        p = self.ps[self.prr % 8]
        self.prr += 1
        return p

    def sb(self, es, name, shape, dt):
        self.uid = getattr(self, "uid", 0) + 1
        name = "%s_%d" % (name, self.uid)
        t = es.enter_context(self.nc.sbuf_tensor(name, shape, dt))
        return t, Buf(name)

    def mm(self, out, lhsT, rhs, start, stop, r, w):
        self.S.pe_slices = getattr(self.S, "pe_slices", 0) + (2 if lhsT.dtype == F32 else 1)
        return self.S.op("pe", lambda e: e.matmul(out, lhsT, rhs, start=start, stop=stop), r=r, w=w)

    def tr(self, out, in_, ident, r, w):
        self.S.pe_slices = getattr(self.S, "pe_slices", 0) + 1
        return self.S.op("pe", lambda e: e.transpose(out, in_, ident), r=r, w=w)

    def ts(self, eng, out, in0, s1, s2, op0, op1, r, w):
        return self.S.op(eng, lambda e: e.tensor_scalar(out, in0, s1, s2, op0, op1), r=r, w=w)

    def stt(self, eng, out, in0, scalar, in1, op0, op1, r, w):
        eng = "dve"
        return self.S.op(eng, lambda e: e.scalar_tensor_tensor(out, in0, scalar, in1, op0, op1), r=r, w=w)

    def tt(self, eng, out, in0, in1, op, r, w):
        return self.S.op(eng, lambda e: e.tensor_tensor(out, in0, in1, op), r=r, w=w)

    def cp(self, eng, out, in_, r, w):
        if eng == "act":
            return self.S.op("act", lambda e: e.copy(out, in_), r=r, w=w)
        return self.S.op(eng, lambda e: e.tensor_copy(out, in_), r=r, w=w)

    def act(self, out, in_, func, r, w, bias=None, scale=None):
        kw = {}
        if bias is not None:
            kw["bias"] = bias
        if scale is not None:
            kw["scale"] = scale
        return self.S.op("act", lambda e: e.activation(out, in_, func, **kw), r=r, w=w)

    def memset(self, eng, ap, val, w):
        return self.S.op(eng, lambda e: e.memset(ap, val), w=w)


def emit_load_cols(k, dst_ap, dst_buf, src2d, R, C, Wd=128):
    S = k.S
    with ExitStack() as es:
        st, stb = k.sb(es, "lc_st", [128, 128], F32)
        S.dma("sp", st[0:R, 0:Wd], src2d, w=[stb])
        pt, pb = k.psum()
        k.tr(pt[0:Wd, 0:R], st[0:R, 0:Wd], C["ident"][0:R, 0:R], r=[stb, C["b"]], w=[pb])
        k.cp("dve", dst_ap, pt[0:Wd, 0:R], r=[pb], w=[dst_buf])
        S.barrier()


def emit_adaln(k, C, W, condT, condb, modT, modb, l):
    S = k.S
    with ExitStack() as es:
        bm, bmb = k.sb(es, "ad_bm", [128, 72], F32)
        emit_load_cols(k, bm[:, :], bmb, W["b_mod"][l].rearrange("(r p) -> r p", p=128), 72, C)
        wm = [k.sb(es, "ad_w%d" % i, [128, KC, 512], BF16) for i in range(2)]
        pm, pmb = k.psum()
        pmv = pm[:, 0:144].rearrange("p (j g) -> p j g", g=2)
        for blk in range(18):
            wt, wb = wm[blk % 2]
            S.dma("pool", wt[:, :, :],
                  W["w_mod"][l][:, blk * 512:(blk + 1) * 512].rearrange("(kc p) o -> p kc o", p=128), w=[wb])
            for j4 in range(4):
                j = blk * 4 + j4
                for kc in range(KC):
                    k.mm(pmv[:, j, :], wt[:, kc, j4 * 128:(j4 + 1) * 128], condT[:, kc, :],
                         start=(kc == 0), stop=(kc == KC - 1), r=[wb, condb], w=[pmb])
        for g in range(2):
            k.tt("dve", modT[:, l, g, :], pmv[:, :, g], bm[:, :], ALU.add, r=[pmb, bmb], w=[modb])
        S.barrier()


def emit_ln(k, C, yT, yb, t0, T, gcol, bcol, cb):
    S = k.S
    with ExitStack() as es:
        sq = [k.sb(es, "ln_sq%d" % i, [128, 512], F32) for i in range(2)]
        hl = [[k.sb(es, "ln_hl%d_%d" % (j, i), [128, 512], BF16) for i in range(2)] for j in range(4)]
        mean, meanb = k.sb(es, "ln_mean", [128, 512], F32)
        rstd, rstdb = k.sb(es, "ln_rstd", [128, 512], F32)
        tmp = [k.sb(es, "ln_tmp%d" % i, [128, 512], F32) for i in range(2)]
        ybs = [Buf("ln_y%d" % i) for i in range(KC)]
        for tg in range(T // 512):
            sl = slice(t0 + tg * 512, t0 + (tg + 1) * 512)
            p1, p1b = k.psum()
            p2, p2b = k.psum()
            for kc in range(KC):
                q, qb = sq[kc % 2]
                hi, hib = hl[0][kc % 2]
                lo, lob = hl[1][kc % 2]
                qhi, qhib = hl[2][kc % 2]
                qlo, qlob = hl[3][kc % 2]
                zc = yT[:, kc, sl]
                k.cp("act", hi[:, :], zc, r=[yb], w=[hib])
                k.act(q[:, :], zc, AF.Square, r=[yb], w=[qb])
                k.tt("dve", lo[:, :], zc, hi[:, :], ALU.subtract, r=[yb, hib], w=[lob])
                k.cp("pool", qhi[:, :], q[:, :], r=[qb], w=[qhib])
                k.tt("dve", qlo[:, :], q[:, :], qhi[:, :], ALU.subtract, r=[qb, qhib], w=[qlob])
                k.mm(p1[:, :], C["onesb"][:, :], hi[:, :], start=(kc == 0), stop=False, r=[hib, C["b"]], w=[p1b])
                k.mm(p1[:, :], C["onesb"][:, :], lo[:, :], start=False, stop=(kc == KC - 1), r=[lob, C["b"]], w=[p1b])
                k.mm(p2[:, :], C["onesb"][:, :], qhi[:, :], start=(kc == 0), stop=False, r=[qhib, C["b"]], w=[p2b])
                k.mm(p2[:, :], C["onesb"][:, :], qlo[:, :], start=False, stop=(kc == KC - 1), r=[qlob, C["b"]], w=[p2b])
            k.ts("dve", mean[:, :], p1[:, :], 1.0 / D, None, ALU.mult, ALU.bypass, r=[p1b], w=[meanb])
            t, tb = tmp[0]
            k.tt("dve", t[:, :], mean[:, :], mean[:, :], ALU.mult, r=[meanb], w=[tb])
            k.stt("dve", rstd[:, :], p2[:, :], 1.0 / D, t[:, :], ALU.mult, ALU.subtract, r=[p2b, tb], w=[rstdb])
            k.ts("dve", rstd[:, :], rstd[:, :], LN_EPS, None, ALU.add, ALU.bypass, r=[rstdb], w=[rstdb])
            k.act(rstd[:, :], rstd[:, :], AF.Sqrt, r=[rstdb], w=[rstdb])
            k.S.op("dve", lambda e: e.reciprocal(rstd[:, :], rstd[:, :]), r=[rstdb], w=[rstdb])
            for kc in range(KC):
                t, tb = tmp[kc % 2]
                k.tt("dve", t[:, :], yT[:, kc, sl], mean[:, :], ALU.subtract, r=[yb, meanb], w=[tb])
                k.tt("pool" if kc % 2 else "dve", t[:, :], t[:, :], rstd[:, :], ALU.mult, r=[tb, rstdb], w=[tb])
                k.act(yT[:, kc, sl], t[:, :], AF.Identity, r=[tb, cb], w=[ybs[kc]],
                      bias=bcol[:, kc:kc + 1], scale=gcol[:, kc:kc + 1])
        S.barrier()


def emit_ffn(k, C, W, yT, yb, t0, l, s, mod, modb, lncols, lnb):
    S = k.S
    T = 1024
    with ExitStack() as es:
        hT, hb = k.sb(es, "f_hT", [128, KC, T], BF16)
        aT, ab = k.sb(es, "f_aT", [128, FC, T], BF16)
        for kc in range(KC):
            if kc % 2 == 0:
                k.ts("dve", hT[:, kc, :], yT[:, kc, t0:t0 + T],
                     mod["sc1"][:, kc:kc + 1], mod["shift"][:, kc:kc + 1], ALU.mult, ALU.add, r=[yb, modb], w=[hb])
            else:
                k.act(hT[:, kc, :], yT[:, kc, t0:t0 + T], AF.Identity, r=[yb, modb], w=[hb],
                      bias=mod["shift"][:, kc:kc + 1], scale=mod["sc1"][:, kc:kc + 1])
        with ExitStack() as es2:
            NB = 2
            wg = [k.sb(es2, "f_wg%d" % i, [128, KC, 256], BF16) for i in range(NB)]
            wu = [k.sb(es2, "f_wu%d" % i, [128, KC, 256], BF16) for i in range(NB)]
            sg = [k.sb(es2, "f_sg%d" % i, [128, 512], F32) for i in range(2)]

            def load(gi):
                c0 = gi * 256
                S.dma("pool", wg[gi % NB][0][:, :, :],
                      W["ffn_w_gate"][l, s][:, c0:c0 + 256].rearrange("(kc p) o -> p kc o", p=128),
                      w=[wg[gi % NB][1]])
                S.dma("pool", wu[gi % NB][0][:, :, :],
                      W["ffn_w_up"][l, s][:, c0:c0 + 256].rearrange("(kc p) o -> p kc o", p=128),
                      w=[wu[gi % NB][1]])

            wd = [k.sb(es2, "f_wd%d" % i, [128, FC, 256], BF16) for i in range(2)]

            def loadd(gi):
                c0 = gi * 256
                S.dma("pool", wd[gi % 2][0][:, :, :],
                      W["ffn_w_down"][l, s][:, c0:c0 + 256].rearrange("(kc p) o -> p kc o", p=128),
                      w=[wd[gi % 2][1]])

            load(0)
            n = 0
            for gi in range(FC // 2):
                if gi + 1 < FC // 2:
                    load(gi + 1)
                else:
                    loadd(0)
                    loadd(1)
                wgt, wgb = wg[gi % NB]
                wut, wub = wu[gi % NB]
                for o2 in range(2):
                    oc = gi * 2 + o2
                    for tg in range(T // 512):
                        sl = slice(tg * 512, (tg + 1) * 512)
                        pg, pgb = k.psum()
                        pu, pub = k.psum()
                        for kc in range(KC):
                            k.mm(pg[:, :], wgt[:, kc, o2 * 128:(o2 + 1) * 128], hT[:, kc, sl],
                                 start=(kc == 0), stop=(kc == KC - 1), r=[wgb, hb], w=[pgb])
                        for kc in range(KC):
                            k.mm(pu[:, :], wut[:, kc, o2 * 128:(o2 + 1) * 128], hT[:, kc, sl],
                                 start=(kc == 0), stop=(kc == KC - 1), r=[wub, hb], w=[pub])
                        st, stb = sg[n % 2]
                        n += 1
                        k.act(st[:, :], pg[:, :], AF.Silu, r=[pgb], w=[stb])
                        k.tt("dve", aT[:, oc, sl], st[:, :], pu[:, :], ALU.mult, r=[stb, pub], w=[ab])
            for gi in range(4):
                if 1 <= gi < 3:
                    loadd(gi + 1)
                wdt, wdb = wd[gi % 2]
                for o2 in range(2):
                    oc = gi * 2 + o2
                    for tg in range(T // 512):
                        sl = slice(tg * 512, (tg + 1) * 512)
                        ysl = slice(t0 + tg * 512, t0 + (tg + 1) * 512)
                        pf, pfb = k.psum()
                        for kc in range(FC):
                            k.mm(pf[:, :], wdt[:, kc, o2 * 128:(o2 + 1) * 128], aT[:, kc, sl],
                                 start=(kc == 0), stop=(kc == FC - 1), r=[wdb, ab], w=[pfb])
                        k.stt("dve", yT[:, oc, ysl], pf[:, :], mod["gate"][:, oc:oc + 1], yT[:, oc, ysl],
                              ALU.mult, ALU.add, r=[pfb, yb, modb], w=[yb])
            S.barrier()
    emit_ln(k, C, yT, yb, t0, T, lncols[0], lncols[1], lnb)


def emit_load_x(k, C, yT, yb, xsrc, T):
    S = k.S
    with ExitStack() as es:
        xt = [k.sb(es, "lx%d" % i, [128, D], F32) for i in range(2)]
        for tt in range(T // 128):
            x, xb = xt[tt % 2]
            S.dma("sp", x[:, :], xsrc[tt * 128:(tt + 1) * 128, :], w=[xb])
            for h2 in range(2):
                pt, pb = k.psum()
                for c4 in range(4):
                    kc = h2 * 4 + c4
                    k.tr(pt[:, c4 * 128:(c4 + 1) * 128], x[:, kc * 128:(kc + 1) * 128], C["ident"][:, :],
                         r=[xb, C["b"]], w=[pb])
                k.cp("act" if h2 else "dve",
                     yT[:, h2 * 4:(h2 + 1) * 4, tt * 128:(tt + 1) * 128],
                     pt[:, :].rearrange("p (c t) -> p c t", t=128), r=[pb], w=[yb])
        S.barrier()


def emit_store_y(k, C, yT, yb, ydst, T):
    S = k.S
    with ExitStack() as es:
        ot = [k.sb(es, "sy%d" % i, [128, D], F32) for i in range(2)]
        for tt in range(T // 128):
            o, ob = ot[tt % 2]
            for h2 in range(2):
                pt, pb = k.psum()
                for c4 in range(4):
                    kc = h2 * 4 + c4
                    k.tr(pt[:, c4 * 128:(c4 + 1) * 128], yT[:, kc, tt * 128:(tt + 1) * 128], C["ident"][:, :],
                         r=[yb, C["b"]], w=[pb])
                k.cp("act" if h2 else "dve", o[:, h2 * 512:(h2 + 1) * 512], pt[:, :], r=[pb], w=[ob])
            S.dma("sp", ydst[tt * 128:(tt + 1) * 128, :], o[:, :], r=[ob])
        S.barrier()


NEG = -30000.0
LAST_MARKS = None
DBG = {"attn_stop": 9}
C_SCALE = 64 ** -0.5


def host_consts():
    c = {}
    c["ident"] = np.eye(128, dtype=np.float32)
    c["ones"] = np.ones((128, 128), dtype=np.float32)
    ii = np.arange(128)[:, None]
    jj = np.arange(128)[None, :]
    c["mprev"] = np.where(jj >= ii, 0.0, NEG).astype(np.float32)
    c["mnext"] = np.where(jj <= ii, 0.0, NEG).astype(np.float32)
    t = np.arange(TS)
    row = (t // 64).astype(np.float32)
    col = (t % 64).astype(np.float32)
    inv = (10000.0 ** (-np.arange(16, dtype=np.float32) / 16)).astype(np.float32)
    cos = np.zeros((64, TS), np.float32)
    sin = np.zeros((64, TS), np.float32)
    perm = np.zeros((128, 128), np.float32)
    for d in range(64):
        pos = row if d < 32 else col
        f = inv[d % 16]
        ang = (pos * f).astype(np.float32)
        cos[d] = np.cos(ang)
        first = (d % 32) < 16
        sin[d] = -np.sin(ang) if first else np.sin(ang)
        partner = d + 16 if first else d - 16
        perm[partner, d] = 1.0
    c["tri_f"] = (ii <= jj).astype(np.float32)
    c["tri_b"] = (ii >= jj).astype(np.float32)
    sel = np.zeros((1, 256), np.float32)
    sel[0, 0:64] = 1.0
    sel[0, 128 + 64:256] = 1.0
    c["sel"] = sel
    same = (ii // 64) == (jj // 64)
    c["gtri_f"] = (same & (ii <= jj)).astype(np.float32)
    c["gtri_b"] = (same & (ii >= jj)).astype(np.float32)
    c["gnmsl_f"] = -(same & (jj < ii)).astype(np.float32)
    c["gnmsl_b"] = -(same & (jj > ii)).astype(np.float32)
    c["gblk"] = same.astype(np.float32)
    gch = np.zeros((128, 128), np.float32)
    gch[0:64, 0] = 1.0
    gch[64:128, 1] = 1.0
    c["gch"] = gch
    c["cos"] = cos
    c["sin"] = sin
    c["perm"] = perm
    return c


def emit_mixer_tail(k, C, wout, KCI, inT, inb, yT, yb, T, gate5, modb, lncols, lnb):
    S = k.S
    S.mark("  mixer tail")
    with ExitStack() as es:
        wo = [k.sb(es, "mt_w%d" % i, [128, KCI, 256], BF16) for i in range(2)]

        def load(gi):
            S.dma("pool", wo[gi % 2][0][:, :, :],
                  wout[:, gi * 256:(gi + 1) * 256].rearrange("(kc p) o -> p kc o", p=128), w=[wo[gi % 2][1]])

        load(0)
        for gi in range(4):
            if gi + 1 < 4:
                load(gi + 1)
            wt, wb = wo[gi % 2]
            for o2 in range(2):
                oc = gi * 2 + o2
                for tg in range(T // 512):
                    sl = slice(tg * 512, (tg + 1) * 512)
                    pf, pfb = k.psum()
                    for kc in range(KCI):
                        k.mm(pf[:, :], wt[:, kc, o2 * 128:(o2 + 1) * 128], inT[:, kc, sl],
                             start=(kc == 0), stop=(kc == KCI - 1), r=[wb, inb], w=[pfb])
                    k.stt("dve", yT[:, oc, sl], pf[:, :], gate5[:, oc:oc + 1], yT[:, oc, sl],
                          ALU.mult, ALU.add, r=[pfb, yb, modb], w=[yb])
        S.barrier()
    for t0 in range(0, T, 1024):
        emit_ln(k, C, yT, yb, t0, 1024, lncols[0], lncols[1], lnb)


def emit_modulate(k, hT, hb, yT, yb, T, mod, modb):
    for kc in range(KC):
        if kc % 2 == 0:
            k.ts("dve", hT[:, kc, 0:T], yT[:, kc, 0:T],
                 mod["sc1"][:, kc:kc + 1], mod["shift"][:, kc:kc + 1], ALU.mult, ALU.add, r=[yb, modb], w=[hb])
        else:
            k.act(hT[:, kc, 0:T], yT[:, kc, 0:T], AF.Identity, r=[yb, modb], w=[hb],
                  bias=mod["shift"][:, kc:kc + 1], scale=mod["sc1"][:, kc:kc + 1])


def emit_attn(k, C, W, O, yT, yb, T, grp, mod, modb, lncols, lnb):
    S = k.S
    w_in = W["attn_w_in"][0]
    NT = T // 128
    with ExitStack() as es:
        if grp == 1:
            for nm in ("cos", "sin"):
                t_, _ = k.sb(es, "k_" + nm, [64, TS], F32)
                C[nm] = t_
                S.dma("sp", t_[:, :], W["c_" + nm][:, :], w=[C["b"]])
        OT, OTb = k.sb(es, "at_OT", [128, 8, T], BF16)
        sinkbc, sinkb = k.sb(es, "at_sink", [128, 16], F32)
        with ExitStack() as es1:
            s1, s1b = k.sb(es1, "at_s1", [1, 16], F32)
            S.dma("sp", s1[0:1, :], W["attn_sink"][0:1, :], w=[s1b])
            pt, pb = k.psum()
            k.mm(pt[:, 0:16], C["ones"][0:1, :], s1[0:1, :], start=True, stop=True, r=[s1b, C["b"]], w=[pb])
            k.cp("dve", sinkbc[:, :], pt[:, 0:16], r=[pb], w=[sinkb])
            S.barrier()
        with ExitStack() as es1:
            hts = [k.sb(es1, "at_hT%d" % i, [128, KC, 512], BF16) for i in range(2)]
            hcnt = [0]

            def mod_tile(tg):
                hT_, hb_ = hts[hcnt[0] % 2]
                hcnt[0] += 1
                for kc in range(KC):
                    if kc % 2 == 0:
                        k.ts("dve", hT_[:, kc, :], yT[:, kc, tg * 512:(tg + 1) * 512],
                             mod["sc1"][:, kc:kc + 1], mod["shift"][:, kc:kc + 1], ALU.mult, ALU.add,
                             r=[yb, modb], w=[hb_])
                    else:
                        k.act(hT_[:, kc, :], yT[:, kc, tg * 512:(tg + 1) * 512], AF.Identity, r=[yb, modb], w=[hb_],
                              bias=mod["shift"][:, kc:kc + 1], scale=mod["sc1"][:, kc:kc + 1])
                return hT_, hb_
            vtok, vtb = k.sb(es1, "at_vtok", [128, NT, 256], BF16)
            with ExitStack() as es2:
                wkv, wkvb = k.sb(es2, "at_wkv", [128, KC, 512], BF16)
                S.dma("pool", wkv[:, :, :], w_in[:, 1024:1536].rearrange("(kc p) o -> p kc o", p=128), w=[wkvb])
                st = [k.sb(es2, "at_kvst%d" % i, [128, 512], F32) for i in range(2)]
                for tt in range(NT):
                    if tt % 4 == 0:
                        hT, hb = mod_tile(tt // 4)
                    pk, pkb = k.psum()
                    for kc in range(KC):
                        k.mm(pk[:, :], hT[:, kc, (tt % 4) * 128:(tt % 4 + 1) * 128], wkv[:, kc, :],
                             start=(kc == 0), stop=(kc == KC - 1), r=[hb, wkvb], w=[pkb])
                    k.cp("act", vtok[:, tt, :], pk[:, 256:512], r=[pkb], w=[vtb])
                    if grp == 0:
                        s_, sb_ = st[tt % 2]
                        k.cp("dve", s_[:, :], pk[:, :], r=[pkb], w=[sb_])
                        S.dma("sp", O["new_cache_k"][tt * 128:(tt + 1) * 128, :], s_[:, 0:256], r=[sb_])
                        S.dma("sp", O["new_cache_v"][tt * 128:(tt + 1) * 128, :], s_[:, 256:512], r=[sb_])
                S.barrier()
            NCTX = 0
            if grp == 1:
                NCTX = 2
                ckT, ckb = k.sb(es1, "at_ckT", [64, 4, 256], BF16)
                cvp, cvb = k.sb(es1, "at_cvp", [128, 2, 4, 2, 128], BF16)
                k.memset("pool", cvp[:, :, :, :, :], 0.0, w=[cvb])
                with ExitStack() as es2:
                    ck, ckfb = k.sb(es2, "at_ck", [128, 2, 256], F32)
                    cv, cvfb = k.sb(es2, "at_cv", [128, 2, 256], F32)
                    S.dma("sp", ck[:, :, :], W["cache_k"].rearrange("(t p) f -> p t f", p=128), w=[ckfb])
                    S.dma("sp", cv[:, :, :], W["cache_v"].rearrange("(t p) f -> p t f", p=128), w=[cvfb])
                    for tt in range(2):
                        for g in range(4):
                            pt, pb = k.psum()
                            k.tr(pt[0:64, 0:128], ck[:, tt, g * 64:(g + 1) * 64], C["ident"][:, :],
                                 r=[ckfb, C["b"]], w=[pb])
                            k.cp("dve", ckT[:, g, tt * 128:(tt + 1) * 128], pt[0:64, 0:128], r=[pb], w=[ckb])
                            k.cp("act", cvp[:, tt, g, 0, 0:64], cv[:, tt, g * 64:(g + 1) * 64], r=[cvfb], w=[cvb])
                            k.cp("pool", cvp[:, tt, g, 1, 64:128], cv[:, tt, g * 64:(g + 1) * 64], r=[cvfb], w=[cvb])
                    S.barrier()
            for g in range(4 if DBG["attn_stop"] > 1 else 0):
                with ExitStack() as es2:
                    qT, qb = k.sb(es2, "at_qT", [64, 4, T], BF16)
                    kT, kb = k.sb(es2, "at_kT", [64, T], BF16)
                    vp, vpb = k.sb(es2, "at_vp", [128, NT, 2, 128], BF16)
                    k.memset("pool", vp[:, :, :, :], 0.0, w=[vpb])
                    for tt in range(NT):
                        k.cp("act", vp[:, tt, 0, 0:64], vtok[:, tt, g * 64:(g + 1) * 64], r=[vtb], w=[vpb])
                        k.cp("pool", vp[:, tt, 1, 64:128], vtok[:, tt, g * 64:(g + 1) * 64], r=[vtb], w=[vpb])
                    with ExitStack() as es3:
                        wq, wqb = k.sb(es3, "at_wq", [128, KC, 256], BF16)
                        wk, wkb = k.sb(es3, "at_wk", [128, KC, 64], BF16)
                        S.dma("pool", wq[:, :, :], w_in[:, g * 256:(g + 1) * 256].rearrange("(kc p) o -> p kc o", p=128),
                              w=[wqb])
                        S.dma("pool", wk[:, :, :],
                              w_in[:, 1024 + g * 64:1024 + (g + 1) * 64].rearrange("(kc p) o -> p kc o", p=128), w=[wkb])
                        raw = [k.sb(es3, "at_raw%d" % i, [64, 512], F32) for i in range(2)]
                        tm = [k.sb(es3, "at_tm%d" % i, [64, 512], F32) for i in range(2)]
                        n = 0
                        for tg in range(T // 512):
                            hT, hb = mod_tile(tg)
                            for hh in range(5):
                                sl = slice(tg * 512, (tg + 1) * 512)
                                pq, pqb = k.psum()
                                for kc in range(KC):
                                    lh = wq[:, kc, hh * 64:(hh + 1) * 64] if hh < 4 else wk[:, kc, :]
                                    k.mm(pq[0:64, :], lh, hT[:, kc, :], start=(kc == 0), stop=(kc == KC - 1),
                                         r=[wqb, wkb, hb], w=[pqb])
                                dst = qT[:, hh, sl] if hh < 4 else kT[:, sl]
                                dstb = qb if hh < 4 else kb
                                if grp == 0:
                                    k.cp("act" if n % 2 else "dve", dst, pq[0:64, :], r=[pqb], w=[dstb])
                                else:
                                    rw, rwb = raw[n % 2]
                                    t_, tb_ = tm[n % 2]
                                    k.cp("act", rw[:, :], pq[0:64, :], r=[pqb], w=[rwb])
                                    ps2, ps2b = k.psum()
                                    k.mm(ps2[0:64, :], C["perm"][0:64, 0:64], rw[:, :], start=True, stop=True,
                                         r=[rwb, C["b"]], w=[ps2b])
                                    k.tt("dve", t_[:, :], ps2[0:64, :], C["sin"][:, sl], ALU.mult, r=[ps2b, C["b"]], w=[tb_])
                                    k.tt("pool", rw[:, :], rw[:, :], C["cos"][:, sl], ALU.mult, r=[rwb, C["b"]], w=[rwb])
                                    k.tt("dve", dst, rw[:, :], t_[:, :], ALU.add, r=[rwb, tb_], w=[dstb])
                                n += 1
                        S.barrier()
                    with ExitStack() as es3:
                        NKMAX = 640 if grp == 1 else 256
                        WU = 2
                        sall = [k.sb(es3, "at_sall%d" % i, [128, NKMAX], F32) for i in range(2 * WU)]
                        pn = [k.sb(es3, "at_pn%d" % i, [128, NKMAX], BF16) for i in range(2 * WU)]
                        pT = [k.sb(es3, "at_pT%d" % i, [128, 2, 5, 128], BF16) for i in range(WU)]
                        sm = [k.sb(es3, "at_sm%d" % i, [128, 8], F32) for i in range(2 * WU)]
                        OTbs = [Buf("OT%d" % i) for i in range(WU)]

                        def unit_gen(uid, qt, cpair):
                            qsl = slice(qt * 128, (qt + 1) * 128)
                            if grp == 0:
                                seq = qt // 2
                                kblocks = [("full", seq * 2), ("full", seq * 2 + 1)]
                            else:
                                kblocks = []
                                if qt > 0:
                                    kblocks.append(("prev", qt - 1))
                                kblocks.append(("full", qt))
                                if qt < NT - 1:
                                    kblocks.append(("next", qt + 1))
                                kblocks += [("ctx", 0), ("ctx", 1)]
                            nkb = len(kblocks)
                            nk = nkb * 128
                            ptile, ptb_ = pT[uid % WU]
                            for par in range(2):
                                hh = cpair * 2 + par
                                h = g * 4 + hh
                                slot = (uid % WU) * 2 + par
                                sa, sab = sall[slot]
                                pn_, pnb = pn[slot]
                                sm_, smb = sm[slot]
                                banks = []
                                for b0 in range(0, nkb, 4):
                                    ps, psb = k.psum()
                                    banks.append((ps, psb))
                                    for bi in range(b0, min(nkb, b0 + 4)):
                                        kind, kt = kblocks[bi]
                                        if kind == "ctx":
                                            rhs = ckT[:, g, kt * 128:(kt + 1) * 128]
                                            rb = ckb
                                        else:
                                            rhs = kT[:, kt * 128:(kt + 1) * 128]
                                            rb = kb
                                        k.mm(ps[:, (bi - b0) * 128:(bi - b0 + 1) * 128], qT[:, hh, qsl], rhs,
                                             start=True, stop=True, r=[qb, rb], w=[psb])
                                yield
                                for bi, (kind, kt) in enumerate(kblocks):
                                    ps, psb = banks[bi // 4]
                                    src = ps[:, (bi % 4) * 128:(bi % 4 + 1) * 128]
                                    dsts = sa[:, bi * 128:(bi + 1) * 128]
                                    if kind == "prev":
                                        k.tt("dve", dsts, src, C["mprev"][:, :], ALU.add, r=[psb, C["b"]], w=[sab])
                                    elif kind == "next":
                                        k.tt("dve", dsts, src, C["mnext"][:, :], ALU.add, r=[psb, C["b"]], w=[sab])
                                    else:
                                        k.cp("act", dsts, src, r=[psb], w=[sab])
                                yield
                                k.S.op("dve", lambda e: e.reduce_max(sm_[:, 0:1], sa[:, 0:nk], mybir.AxisListType.X),
                                       r=[sab], w=[smb])
                                k.ts("dve", sm_[:, 1:2], sm_[:, 0:1], C_SCALE, sinkbc[:, h:h + 1], ALU.mult, ALU.max,
                                     r=[smb, sinkb], w=[smb])
                                k.ts("dve", sm_[:, 2:3], sm_[:, 1:2], -1.0, None, ALU.mult, ALU.bypass, r=[smb], w=[smb])
                                yield
                                k.act(sa[:, 0:nk], sa[:, 0:nk], AF.Exp, r=[sab, smb], w=[sab],
                                      bias=sm_[:, 2:3], scale=C_SCALE)
                                k.act(sm_[:, 3:4], sinkbc[:, h:h + 1], AF.Exp, r=[sinkb, smb], w=[smb],
                                      bias=sm_[:, 2:3], scale=1.0)
                                yield
                                k.S.op("dve", lambda e: e.reduce_sum(sm_[:, 4:5], sa[:, 0:nk], mybir.AxisListType.X),
                                       r=[sab], w=[smb])
                                k.tt("dve", sm_[:, 5:6], sm_[:, 4:5], sm_[:, 3:4], ALU.add, r=[smb], w=[smb])
                                k.S.op("dve", lambda e: e.reciprocal(sm_[:, 6:7], sm_[:, 5:6]), r=[smb], w=[smb])
                                yield
                                k.act(pn_[:, 0:nk], sa[:, 0:nk], AF.Identity, r=[sab, smb], w=[pnb], scale=sm_[:, 6:7])
                                yield
                                for b0 in range(0, nkb, 4):
                                    pt_, ptb2 = k.psum()
                                    ptv = pt_[:, :].bitcast(BF16)
                                    nb = min(nkb, b0 + 4) - b0
                                    for bi in range(b0, b0 + nb):
                                        k.tr(ptv[:, (bi - b0) * 128:(bi - b0 + 1) * 128], pn_[:, bi * 128:(bi + 1) * 128],
                                             C["identb"][:, :], r=[pnb, C["b"]], w=[ptb2])
                                    k.cp("act" if b0 else "dve", ptile[:, par, b0:b0 + nb, :],
                                         ptv[:, 0:nb * 128].rearrange("p (b q) -> p b q", q=128), r=[ptb2], w=[ptb_])
                                yield
                            po, pob = k.psum()
                            nmm = 2 * nkb
                            i_ = 0
                            for par in range(2):
                                for bi, (kind, kt) in enumerate(kblocks):
                                    if kind == "ctx":
                                        lh = cvp[:, kt, g, par, :]
                                        lb = cvb
                                    else:
                                        lh = vp[:, kt, par, :]
                                        lb = vpb
                                    k.mm(po[:, 0:128], lh, ptile[:, par, bi, :], start=(i_ == 0), stop=(i_ == nmm - 1),
                                         r=[lb, ptb_], w=[pob])
                                    i_ += 1
                            yield
                            k.cp("act", OT[:, g * 2 + cpair, qsl], po[:, 0:128], r=[pob], w=[OTbs[uid % WU]])

                        if DBG.get("mem") and g == 0:
                            print("ATTN grp", grp, "sbuf free", k.nc.sbuf_bytes_remaining)
                        units = [(qt, cp_) for qt in range(NT) for cp_ in range(2)]
                        run_interleaved([unit_gen(i, qt, cp_) for i, (qt, cp_) in enumerate(units)], WU)
                        k.S.op("dve", lambda e: e.memset(sm[0][0][:, 7:8], 0.0), r=OTbs, w=[OTb, sm[0][1]])
                        S.barrier()
            S.barrier()
        emit_mixer_tail(k, C, W["attn_w_out"][0], 8, OT, OTb, yT, yb, T, mod["gate"], modb, lncols, lnb)


def run_interleaved(gens, width):
    gens = list(gens)
    active = []
    while gens or active:
        while gens and len(active) < width:
            active.append(gens.pop(0))
        for g_ in list(active):
            try:
                next(g_)
            except StopIteration:
                active.remove(g_)


def bc(ap, axis, n):
    shp = list(ap.shape)
    shp.insert(axis, n)
    return ap.unsqueeze(axis).to_broadcast(shp)


def emit_softplus(k, C, out_ap, outb, xin, xinb, P, N, es):
    a, ab = k.sb(es, "sp_a", [128, N], F32)
    k.stt("dve", a[0:P, :], xin, -1.0, xin, ALU.mult, ALU.max, r=[xinb], w=[ab])
    k.act(a[0:P, :], a[0:P, :], AF.Exp, r=[ab], w=[ab], scale=-1.0)
    k.act(a[0:P, :], a[0:P, :], AF.Ln, r=[ab, C["b"]], w=[ab], bias=C["ones"][0:P, 0:1], scale=1.0)
    k.stt("dve", out_ap, xin, 0.0, a[0:P, :], ALU.max, ALU.add, r=[xinb, ab], w=[outb])


def emit_ssd(k, C, W, O, SCR, yT, yb, T, grp, j, mod, modb, lncols, lnb):
    S = k.S
    w_in = W["ssd_w_in"][j]
    NT = T // 128
    nseq, L = (4, 256) if grp == 0 else (1, 2048)
    TPS = L // 128
    with ExitStack() as es:
        cw, cwb = k.sb(es, "sd_cw", [128, 120], F32)
        emit_load_cols(k, cw[:, :], cwb, W["ssd_conv_w"][j].rearrange("k (c p) -> (k c) p", p=128), 120, C)
        cbias, cbb = k.sb(es, "sd_cb", [128, 24], F32)
        emit_load_cols(k, cbias[:, :], cbb, W["ssd_conv_b"][j].rearrange("(c p) -> c p", p=128), 24, C)
        normg, ngb = k.sb(es, "sd_ng", [128, 16], F32)
        emit_load_cols(k, normg[:, :], ngb, W["ssd_norm"][j].rearrange("(c p) -> c p", p=128), 16, C)
        dtb, dtbb = k.sb(es, "sd_dtb", [64, 2], F32)
        emit_load_cols(k, dtb[:, 0:1], dtbb, W["ssd_dt_bias"][j:j + 1].rearrange("o d h -> o (d h)"), 1, C, Wd=64)
        emit_load_cols(k, dtb[:, 1:2], dtbb, W["ssd_a_log"][j:j + 1].rearrange("o d h -> o (d h)"), 1, C, Wd=64)
        k.act(dtb[:, 1:2], dtb[:, 1:2], AF.Exp, r=[dtbb], w=[dtbb])
        k.ts("dve", dtb[:, 1:2], dtb[:, 1:2], -1.0, None, ALU.mult, ALU.bypass, r=[dtbb], w=[dtbb])
        dcol, dcb = k.sb(es, "sd_dcol", [128, 16], F32)
        with ExitStack() as es1:
            drow, drb = k.sb(es1, "sd_drow", [1, 32], F32)
            S.dma("sp", drow[0:1, :], W["ssd_d"][j:j + 1, :], w=[drb])
            pt, pb = k.psum()
            for par in range(2):
                k.mm(pt[:, 0:16], C["sel"][0:1, par * 128:(par + 1) * 128], drow[0:1, par:32:2],
                     start=(par == 0), stop=(par == 1), r=[drb, C["b"]], w=[pb])
            k.cp("dve", dcol[:, :], pt[:, 0:16], r=[pb], w=[dcb])
            S.barrier()
        dtT, dtTb = k.sb(es, "sd_dtT", [64, T], F32)
        dtaT, dtaTb = k.sb(es, "sd_dtaT", [64, T], F32)

        with ExitStack() as es1:
            hT, hb = k.sb(es1, "sd_hT", [128, KC, T], BF16)
            emit_modulate(k, hT, hb, yT, yb, T, mod, modb)
            wbuf = [k.sb(es1, "sd_w%d" % i, [128, KC, 256], BF16) for i in range(2)]
            pad, padb = k.sb(es1, "sd_pad", [128, nseq, L + 4], F32)
            k.memset("pool", pad[:, :, :], 0.0, w=[padb])
            acc = [k.sb(es1, "sd_acc%d" % i, [128, nseq, L], F32) for i in range(2)]
            ob16 = [k.sb(es1, "sd_o%d" % i, [128, T], BF16) for i in range(2)]

            def loadw(gi):
                S.dma("pool", wbuf[gi % 2][0][:, :, :],
                      w_in[:, gi * 256:(gi + 1) * 256].rearrange("(kc p) o -> p kc o", p=128), w=[wbuf[gi % 2][1]])

            loadw(0)
            n = 0
            for gi in range(20):
                if gi + 1 < 20:
                    loadw(gi + 1)
                wt, wb = wbuf[gi % 2]
                for o2 in range(2):
                    oc = gi * 2 + o2
                    o16, o16b = ob16[n % 2]
                    ac, acb = acc[n % 2]
                    n += 1
                    for tg in range(T // 512):
                        ps, psb = k.psum()
                        for kc in range(KC):
                            k.mm(ps[:, :], wt[:, kc, o2 * 128:(o2 + 1) * 128], hT[:, kc, tg * 512:(tg + 1) * 512],
                                 start=(kc == 0), stop=(kc == KC - 1), r=[wb, hb], w=[psb])
                        if oc < 16:
                            k.act(o16[:, tg * 512:(tg + 1) * 512], ps[:, :], AF.Silu, r=[psb], w=[o16b])
                        else:
                            if grp == 0:
                                k.cp("act", pad[:, tg * 2:(tg + 1) * 2, 2:2 + L],
                                     ps[:, :].rearrange("p (s l) -> p s l", l=L), r=[psb], w=[padb])
                            else:
                                k.cp("act", pad[:, 0, 2 + tg * 512:2 + (tg + 1) * 512], ps[:, :], r=[psb], w=[padb])
                    if oc < 16:
                        S.dma("sp", SCR["zs"][oc, :, 0:T], o16[:, :], r=[o16b], w=[SCR["zsb"]])
                    else:
                        c = oc - 16
                        e1 = "dve" if c % 2 == 0 else "pool"
                        k.ts(e1, ac[:, :, :], pad[:, :, 0:L], cw[:, c:c + 1], cbias[:, c:c + 1], ALU.mult, ALU.add,
                             r=[padb, cwb, cbb], w=[acb])
                        for kk in range(1, 5):
                            k.stt(e1, ac[:, :, :], pad[:, :, kk:kk + L], cw[:, kk * 24 + c:kk * 24 + c + 1], ac[:, :, :],
                                  ALU.mult, ALU.add, r=[padb, cwb, acb], w=[acb])
                        k.act(o16[:, :], ac[:, :, :].rearrange("p s l -> p (s l)"), AF.Silu, r=[acb], w=[o16b])
                        S.dma("sp", SCR["xc"][c, :, 0:T], o16[:, :], r=[o16b], w=[SCR["xcb"]])
            wdt, wdtb = k.sb(es1, "sd_wdt", [128, KC, 64], BF16)
            S.dma("pool", wdt[:, :, :], w_in[:, 5120:5184].rearrange("(kc p) o -> p kc o", p=128), w=[wdtb])
            xs_, xsb_ = k.sb(es1, "sd_dtx", [64, 512], F32)
            for tg in range(T // 512):
                ps, psb = k.psum()
                for kc in range(KC):
                    k.mm(ps[0:64, :], wdt[:, kc, :], hT[:, kc, tg * 512:(tg + 1) * 512],
                         start=(kc == 0), stop=(kc == KC - 1), r=[wdtb, hb], w=[psb])
                k.act(xs_[:, :], ps[0:64, :], AF.Identity, r=[psb, dtbb], w=[xsb_], bias=dtb[:, 0:1], scale=1.0)
                with ExitStack() as es2:
                    emit_softplus(k, C, dtT[:, tg * 512:(tg + 1) * 512], dtTb, xs_[:, :], xsb_, 64, 512, es2)
                    S.barrier()
            k.ts("dve", dtaT[:, :], dtT[:, :], dtb[:, 1:2], None, ALU.mult, ALU.bypass, r=[dtTb, dtbb], w=[dtaTb])
            S.barrier()

        for dirn in range(2):
            S.mark("  ssd sweep%d" % dirn)
            final = dirn == 1
            TRI = C["tri_f"] if dirn == 0 else C["tri_b"]
            END = 127 if dirn == 0 else 0
            with ExitStack() as es1:
                xTt, xTb = k.sb(es1, "sw_xT", [128, 16, 128], BF16)
                bcT, bcb = k.sb(es1, "sw_bcT", [128, 8, 128], BF16)
                xz, xzb = k.sb(es1, "sw_xz", [128, 32, 128], BF16)
                hz, hzb = k.sb(es1, "sw_hz", [128, 32, 128], BF16)
                hT_, hTb = k.sb(es1, "sw_hT", [128, 2048], F32)
                k.memset("pool", xz[:, :, :], 0.0, w=[xzb])
                xzv = xz[:, :, :].rearrange("p (c r) f -> p c r f", r=2)
                hzv = hz[:, :, :].rearrange("p (c r) f -> p c r f", r=2)
                hTv = hT_[:, :].rearrange("p (c r f) -> p c r f", r=2, f=64)
                btok, btb = k.sb(es1, "sw_btok", [128, 512], BF16)
                dtk, dtkb = k.sb(es1, "sw_dtk", [128, 128], F32)
                cbm, cbmb = k.sb(es1, "sw_cbm", [128, 4, 128], F32)
                nacs, nacsb = k.sb(es1, "sw_nacs", [128, 32], F32)
                nb, nbb = k.sb(es1, "sw_nb", [128, 32], F32)
                rhsA = [k.sb(es1, "sw_rhsA%d" % i, [128, 4, 128], F32) for i in range(2)]
                e4 = [k.sb(es1, "sw_e4%d" % i, [128, 4, 128], F32) for i in range(2)]
                ea4 = [k.sb(es1, "sw_ea4%d" % i, [128, 4, 128], F32) for i in range(2)]
                MT, MTb = k.sb(es1, "sw_MT", [128, 32, 128], BF16)
                CE, CEb = k.sb(es1, "sw_CE", [128, 32, 128], BF16)
                MTbs = [Buf("MT%d" % i) for i in range(8)]
                CEbs = [Buf("CE%d" % i) for i in range(8)]
                wraw, wrb = k.sb(es1, "sw_wraw", [128, 32], F32)
                craw, crb = k.sb(es1, "sw_craw", [128, 32], F32)
                gt, gtb = k.sb(es1, "sw_gt", [128, 16, 128], F32)
                xs, xsb = k.sb(es1, "sw_xs", [128, 32, 64], BF16)
                xsv = xs[:, :, :].rearrange("p (c r) f -> p c r f", r=2)
                stg, stgb = k.sb(es1, "sw_stg", [128, 16, 128], F32)
                if final:
                    zst, zsb_ = k.sb(es1, "sw_zs", [128, 16, 128], BF16)
                    sq = [k.sb(es1, "sw_sq%d" % i, [128, 128], F32) for i in range(2)]
                    rstd, rstdb = k.sb(es1, "sw_rstd", [128, 128], F32)
                    g16, g16b = k.sb(es1, "sw_g16", [128, 16, 128], BF16)
                    sqa, sqab = k.sb(es1, "sw_sqa", [128, 16, 128], F32)

                if DBG.get("mem"):
                    print("SSD sweep", dirn, "grp", grp, "sbuf free", k.nc.sbuf_bytes_remaining)

                def hz_refresh():
                    k.cp("act", hzv[:, :, 0, 0:64], hTv[:, :, 0, :], r=[hTb], w=[hzb])
                    k.cp("pool", hzv[:, :, 1, 64:128], hTv[:, :, 1, :], r=[hTb], w=[hzb])

                order = []
                for sq_i in range(nseq):
                    tl = list(range(sq_i * TPS, (sq_i + 1) * TPS))
                    order += tl[::-1] if dirn == 1 else tl
                NBF = 2 if k.nc.sbuf_bytes_remaining > 6144 + 4096 else 1
                NSL = 2
                if k.nc.sbuf_bytes_remaining > 6144 * (NBF - 1) + 6144 + 4096:
                    NSL = 3
                    rhsA.append(k.sb(es1, "sw_rhsA2", [128, 4, 128], F32))
                    e4.append(k.sb(es1, "sw_e42", [128, 4, 128], F32))
                    ea4.append(k.sb(es1, "sw_ea42", [128, 4, 128], F32))
                xTts = [(xTt, xTb)]
                bcTs = [(bcT, bcb)]
                if NBF == 2:
                    xTts.append(k.sb(es1, "sw_xT2", [128, 16, 128], BF16))
                    bcTs.append(k.sb(es1, "sw_bcT2", [128, 8, 128], BF16))

                def issue_loads(i):
                    tsl_ = slice(order[i] * 128, (order[i] + 1) * 128)
                    S.dma("sp", xTts[i % NBF][0][:, :, :], SCR["xc"][0:16, :, tsl_].rearrange("c p t -> p c t"),
                          r=[SCR["xcb"]], w=[xTts[i % NBF][1]])
                    S.dma("sp", bcTs[i % NBF][0][:, :, :], SCR["xc"][16:24, :, tsl_].rearrange("c p t -> p c t"),
                          r=[SCR["xcb"]], w=[bcTs[i % NBF][1]])

                issue_loads(0)
                gi = 0
                for sq_i in range(nseq):
                    k.memset("pool", hz[:, :, :], 0.0, w=[hzb])
                    if grp == 0:
                        k.memset("dve", hT_[:, :], 0.0, w=[hTb])
                    else:
                        S.dma("sp", stg[:, :, :], W["state_ssd"][j, dirn].rearrange("h p n -> (h p) n")
                              .rearrange("(c q) n -> q c n", q=128), w=[stgb])
                        for c4 in range(4):
                            pt, pb = k.psum()
                            for cc in range(4):
                                c = c4 * 4 + cc
                                k.tr(pt[:, cc * 128:(cc + 1) * 128], stg[:, c, :], C["ident"][:, :], r=[stgb, C["b"]], w=[pb])
                            k.cp("dve", hT_[:, c4 * 512:(c4 + 1) * 512], pt[:, :], r=[pb], w=[hTb])
                        hz_refresh()
                    tiles = list(range(sq_i * TPS, (sq_i + 1) * TPS))
                    if dirn == 1:
                        tiles = tiles[::-1]
                    for tt in tiles:
                        tsl = slice(tt * 128, (tt + 1) * 128)
                        if NBF == 1:
                            if gi > 0:
                                issue_loads(gi)
                        elif gi + 1 < len(order):
                            issue_loads(gi + 1)
                        xTt, xTb = xTts[gi % NBF]
                        bcT, bcb = bcTs[gi % NBF]
                        gi += 1
                        if final:
                            S.dma("sp", zst[:, :, :], SCR["zs"][:, :, tsl].rearrange("c p t -> p c t"), r=[SCR["zsb"]], w=[zsb_])
                            S.dma("sp", stg[:, :, :], SCR["yf"][:, :, tsl].rearrange("c p t -> p c t"), r=[SCR["yfb"]], w=[stgb])
                        for half in range(2):
                            pt, pb = k.psum()
                            ptv = pt[:, :].bitcast(BF16)
                            for c8 in range(8):
                                k.tr(ptv[:, c8 * 128:(c8 + 1) * 128], xTt[:, half * 8 + c8, :], C["identb"][:, :],
                                     r=[xTb, C["b"]], w=[pb])
                            src = ptv[:, 0:1024].rearrange("p (c r f) -> p c r f", r=2, f=64)
                            k.cp("dve", xzv[:, half * 8:(half + 1) * 8, 0, 0:64], src[:, :, 0, :], r=[pb], w=[xzb])
                            k.cp("act", xzv[:, half * 8:(half + 1) * 8, 1, 64:128], src[:, :, 1, :], r=[pb], w=[xzb])
                        pt, pb = k.psum()
                        ptv = pt[:, :].bitcast(BF16)
                        for g in range(4):
                            k.tr(ptv[:, g * 128:(g + 1) * 128], bcT[:, g, :], C["identb"][:, :], r=[bcb, C["b"]], w=[pb])
                        k.cp("dve", btok[:, :], ptv[:, 0:512], r=[pb], w=[btb])
                        pt, pb = k.psum()
                        k.tr(pt[:, 0:64], dtT[0:64, tsl], C["ident"][0:64, 0:64], r=[dtTb, C["b"]], w=[pb])
                        k.tr(pt[:, 64:128], dtaT[0:64, tsl], C["ident"][0:64, 0:64], r=[dtaTb, C["b"]], w=[pb])
                        k.cp("act", dtk[:, :], pt[:, 0:128], r=[pb], w=[dtkb])
                        dt_d = dtk[:, dirn * 32:(dirn + 1) * 32]
                        dta_d = dtk[:, 64 + dirn * 32:64 + (dirn + 1) * 32]
                        pt, pb = k.psum()
                        for g in range(4):
                            k.mm(pt[:, g * 128:(g + 1) * 128], bcT[:, g, :], bcT[:, 4 + g, :], start=True, stop=True,
                                 r=[bcb], w=[pb])
                        k.tt("dve", cbm[:, :, :], pt[:, :].rearrange("p (g q) -> p g q", q=128), bc(TRI[:, :], 1, 4),
                             ALU.mult, r=[pb, C["b"]], w=[cbmb])
                        pt, pb = k.psum()
                        k.mm(pt[:, 0:32], TRI[:, :], dta_d, start=True, stop=True, r=[dtkb, C["b"]], w=[pb])
                        k.ts("dve", nacs[:, :], pt[:, 0:32], -1.0, None, ALU.mult, ALU.bypass, r=[pb], w=[nacsb])
                        k.act(nb[:, :], dt_d, AF.Ln, r=[dtkb], w=[nbb])
                        k.tt("dve", nb[:, :], nb[:, :], nacs[:, :], ALU.add, r=[nbb, nacsb], w=[nbb])
                        def bank_gen(hb_):
                            h0 = hb_ * 4
                            g = hb_ // 2
                            ra, rab = rhsA[hb_ % NSL]
                            e_, eb = e4[hb_ % NSL]
                            a_, aeb = ea4[hb_ % NSL]
                            for i in range(4):
                                k.act(ra[:, i, :], TRI[:, :], AF.Identity, r=[C["b"], dtkb], w=[rab],
                                      scale=dta_d[:, h0 + i:h0 + i + 1])
                            R, Rb = k.psum()
                            k.mm(R[:, :], C["ones"][:, :], ra[:, :, :].rearrange("p h q -> p (h q)"), start=True, stop=True,
                                 r=[rab, C["b"]], w=[Rb])
                            Rv = R[:, :].rearrange("p (h q) -> p h q", q=128)
                            yield
                            k.tt("dve", e_[:, :, :], Rv, bc(nb[:, h0:h0 + 4], 2, 128), ALU.add, r=[Rb, nbb], w=[eb])
                            k.ts("dve", e_[:, :, :], e_[:, :, :], 20.0, None, ALU.min, ALU.bypass, r=[eb], w=[eb])
                            k.tt("dve", wraw[:, h0:h0 + 4], Rv[:, :, END], nacs[:, h0:h0 + 4], ALU.add, r=[Rb, nacsb], w=[wrb])
                            k.cp("dve", craw[:, h0:h0 + 4], Rv[:, :, END], r=[Rb], w=[crb])
                            k.act(a_[:, :, :], Rv, AF.Exp, r=[Rb], w=[aeb])
                            yield
                            k.act(e_[:, :, :], e_[:, :, :], AF.Exp, r=[eb], w=[eb])
                            k.tt("pool", CE[:, h0:h0 + 4, :], a_[:, :, :], bc(bcT[:, 4 + g, :], 1, 4), ALU.mult,
                                 r=[aeb, bcb], w=[CEbs[hb_]])
                            yield
                            k.tt("dve", MT[:, h0:h0 + 4, :], e_[:, :, :], bc(cbm[:, g, :], 1, 4), ALU.mult,
                                 r=[eb, cbmb], w=[MTbs[hb_]])

                        run_interleaved([bank_gen(hb_) for hb_ in range(8)], NSL)
                        k.act(wraw[:, :], wraw[:, :], AF.Exp, r=[wrb], w=[wrb])
                        k.tt("dve", wraw[:, :], wraw[:, :], dt_d, ALU.mult, r=[wrb, dtkb], w=[wrb])
                        k.act(craw[:, :], craw[:, :], AF.Exp, r=[crb], w=[crb])
                        for c4 in range(4):
                            po, pob = k.psum()
                            for cc in range(4):
                                c = c4 * 4 + cc
                                o_ = po[:, cc * 128:(cc + 1) * 128]
                                k.mm(o_, xz[:, 2 * c, :], MT[:, 2 * c, :], start=True, stop=False, r=[xzb, MTbs[c // 2]], w=[pob])
                                k.mm(o_, xz[:, 2 * c + 1, :], MT[:, 2 * c + 1, :], start=False, stop=False, r=[xzb, MTbs[c // 2]], w=[pob])
                                k.mm(o_, hz[:, 2 * c, :], CE[:, 2 * c, :], start=False, stop=False, r=[hzb, CEbs[c // 2]], w=[pob])
                                k.mm(o_, hz[:, 2 * c + 1, :], CE[:, 2 * c + 1, :], start=False, stop=True, r=[hzb, CEbs[c // 2]], w=[pob])
                            pov = po[:, :].rearrange("p (c q) -> p c q", q=128)
                            if not final:
                                k.cp("act", gt[:, c4 * 4:(c4 + 1) * 4, :], pov, r=[pob], w=[gtb])
                            else:
                                k.tt("dve", gt[:, c4 * 4:(c4 + 1) * 4, :], pov, stg[:, c4 * 4:(c4 + 1) * 4, :], ALU.add,
                                     r=[pob, stgb], w=[gtb])
                        k.tt("dve", xsv[:, :, 0, :], xzv[:, :, 0, 0:64], bc(wraw[:, 0:32:2], 2, 64), ALU.mult,
                             r=[xzb, wrb], w=[xsb])
                        k.tt("pool", xsv[:, :, 1, :], xzv[:, :, 1, 64:128], bc(wraw[:, 1:32:2], 2, 64), ALU.mult,
                             r=[xzb, wrb], w=[xsb])
                        k.tt("dve", hT_[:, :].rearrange("p (h f) -> p h f", f=64), hT_[:, :].rearrange("p (h f) -> p h f", f=64),
                             bc(craw[:, :], 2, 64), ALU.mult, r=[hTb, crb], w=[hTb])
                        for g in range(4):
                            pst, pstb = k.psum()
                            k.mm(pst[:, :], btok[:, g * 128:(g + 1) * 128], xs[:, g * 8:(g + 1) * 8, :].rearrange("p h f -> p (h f)"),
                                 start=True, stop=True, r=[btb, xsb], w=[pstb])
                            k.tt("dve", hT_[:, g * 512:(g + 1) * 512], hT_[:, g * 512:(g + 1) * 512], pst[:, :], ALU.add,
                                 r=[hTb, pstb], w=[hTb])
                        hz_refresh()
                        if not final:
                            S.dma("sp", SCR["yf"][:, :, tsl].rearrange("c p t -> p c t"), gt[:, :, :], r=[gtb], w=[SCR["yfb"]])
                        else:
                            k.tt("pool", g16[:, :, :], xTt[:, :, :], bc(dcol[:, :], 2, 128), ALU.mult, r=[xTb, dcb], w=[g16b])
                            k.tt("dve", gt[:, :, :], gt[:, :, :], g16[:, :, :], ALU.add, r=[gtb, g16b], w=[gtb])
                            k.tt("dve", gt[:, :, :], gt[:, :, :], zst[:, :, :], ALU.mult, r=[gtb, zsb_], w=[gtb])
                            k.act(sqa[:, :, :], gt[:, :, :], AF.Square, r=[gtb], w=[sqab])
                            S.op("dve", lambda e: e.reduce_sum(sq[0][0][:, :], sqa[:, :, :].rearrange("p c q -> p q c"),
                                                               mybir.AxisListType.X), r=[sqab], w=[sq[0][1]])
                            pss, pssb = k.psum()
                            k.mm(pss[:, 0:128], C["ones"][:, :], sq[0][0][:, :], start=True, stop=True,
                                 r=[sq[0][1], C["b"]], w=[pssb])
                            k.ts("dve", rstd[:, :], pss[:, 0:128], 1.0 / 2048, 1e-6, ALU.mult, ALU.add, r=[pssb], w=[rstdb])
                            k.act(rstd[:, :], rstd[:, :], AF.Sqrt, r=[rstdb], w=[rstdb])
                            S.op("dve", lambda e: e.reciprocal(rstd[:, :], rstd[:, :]), r=[rstdb], w=[rstdb])
                            k.tt("dve", gt[:, :, :], gt[:, :, :], bc(rstd[:, :], 1, 16), ALU.mult, r=[gtb, rstdb], w=[gtb])
                            k.tt("pool", g16[:, :, :], gt[:, :, :], bc(normg[:, :], 2, 128), ALU.mult, r=[gtb, ngb], w=[g16b])
                            S.dma("sp", SCR["gt"][:, :, tsl].rearrange("c p t -> p c t"), g16[:, :, :], r=[g16b], w=[SCR["gtb"]])
                    if grp == 0:
                        for c4 in range(4):
                            pt, pb = k.psum()
                            for cc in range(4):
                                c = c4 * 4 + cc
                                k.tr(pt[:, cc * 128:(cc + 1) * 128], hT_[:, c * 128:(c + 1) * 128], C["ident"][:, :],
                                     r=[hTb, C["b"]], w=[pb])
                            k.cp("act", stg[:, c4 * 4:(c4 + 1) * 4, :], pt[:, :].rearrange("p (c n) -> p c n", n=128),
                                 r=[pb], w=[stgb])
                        S.dma("sp", O["new_state_ssd"][sq_i, j, dirn].rearrange("h p n -> (h p) n")
                              .rearrange("(c q) n -> q c n", q=128), stg[:, :, :], r=[stgb])
                S.barrier()
        S.barrier()
    with ExitStack() as es:
        GT, GTb = k.sb(es, "sd_GT", [128, 16, T], BF16)
        S.dma("sp", GT[:, :, :], SCR["gt"][:, :, 0:T].rearrange("c p t -> p c t"), r=[SCR["gtb"]], w=[GTb])
        emit_mixer_tail(k, C, W["ssd_w_out"][j], 16, GT, GTb, yT, yb, T, mod["gate"], modb, lncols, lnb)


def emit_gdn(k, C, W, O, SCR, yT, yb, T, grp, mod, modb, lncols, lnb):
    S = k.S
    w_in = W["gdn_w_in"][0]
    nseq, L = (4, 256) if grp == 0 else (1, 2048)
    TPS = L // 128
    with ExitStack() as es:
        cw, cwb = k.sb(es, "gd_cw", [128, 160], F32)
        for half in range(2):
            emit_load_cols(k, cw[:, half * 80:(half + 1) * 80], cwb,
                           W["gdn_conv_w"][0].rearrange("k (c p) -> (k c) p", p=128)[half * 80:(half + 1) * 80, :], 80, C)
        cbias, cbb = k.sb(es, "gd_cb", [128, 32], F32)
        emit_load_cols(k, cbias[:, :], cbb, W["gdn_conv_b"][0].rearrange("(c p) -> c p", p=128), 32, C)
        normg, ngb = k.sb(es, "gd_ng", [128, 2], F32)
        emit_load_cols(k, normg[:, :], ngb, W["gdn_norm"][0].rearrange("(c p) -> c p", p=128), 2, C)
        dtb, dtbb = k.sb(es, "gd_dtb", [16, 2], F32)
        emit_load_cols(k, dtb[:, 0:1], dtbb, W["gdn_dt_bias"][0:1].rearrange("o d h -> o (d h)"), 1, C, Wd=16)
        emit_load_cols(k, dtb[:, 1:2], dtbb, W["gdn_a_log"][0:1].rearrange("o d h -> o (d h)"), 1, C, Wd=16)
        k.act(dtb[:, 1:2], dtb[:, 1:2], AF.Exp, r=[dtbb], w=[dtbb])
        k.ts("dve", dtb[:, 1:2], dtb[:, 1:2], -1.0, None, ALU.mult, ALU.bypass, r=[dtbb], w=[dtbb])
        betaT, betaTb = k.sb(es, "gd_betaT", [16, T], F32)
        gT_, gTb_ = k.sb(es, "gd_gT", [16, T], F32)

        with ExitStack() as es1:
            hT, hb = k.sb(es1, "gd_hT", [128, KC, T], BF16)
            emit_modulate(k, hT, hb, yT, yb, T, mod, modb)
            wbuf = [k.sb(es1, "gd_w%d" % i, [128, KC, 256], BF16) for i in range(2)]
            pad, padb = k.sb(es1, "gd_pad", [128, nseq, L + 4], F32)
            k.memset("pool", pad[:, :, :], 0.0, w=[padb])
            acc = [k.sb(es1, "gd_acc%d" % i, [128, nseq, L], F32) for i in range(2)]
            o32, o32b = k.sb(es1, "gd_o32", [128, T], F32)
            sq, sqb = k.sb(es1, "gd_sq", [128, 512], F32)
            rn, rnb = k.sb(es1, "gd_rn", [128, 512], F32)
            ob16 = [k.sb(es1, "gd_o%d" % i, [128, T], BF16) for i in range(2)]

            def loadw(gi):
                S.dma("pool", wbuf[gi % 2][0][:, :, :],
                      w_in[:, gi * 256:(gi + 1) * 256].rearrange("(kc p) o -> p kc o", p=128), w=[wbuf[gi % 2][1]])

            loadw(0)
            n = 0
            for gi in range(24):
                if gi + 1 < 24:
                    loadw(gi + 1)
                wt, wb = wbuf[gi % 2]
                for o2 in range(2):
                    oc = gi * 2 + o2
                    o16, o16b = ob16[n % 2]
                    ac, acb = acc[n % 2]
                    n += 1
                    for tg in range(T // 512):
                        ps, psb = k.psum()
                        for kc in range(KC):
                            k.mm(ps[:, :], wt[:, kc, o2 * 128:(o2 + 1) * 128], hT[:, kc, tg * 512:(tg + 1) * 512],
                                 start=(kc == 0), stop=(kc == KC - 1), r=[wb, hb], w=[psb])
                        if oc >= 32:
                            k.act(o16[:, tg * 512:(tg + 1) * 512], ps[:, :], AF.Silu, r=[psb], w=[o16b])
                        elif grp == 0:
                            k.cp("act", pad[:, tg * 2:(tg + 1) * 2, 2:2 + L],
                                 ps[:, :].rearrange("p (s l) -> p s l", l=L), r=[psb], w=[padb])
                        else:
                            k.cp("act", pad[:, 0, 2 + tg * 512:2 + (tg + 1) * 512], ps[:, :], r=[psb], w=[padb])
                    if oc >= 32:
                        S.dma("sp", SCR["zs"][oc - 32, :, 0:T], o16[:, :], r=[o16b], w=[SCR["zsb"]])
                        continue
                    c = oc
                    e1 = "dve" if c % 2 == 0 else "pool"
                    k.ts(e1, ac[:, :, :], pad[:, :, 0:L], cw[:, c:c + 1], cbias[:, c:c + 1], ALU.mult, ALU.add,
                         r=[padb, cwb, cbb], w=[acb])
                    for kk in range(1, 5):
                        k.stt(e1, ac[:, :, :], pad[:, :, kk:kk + L], cw[:, kk * 32 + c:kk * 32 + c + 1], ac[:, :, :],
                              ALU.mult, ALU.add, r=[padb, cwb, acb], w=[acb])
                    acf = ac[:, :, :].rearrange("p s l -> p (s l)")
                    if c >= 16:
                        k.act(o16[:, :], acf, AF.Silu, r=[acb], w=[o16b])
                    else:
                        k.act(o32[:, :], acf, AF.Silu, r=[acb], w=[o32b])
                        for tg in range(T // 512):
                            sl = slice(tg * 512, (tg + 1) * 512)
                            k.act(sq[:, :], o32[:, sl], AF.Square, r=[o32b], w=[sqb])
                            ps, psb = k.psum()
                            k.mm(ps[:, :], C["ones"][:, :], sq[:, :], start=True, stop=True, r=[sqb, C["b"]], w=[psb])
                            k.ts("dve", rn[:, :], ps[:, :], 1e-6, None, ALU.add, ALU.bypass, r=[psb], w=[rnb])
                            k.act(rn[:, :], rn[:, :], AF.Sqrt, r=[rnb], w=[rnb])
                            S.op("dve", lambda e: e.reciprocal(rn[:, :], rn[:, :]), r=[rnb], w=[rnb])
                            k.stt("dve", o16[:, sl], o32[:, sl], (128 ** -0.5) if c < 8 else 1.0, rn[:, :],
                                  ALU.mult, ALU.mult, r=[o32b, rnb], w=[o16b])
                    S.dma("sp", SCR["xc"][c, :, 0:T], o16[:, :], r=[o16b], w=[SCR["xcb"]])
            wab, wabb = k.sb(es1, "gd_wab", [128, KC, 32], BF16)
            S.dma("pool", wab[:, :, :], w_in[:, 6144:6176].rearrange("(kc p) o -> p kc o", p=128), w=[wabb])
            xs_, xsb_ = k.sb(es1, "gd_abx", [16, 512], F32)
            for tg in range(T // 512):
                sl = slice(tg * 512, (tg + 1) * 512)
                ps, psb = k.psum()
                for kc in range(KC):
                    k.mm(ps[0:16, :], wab[:, kc, 0:16], hT[:, kc, sl], start=(kc == 0), stop=(kc == KC - 1),
                         r=[wabb, hb], w=[psb])
                k.act(betaT[:, sl], ps[0:16, :], AF.Sigmoid, r=[psb], w=[betaTb])
                ps, psb = k.psum()
                for kc in range(KC):
                    k.mm(ps[0:16, :], wab[:, kc, 16:32], hT[:, kc, sl], start=(kc == 0), stop=(kc == KC - 1),
                         r=[wabb, hb], w=[psb])
                k.act(xs_[:, :], ps[0:16, :], AF.Identity, r=[psb, dtbb], w=[xsb_], bias=dtb[:, 0:1], scale=1.0)
                with ExitStack() as es2:
                    emit_softplus(k, C, gT_[:, sl], gTb_, xs_[:, :], xsb_, 16, 512, es2)
                    S.barrier()
            k.ts("dve", gT_[:, :], gT_[:, :], dtb[:, 1:2], None, ALU.mult, ALU.bypass, r=[gTb_, dtbb], w=[gTb_])
            S.barrier()

        for dirn in range(2):
            S.mark("  gdn sweep%d" % dirn)
            final = dirn == 1
            sfx = "_f" if dirn == 0 else "_b"
            TRIc = C["gtri" + sfx]
            NMSL = C["gnmsl" + sfx]
            with ExitStack() as es1:
                qkT, qkb = k.sb(es1, "gs_qkT", [128, 16, 128], BF16)
                vT, vTb = k.sb(es1, "gs_vT", [128, 16, 128], BF16)
                ktok, ktb = k.sb(es1, "gs_ktok", [128, 8, 128], BF16)
                vtok, vtb = k.sb(es1, "gs_vtok", [128, 8, 256], BF16)
                vb_, vbb = k.sb(es1, "gs_vb", [128, 8, 256], BF16)
                kbg, kbgb = k.sb(es1, "gs_kbg", [128, 8, 128], BF16)
                kdec, kdb = k.sb(es1, "gs_kdec", [128, 8, 128], BF16)
                bg, bgb = k.sb(es1, "gs_bg", [128, 64], F32)
                sc, scb = k.sb(es1, "gs_sc", [128, 48], F32)
                rhsA = [k.sb(es1, "gs_rhsA%d" % i, [128, 4, 128], F32) for i in range(2)]
                d1 = [k.sb(es1, "gs_d1%d" % i, [128, 4, 128], F32) for i in range(2)]
                d2 = [k.sb(es1, "gs_d2%d" % i, [128, 4, 128], F32) for i in range(2)]
                kkms = [k.sb(es1, "gs_kkm%d" % q, [128, 4, 128], F32) for q in range(2)]
                CDT = BF16 if DBG.get("gdn_bf16", False) else F32
                Xs = [[k.sb(es1, "gs_X%d_%d" % (q, i), [128, 4, 128], CDT) for i in range(2)] for q in range(2)]
                XTs = [[k.sb(es1, "gs_XT%d_%d" % (q, i), [128, 4, 128], CDT) for i in range(2)] for q in range(2)]
                Paccs = [k.sb(es1, "gs_Pacc%d" % q, [128, 4, 128], F32) for q in range(2)]
                PaccBs = [k.sb(es1, "gs_PaccB%d" % q, [128, 4, 128], CDT) for q in range(2)] if CDT == BF16 else Paccs
                TTb, TTbb = k.sb(es1, "gs_TTb", [128, 8, 128], BF16)
                u, ub = k.sb(es1, "gs_u", [128, 8, 256], F32)
                wT, wTb = k.sb(es1, "gs_wT", [128, 8, 128], BF16)
                qkm, qkmb = k.sb(es1, "gs_qkm", [128, 8, 128], BF16)
                delta, dlb = k.sb(es1, "gs_delta", [128, 8, 256], BF16)
                o_, ob_ = k.sb(es1, "gs_o", [128, 8, 256], F32)
                dlbs = [Buf("dl%d" % i) for i in range(8)]
                obs = [Buf("o%d" % i) for i in range(8)]
                ubs = [Buf("u%d" % i) for i in range(2)]
                wTbs = [Buf("wT%d" % i) for i in range(2)]
                qkmbs = [Buf("qkm%d" % i) for i in range(2)]
                TTbbs = [Buf("TTb%d" % i) for i in range(2)]
                Sst = [k.sb(es1, "gs_S%d" % h, [128, 256], F32) for h in range(8)]
                Sbf = [k.sb(es1, "gs_Sb%d" % h, [128, 256], BF16) for h in range(8)]
                if final:
                    of_, ofb = k.sb(es1, "gs_of", [128, 8, 256], F32)
                    zst, zsb_ = k.sb(es1, "gs_zs", [128, 16, 128], BF16)
                    ss, ssb = k.sb(es1, "gs_ss", [128, 16], F32)
                    on16, onb = vb_, vbb
                    g16, g16b = k.sb(es1, "gs_g16", [128, 16, 128], BF16)

                if DBG.get("mem"):
                    print("GDN sweep", dirn, "grp", grp, "sbuf free", k.nc.sbuf_bytes_remaining)
                order = []
                for sq_i in range(nseq):
                    tl = list(range(sq_i * TPS, (sq_i + 1) * TPS))
                    order += tl[::-1] if dirn == 1 else tl
                NBF = 2 if (k.nc.sbuf_bytes_remaining > 8192 + 4096 and not DBG.get("nopf")) else 1
                qkTs = [(qkT, qkb)]
                vTs = [(vT, vTb)]
                if NBF == 2:
                    qkTs.append(k.sb(es1, "gs_qkT2", [128, 16, 128], BF16))
                    vTs.append(k.sb(es1, "gs_vT2", [128, 16, 128], BF16))

                def issue_loads(i):
                    tsl_ = slice(order[i] * 128, (order[i] + 1) * 128)
                    S.dma("sp", qkTs[i % NBF][0][:, :, :], SCR["xc"][0:16, :, tsl_].rearrange("c p t -> p c t"),
                          r=[SCR["xcb"]], w=[qkTs[i % NBF][1]])
                    S.dma("sp", vTs[i % NBF][0][:, :, :], SCR["xc"][16:32, :, tsl_].rearrange("c p t -> p c t"),
                          r=[SCR["xcb"]], w=[vTs[i % NBF][1]])

                issue_loads(0)
                gi = 0
                for sq_i in range(nseq):
                    for h in range(8):
                        if grp == 0:
                            k.memset("dve" if h % 2 else "pool", Sst[h][0][:, :], 0.0, w=[Sst[h][1]])
                        else:
                            S.dma("sp", Sst[h][0][:, :], W["state_delta"][0, dirn, h], w=[Sst[h][1]])
                        k.cp("act", Sbf[h][0][:, :], Sst[h][0][:, :], r=[Sst[h][1]], w=[Sbf[h][1]])
                    tiles = list(range(sq_i * TPS, (sq_i + 1) * TPS))
                    if dirn == 1:
                        tiles = tiles[::-1]
                    for tt in tiles:
                        tsl = slice(tt * 128, (tt + 1) * 128)
                        if NBF == 1:
                            if gi > 0:
                                issue_loads(gi)
                        elif gi + 1 < len(order):
                            issue_loads(gi + 1)
                        qkT, qkb = qkTs[gi % NBF]
                        vT, vTb = vTs[gi % NBF]
                        gi += 1
                        if final:
                            S.dma("sp", zst[:, :, :], SCR["zs"][:, :, tsl].rearrange("c p t -> p c t"), r=[SCR["zsb"]], w=[zsb_])
                            S.dma("sp", of_[:, :, :], SCR["of"][tsl, :].rearrange("t (h v) -> t h v", v=256),
                                  r=[SCR["ofb"]], w=[ofb])
                        pt, pb = k.psum()
                        ptv = pt[:, :].bitcast(BF16)
                        for h in range(8):
                            k.tr(ptv[:, h * 128:(h + 1) * 128], qkT[:, 8 + h, :], C["identb"][:, :], r=[qkb, C["b"]], w=[pb])
                        k.cp("dve", ktok[:, :, :], ptv[:, 0:1024].rearrange("p (h f) -> p h f", f=128), r=[pb], w=[ktb])
                        for half in range(2):
                            pt, pb = k.psum()
                            ptv = pt[:, :].bitcast(BF16)
                            for c8 in range(8):
                                k.tr(ptv[:, c8 * 128:(c8 + 1) * 128], vT[:, half * 8 + c8, :], C["identb"][:, :],
                                     r=[vTb, C["b"]], w=[pb])
                            k.cp("act", vtok[:, half * 4:(half + 1) * 4, :],
                                 ptv[:, 0:1024].rearrange("p (h v) -> p h v", v=256), r=[pb], w=[vtb])
                        pt, pb = k.psum()
                        k.tr(pt[:, 0:16], betaT[0:16, tsl], C["ident"][0:16, 0:16], r=[betaTb, C["b"]], w=[pb])
                        k.tr(pt[:, 16:32], gT_[0:16, tsl], C["ident"][0:16, 0:16], r=[gTb_, C["b"]], w=[pb])
                        k.cp("act", bg[:, 0:32], pt[:, 0:32], r=[pb], w=[bgb])
                        beta_d = bg[:, dirn * 8:(dirn + 1) * 8]
                        g_d = bg[:, 16 + dirn * 8:16 + (dirn + 1) * 8]
                        pt, pb = k.psum()
                        k.mm(pt[:, 0:8], TRIc[:, :], g_d, start=True, stop=True, r=[bgb, C["b"]], w=[pb])
                        k.mm(pt[:, 8:16], C["gblk"][:, :], g_d, start=True, stop=True, r=[bgb, C["b"]], w=[pb])
                        k.cp("dve", bg[:, 32:40], pt[:, 0:8], r=[pb], w=[bgb])
                        k.ts("dve", bg[:, 40:48], pt[:, 0:8], -1.0, None, ALU.mult, ALU.bypass, r=[pb], w=[bgb])
                        k.tt("dve", sc[:, 16:24], pt[:, 8:16], bg[:, 40:48], ALU.add, r=[pb, bgb], w=[scb])
                        k.act(sc[:, 16:24], sc[:, 16:24], AF.Exp, r=[scb], w=[scb])
                        k.act(bg[:, 48:56], bg[:, 32:40], AF.Exp, r=[bgb], w=[bgb])
                        k.act(bg[:, 56:64], beta_d, AF.Ln, r=[bgb], w=[bgb])
                        k.tt("dve", bg[:, 56:64], bg[:, 56:64], bg[:, 32:40], ALU.add, r=[bgb], w=[bgb])
                        k.tt("dve", sc[:, 8:16], beta_d, bg[:, 48:56], ALU.mult, r=[bgb], w=[scb])
                        ra, rab = rhsA[0]
                        rav = ra[:, 0, 0:16].rearrange("p (h c) -> p h c", c=2)
                        k.tt("dve", rav, bc(g_d, 2, 2), bc(C["gch"][:, 0:2], 1, 8), ALU.mult, r=[bgb, C["b"]], w=[rab])
                        pt, pb = k.psum()
                        k.mm(pt[:, 0:16], C["ones"][:, :], ra[:, 0, 0:16], start=True, stop=True, r=[rab, C["b"]], w=[pb])
                        k.act(sc[:, 24:40], pt[:, 0:16], AF.Exp, r=[pb], w=[scb])
                        k.tt("pool", vb_[:, :, :], vtok[:, :, :], bc(beta_d, 2, 256), ALU.mult, r=[vtb, bgb], w=[vbb])
                        k.tt("dve", kbg[:, :, :], ktok[:, :, :], bc(sc[:, 8:16], 2, 128), ALU.mult, r=[ktb, scb], w=[kbgb])
                        k.tt("pool", kdec[:, :, :], ktok[:, :, :], bc(sc[:, 16:24], 2, 128), ALU.mult, r=[ktb, scb], w=[kdb])
                        def quad_gen(qd):
                            h0 = qd * 4
                            ra, rab = rhsA[qd]
                            d1_, d1b = d1[qd]
                            d2_, d2b = d2[qd]
                            kkm, kkmb = kkms[qd]
                            Pacc, Pab = Paccs[qd]
                            X = Xs[qd]
                            XT = XTs[qd]
                            for i in range(4):
                                k.act(ra[:, i, :], TRIc[:, :], AF.Identity, r=[C["b"], bgb], w=[rab],
                                      scale=g_d[:, h0 + i:h0 + i + 1])
                            R, Rb = k.psum()
                            k.mm(R[:, :], C["ones"][:, :], ra[:, :, :].rearrange("p h q -> p (h q)"), start=True, stop=True,
                                 r=[rab, C["b"]], w=[Rb])
                            Rv = R[:, :].rearrange("p (h q) -> p h q", q=128)
                            pk, pkb = k.psum()
                            for i in range(4):
                                k.mm(pk[:, i * 128:(i + 1) * 128], qkT[:, 8 + h0 + i, :], qkT[:, 8 + h0 + i, :],
                                     start=True, stop=True, r=[qkb], w=[pkb])
                            yield
                            k.tt("dve", d1_[:, :, :], Rv, bc(bg[:, 40 + h0:44 + h0], 2, 128), ALU.add, r=[Rb, bgb], w=[d1b])
                            k.ts("dve", d1_[:, :, :], d1_[:, :, :], 0.0, None, ALU.min, ALU.bypass, r=[d1b], w=[d1b])
                            k.stt("dve", d2_[:, :, :], Rv, -1.0, bc(bg[:, 56 + h0:60 + h0], 2, 128), ALU.mult, ALU.add,
                                  r=[Rb, bgb], w=[d2b])
                            k.ts("dve", d2_[:, :, :], d2_[:, :, :], 0.0, None, ALU.min, ALU.bypass, r=[d2b], w=[d2b])
                            k.tt("dve", kkm[:, :, :], pk[:, :].rearrange("p (h q) -> p h q", q=128), bc(NMSL[:, :], 1, 4),
                                 ALU.mult, r=[pkb, C["b"]], w=[kkmb])
                            yield
                            k.act(d2_[:, :, :], d2_[:, :, :], AF.Exp, r=[d2b], w=[d2b])
                            k.act(d1_[:, :, :], d1_[:, :, :], AF.Exp, r=[d1b], w=[d1b])
                            k.tt("pool", d1_[:, :, :], d1_[:, :, :], bc(TRIc[:, :], 1, 4), ALU.mult, r=[d1b, C["b"]], w=[d1b])
                            yield
                            xt, xtb = XT[0]
                            x_, xb_ = X[0]
                            k.tt("dve", xt[:, :, :], d2_[:, :, :], kkm[:, :, :], ALU.mult, r=[d2b, kkmb], w=[xtb])
                            pn, pnb = k.psum()
                            if CDT == BF16:
                                pnf = pn[:, :].bitcast(BF16)
                                for i in range(4):
                                    k.tr(pnf[:, i * 128:(i + 1) * 128], xt[:, i, :], C["identb"][:, :], r=[xtb, C["b"]], w=[pnb])
                                pnv = pnf[:, 0:512].rearrange("p (h q) -> p h q", q=128)
                            else:
                                for i in range(4):
                                    k.tr(pn[:, i * 128:(i + 1) * 128], xt[:, i, :], C["ident"][:, :], r=[xtb, C["b"]], w=[pnb])
                                pnv = pn[:, :].rearrange("p (h q) -> p h q", q=128)
                            PaccB, PaBb = PaccBs[qd]
                            yield
                            k.cp("act", x_[:, :, :], pnv, r=[pnb], w=[xb_])
                            k.tt("dve", Pacc[:, :, :], pnv, bc(C["ident"][:, :], 1, 4), ALU.add, r=[pnb, C["b"]], w=[Pab])
                            if CDT == BF16:
                                k.cp("act", PaccB[:, :, :], Pacc[:, :, :], r=[Pab], w=[PaBb])
                            cur = 0
                            for lvl in range(1, 6):
                                x_, xb_ = X[cur]
                                xt, xtb = XT[cur]
                                x2, x2b = X[1 - cur]
                                xt2, xt2b = XT[1 - cur]
                                p1, p1b = k.psum()
                                for i in range(4):
                                    k.mm(p1[:, i * 128:(i + 1) * 128], x_[:, i, :], xt[:, i, :], start=True, stop=True,
                                         r=[xb_, xtb], w=[p1b])
                                if lvl < 5:
                                    p2, p2b = k.psum()
                                    for i in range(4):
                                        k.mm(p2[:, i * 128:(i + 1) * 128], xt[:, i, :], x_[:, i, :], start=True, stop=True,
                                             r=[xb_, xtb], w=[p2b])
                                yield
                                k.cp("act", xt2[:, :, :], p1[:, :].rearrange("p (h q) -> p h q", q=128), r=[p1b], w=[xt2b])
                                if lvl < 5:
                                    k.cp("dve", x2[:, :, :], p2[:, :].rearrange("p (h q) -> p h q", q=128), r=[p2b], w=[x2b])
                                yield
                                p3, p3b = k.psum()
                                for i in range(4):
                                    k.mm(p3[:, i * 128:(i + 1) * 128], xt2[:, i, :], PaccB[:, i, :], start=True, stop=True,
                                         r=[xt2b, PaBb], w=[p3b])
                                yield
                                k.tt("dve", Pacc[:, :, :], Pacc[:, :, :], p3[:, :].rearrange("p (h q) -> p h q", q=128),
                                     ALU.add, r=[Pab, p3b], w=[Pab])
                                if lvl < 5 and CDT == BF16:
                                    k.cp("act", PaccB[:, :, :], Pacc[:, :, :], r=[Pab], w=[PaBb])
                                cur = 1 - cur
                            k.cp("act", TTb[:, h0:h0 + 4, :], Pacc[:, :, :], r=[Pab], w=[TTbbs[qd]])
                            pq, pqb = k.psum()
                            for i in range(4):
                                k.mm(pq[:, i * 128:(i + 1) * 128], qkT[:, 8 + h0 + i, :], qkT[:, h0 + i, :], start=True, stop=True,
                                     r=[qkb], w=[pqb])
                            yield
                            k.tt("dve", qkm[:, h0:h0 + 4, :], pq[:, :].rearrange("p (h q) -> p h q", q=128), d1_[:, :, :],
                                 ALU.mult, r=[pqb, d1b], w=[qkmbs[qd]])
                            for i2 in range(2):
                                pu, pub = k.psum()
                                for i in range(2):
                                    h = h0 + i2 * 2 + i
                                    k.mm(pu[:, i * 256:(i + 1) * 256], TTb[:, h, :], vb_[:, h, :], start=True, stop=True,
                                         r=[TTbbs[qd], vbb], w=[pub])
                                k.cp("act", u[:, h0 + i2 * 2:h0 + i2 * 2 + 2, :], pu[:, :].rearrange("p (h v) -> p h v", v=256),
                                     r=[pub], w=[ubs[qd]])
                            pw, pwb = k.psum()
                            for i in range(4):
                                k.mm(pw[:, i * 128:(i + 1) * 128], kbg[:, h0 + i, :], TTb[:, h0 + i, :], start=True, stop=True,
                                     r=[kbgb, TTbbs[qd]], w=[pwb])
                            yield
                            k.cp("act", wT[:, h0:h0 + 4, :], pw[:, :].rearrange("p (h q) -> p h q", q=128), r=[pwb], w=[wTbs[qd]])

                        run_interleaved([quad_gen(0), quad_gen(1)], 2)
                        chunks = [0, 1] if dirn == 0 else [1, 0]

                        def head_gen(h):
                            St, Stb = Sst[h]
                            Sb, Sbb = Sbf[h]
                            qd = h // 4
                            for ci in chunks:
                                rs = slice(ci * 64, (ci + 1) * 64)
                                pa, pab_ = k.psum()
                                k.mm(pa[:, 0:256], wT[:, h, :], Sb[:, :], start=True, stop=True, r=[wTbs[qd], Sbb], w=[pab_])
                                k.mm(pa[:, 256:512], qkT[:, h, :], Sb[:, :], start=True, stop=True, r=[qkb, Sbb], w=[pab_])
                                yield
                                k.tt("dve", delta[rs, h, :], u[rs, h, :], pa[rs, 0:256], ALU.subtract, r=[ubs[qd], pab_], w=[dlbs[h]])
                                k.act(o_[rs, h, :], pa[rs, 256:512], AF.Identity, r=[pab_, bgb], w=[obs[h]],
                                      scale=bg[rs, 48 + h:49 + h])
                                yield
                                pS, pSb = k.psum()
                                k.mm(pS[:, 0:256], kdec[rs, h, :], delta[rs, h, :], start=True, stop=True, r=[kdb, dlbs[h]], w=[pSb])
                                yield
                                k.stt("dve", St[:, :], St[:, :], sc[:, 24 + 2 * h + ci:25 + 2 * h + ci], pS[:, 0:256],
                                      ALU.mult, ALU.add, r=[Stb, scb, pSb], w=[Stb])
                                k.cp("act", Sb[:, :], St[:, :], r=[Stb], w=[Sbb])
                                yield
                            po, pob = k.psum()
                            k.mm(po[:, 0:256], qkm[:, h, :], delta[:, h, :], start=True, stop=True, r=[qkmbs[qd], dlbs[h]], w=[pob])
                            yield
                            k.tt("dve", o_[:, h, :], o_[:, h, :], po[:, 0:256], ALU.add, r=[obs[h], pob], w=[obs[h]])

                        run_interleaved([head_gen(h) for h in range(8)], 4)
                        if not final:
                            S.dma("sp", SCR["of"][tsl, :].rearrange("t (h v) -> t h v", v=256), o_[:, :, :],
                                  r=obs, w=[SCR["ofb"]])
                        else:
                            k.tt("dve", o_[:, :, :], o_[:, :, :], of_[:, :, :], ALU.add, r=obs + [ofb], w=obs)
                            k.tt("pool", of_[:, :, :], o_[:, :, :], o_[:, :, :], ALU.mult, r=obs, w=[ofb])
                            S.op("dve", lambda e: e.reduce_sum(ss[:, 0:8], of_[:, :, :], mybir.AxisListType.X), r=[ofb], w=[ssb])
                            k.ts("dve", ss[:, 8:16], ss[:, 0:8], 1.0 / 256, 1e-6, ALU.mult, ALU.add, r=[ssb], w=[ssb])
                            k.act(ss[:, 8:16], ss[:, 8:16], AF.Sqrt, r=[ssb], w=[ssb])
                            S.op("dve", lambda e: e.reciprocal(ss[:, 8:16], ss[:, 8:16]), r=[ssb], w=[ssb])
                            k.tt("dve", on16[:, :, :], o_[:, :, :], bc(ss[:, 8:16], 2, 256), ALU.mult, r=obs + [ssb], w=[onb])
                            for half in range(2):
                                pt, pb = k.psum()
                                ptv = pt[:, :].bitcast(BF16)
                                for c8 in range(8):
                                    c = half * 8 + c8
                                    k.tr(ptv[:, c8 * 128:(c8 + 1) * 128], on16[:, c // 2, (c % 2) * 128:(c % 2 + 1) * 128],
                                         C["identb"][:, :], r=[onb, C["b"]], w=[pb])
                                for c8 in range(8):
                                    c = half * 8 + c8
                                    k.stt("dve", g16[:, c, :], ptv[:, c8 * 128:(c8 + 1) * 128], normg[:, c % 2:c % 2 + 1],
                                          zst[:, c, :], ALU.mult, ALU.mult, r=[pb, ngb, zsb_], w=[g16b])
                            S.dma("sp", SCR["gt"][:, :, tsl].rearrange("c p t -> p c t"), g16[:, :, :], r=[g16b], w=[SCR["gtb"]])
                    if grp == 0:
                        for h in range(8):
                            S.dma("sp", O["new_state_delta"][sq_i, 0, dirn, h], Sst[h][0][:, :], r=[Sst[h][1]])
                S.barrier()
        S.barrier()
    with ExitStack() as es:
        GT, GTb = k.sb(es, "gd_GT", [128, 16, T], BF16)
        S.dma("sp", GT[:, :, :], SCR["gt"][:, :, 0:T].rearrange("c p t -> p c t"), r=[SCR["gtb"]], w=[GTb])
        emit_mixer_tail(k, C, W["gdn_w_out"][0], 16, GT, GTb, yT, yb, T, mod["gate"], modb, lncols, lnb)


IN_SPECS = {
    "x_prompt": (TP, D), "x_sample": (TS, D), "c": (1, D), "c_ctx": (D,),
    "state_ssd": (2, 2, 32, 64, 128), "state_delta": (1, 2, 8, 128, 256),
    "cache_k": (256, 256), "cache_v": (256, 256),
    "w_mod": (DEPTH, D, 9 * D), "b_mod": (DEPTH, 9 * D), "ln_g": (DEPTH, 3, D), "ln_b": (DEPTH, 3, D),
    "ffn_w_gate": (DEPTH, 2, D, DFF), "ffn_w_up": (DEPTH, 2, D, DFF), "ffn_w_down": (DEPTH, 2, DFF, D),
    "ssd_w_in": (2, D, 5184), "ssd_conv_w": (2, 5, 3072), "ssd_conv_b": (2, 3072), "ssd_dt_bias": (2, 2, 32),
    "ssd_a_log": (2, 2, 32), "ssd_d": (2, 32), "ssd_norm": (2, 2048), "ssd_w_out": (2, 2048, D),
    "gdn_w_in": (1, D, 6176), "gdn_conv_w": (1, 5, 4096), "gdn_conv_b": (1, 4096), "gdn_dt_bias": (1, 2, 8),
    "gdn_a_log": (1, 2, 8), "gdn_norm": (1, 256), "gdn_w_out": (1, 2048, D),
    "attn_w_in": (1, D, 1536), "attn_sink": (1, 16), "attn_w_out": (1, D, D),
}
OUT_SPECS = {
    "y_prompt": (TP, D), "y_sample": (TS, D),
    "new_state_ssd": (4, 2, 2, 32, 64, 128), "new_state_delta": (4, 1, 2, 8, 128, 256),
    "new_cache_k": (TP, 256), "new_cache_v": (TP, 256),
}
MIXER_OF_LAYER = {0: "ssd", 1: "gdn", 2: "attn", 3: "ssd"}


def build_program(cfg):
    nc = bass.Bass("TRN2", target_bir_lowering=False)
    names = cfg.get("inputs", list(IN_SPECS))
    W = {}
    for name in names:
        W[name] = nc.dram_tensor(name, list(IN_SPECS[name]), F32, kind="ExternalInput").ap()
    hc = host_consts()
    for name, arr in hc.items():
        W["c_" + name] = nc.dram_tensor("c_" + name, list(arr.shape), F32, kind="ExternalInput").ap()
    O = {}
    for name in cfg.get("outputs", list(OUT_SPECS)):
        O[name] = nc.dram_tensor(name, list(OUT_SPECS[name]), F32, kind="ExternalOutput").ap()
    SCR = {}
    for nm, shp, dt_ in (("xc", [32, 128, TS], BF16), ("zs", [16, 128, TS], BF16), ("yf", [16, 128, TS], F32),
                         ("gt", [16, 128, TS], BF16), ("of", [TS, 2048], F32)):
        SCR[nm] = nc.dram_tensor("scr_" + nm, shp, dt_).ap()
        SCR[nm + "b"] = Buf("scr_" + nm)
    layers = cfg.get("layers", list(range(DEPTH)))
    stages = cfg.get("stages", (0, 1, 2))
    mixer = dict(MIXER_OF_LAYER)
    mixer.update(cfg.get("mixer", {}))

    with ExitStack() as es:
        S = Sy(nc, es)
        k = K(nc, es, S)
        C = {"b": Buf("consts")}
        for name, arr in hc.items():
            if name in ("cos", "sin"):
                continue
            t, _ = k.sb(es, "k_" + name, list(arr.shape), F32)
            C[name] = t
            S.dma("sp", t[:, :], W["c_" + name][:, :], w=[C["b"]])
        idb, _ = k.sb(es, "k_identb", [128, 128], BF16)
        C["identb"] = idb
        S.dma("pool", idb[:, :], W["c_ident"][:, :], w=[C["b"]])
        onb_, _ = k.sb(es, "k_onesb", [128, 128], BF16)
        C["onesb"] = onb_
        S.dma("pool", onb_[:, :], W["c_ones"][:, :], w=[C["b"]])
        S.barrier()

        condT, condb = k.sb(es, "condT", [128, KC, 2], BF16)
        with ExitStack() as es1:
            cf, cfb = k.sb(es1, "cond_f", [128, 16], F32)
            st, stb = k.sb(es1, "cond_st", [16, 128], F32)
            S.dma("sp", st[0:8, :], W["c_ctx"].rearrange("(r p) -> r p", p=128), w=[stb])
            S.dma("sp", st[8:16, :], W["c"][0].rearrange("(r p) -> r p", p=128), w=[stb])
            pt, pb = k.psum()
            k.tr(pt[:, 0:16], st[0:16, :], C["ident"][0:16, 0:16], r=[stb, C["b"]], w=[pb])
            k.act(cf[:, :], pt[:, 0:16], AF.Silu, r=[pb], w=[cfb])
            k.cp("dve", condT[:, :, :], cf[:, :].rearrange("p (g c) -> p c g", g=2), r=[cfb], w=[condb])
            S.barrier()

        modT, modb = k.sb(es, "modT", [128, DEPTH, 2, 72], F32)
        for l in layers:
            S.mark("adaln%d" % l)
            emit_adaln(k, C, W, condT, condb, modT, modb, l)
        for l in layers:
            for g in range(2):
                for srow in (1, 4, 7):
                    k.ts("dve", modT[:, l, g, srow * 8:(srow + 1) * 8], modT[:, l, g, srow * 8:(srow + 1) * 8],
                         1.0, None, ALU.add, ALU.bypass, r=[modb], w=[modb])
                for srow, f in ((2, 0.5 / ALPHA), (5, 1.0 / ALPHA), (8, 0.5 / ALPHA)):
                    k.ts("dve", modT[:, l, g, srow * 8:(srow + 1) * 8], modT[:, l, g, srow * 8:(srow + 1) * 8],
                         f, None, ALU.mult, ALU.bypass, r=[modb], w=[modb])
        lnT, lnb = k.sb(es, "lnT", [128, DEPTH * 3 * 2 * 8], F32)
        emit_load_cols(k, lnT[:, 0:96], lnb, W["ln_g"].rearrange("l s (r p) -> (l s r) p", p=128), 96, C)
        emit_load_cols(k, lnT[:, 96:192], lnb, W["ln_b"].rearrange("l s (r p) -> (l s r) p", p=128), 96, C)

        def lncols(l, i):
            o = (l * 3 + i) * 8
            return (lnT[:, o:o + 8], lnT[:, 96 + o:96 + o + 8])

        def modcols(l, g, s3):
            return {"shift": modT[:, l, g, (3 * s3) * 8:(3 * s3 + 1) * 8],
                    "sc1": modT[:, l, g, (3 * s3 + 1) * 8:(3 * s3 + 2) * 8],
                    "gate": modT[:, l, g, (3 * s3 + 2) * 8:(3 * s3 + 3) * 8]}

        yT, yb = k.sb(es, "yT", [128, KC, TS], F32)

        for g, (T, xname, oname) in enumerate(((TP, "x_prompt", "y_prompt"), (TS, "x_sample", "y_sample"))):
            if g not in cfg.get("groups", (0, 1)):
                continue
            S.mark("g%d loadx" % g)
            emit_load_x(k, C, yT, yb, W[xname], T)
            for l in layers:
                if 0 in stages:
                    for t0 in range(0, T, 1024):
                        S.mark("g%d l%d ffn0 t%d" % (g, l, t0))
                        emit_ffn(k, C, W, yT, yb, t0, l, 0, modcols(l, g, 0), modb, lncols(l, 0), lnb)
                if 1 in stages:
                    kind = mixer[l]
                    S.mark("g%d l%d mixer %s" % (g, l, kind))
                    if kind == "attn":
                        emit_attn(k, C, W, O, yT, yb, T, g, modcols(l, g, 1), modb, lncols(l, 1), lnb)
                    elif kind == "ssd":
                        emit_ssd(k, C, W, O, SCR, yT, yb, T, g, l // 3, modcols(l, g, 1), modb, lncols(l, 1), lnb)
                    elif kind == "gdn":
                        emit_gdn(k, C, W, O, SCR, yT, yb, T, g, modcols(l, g, 1), modb, lncols(l, 1), lnb)
                if 2 in stages:
                    for t0 in range(0, T, 1024):
                        S.mark("g%d l%d ffn1 t%d" % (g, l, t0))
                        emit_ffn(k, C, W, yT, yb, t0, l, 1, modcols(l, g, 2), modb, lncols(l, 2), lnb)
            S.mark("g%d store" % g)
            emit_store_y(k, C, yT, yb, O[oname], T)
        S.mark("end")
        S.final_wait()
        global LAST_MARKS
        LAST_MARKS = S.marks
        print("instructions:", S.ninst)
    return nc


def make_in_maps(inputs, names):
    hc = host_consts()
    maps = []
    for i in range(NCORES):
        m = {}
        for name in names:
            a = inputs[name]
            if name == "x_prompt":
                a = a[4 * i:4 * i + 4].reshape(TP, D)
            elif name == "x_sample":
                a = a[i].reshape(TS, D)
            elif name == "c":
                a = a[i:i + 1]
            elif name in ("state_ssd", "state_delta"):
                a = a[i]
            elif name in ("cache_k", "cache_v"):
                a = a[i, 0].reshape(256, 256)
            m[name] = np.ascontiguousarray(a, dtype=np.float32)
        for name, arr in hc.items():
            m["c_" + name] = arr
        maps.append(m)
    return maps


def run(inputs, cfg):
    nc = build_program(cfg)
    maps = make_in_maps(inputs, cfg.get("inputs", list(IN_SPECS)))
    res = run_bass_kernel_spmd(nc, maps, core_ids=list(range(NCORES)))
    return res.results


def kernel(**inputs):
    inputs = {k_: np.asarray(v) for k_, v in inputs.items()}
    res = run(inputs, {})
    y_prompt = np.concatenate([r["y_prompt"].reshape(4, 256, D) for r in res], axis=0)
    y_sample = np.stack([r["y_sample"].reshape(TS, D) for r in res], axis=0)
    nss = np.concatenate([r["new_state_ssd"] for r in res], axis=0)
    nsd = np.concatenate([r["new_state_delta"] for r in res], axis=0)
    nck = np.concatenate([r["new_cache_k"].reshape(4, 1, 256, 4, 64) for r in res], axis=0)
    ncv = np.concatenate([r["new_cache_v"].reshape(4, 1, 256, 4, 64) for r in res], axis=0)
    return (y_prompt, y_sample, nss, nsd, nck, ncv)
```

```python
import numpy as np
import concourse.bass as bass
import concourse.mybir as mybir
from concourse.bass_utils import run_bass_kernel_spmd
from contextlib import ExitStack

F32 = mybir.dt.float32
BF16 = mybir.dt.bfloat16
AF = mybir.ActivationFunctionType
ALU = mybir.AluOpType

D = 1024
KC = 8
DFF = 2816
FC = 22
DEPTH = 4
ALPHA = (2 * DEPTH) ** 0.25
LN_EPS = 1e-5 / (ALPHA * ALPHA)
NCORES = 8
TP = 1024
TS = 2048


class Buf:
    __slots__ = ("w", "r", "name", "excl")

    def __init__(self, name="", excl=False):
        self.w = None
        self.r = {}
        self.name = name
        self.excl = excl


class Sy:
    SAME = {"pe": False, "dve": True, "act": True, "pool": True, "sp": False}

    def __init__(self, nc, es, ndma=12):
        self.nc = nc
        self.E = {"pe": nc.tensor, "dve": nc.vector, "act": nc.scalar, "pool": nc.gpsimd, "sp": nc.sync}
        self.sem = {k: es.enter_context(nc.semaphore("s_" + k)) for k in self.E}
        self.cnt = {k: 0 for k in self.E}
        self.waited = {k: {} for k in self.E}
        self.dsem = {}
        for q in ("sp", "pool"):
            self.dsem[q] = [[es.enter_context(nc.semaphore("d_%s%d" % (q, i))), 0] for i in range(ndma)]
        self.drr = {q: 0 for q in self.dsem}
        self.ninst = 0

    def _wait(self, e, tok):
        sem, val, src = tok
        if src == e and not self.SAME[e]:
            return
        k = sem.num
        if self.waited[e].get(k, 0) >= val:
            return
        self.E[e].wait_ge(sem, val)
        self.waited[e][k] = val

    def _deps(self, e, reads, writes):
        for b in reads:
            if b.w is not None:
                self._wait(e, b.w)
        for b in writes:
            if b.w is not None:
                self._wait(e, b.w)
            for t in b.r.values():
                self._wait(e, t)

    def _commit(self, tok, key, reads, writes):
        for b in writes:
            b.w = tok
            b.r = {}
        for b in reads:
            if b not in writes:
                b.r[key] = tok

    def op(self, e, fn, r=(), w=()):
        ex = [b for b in r if b.excl]
        if ex:
            w = list(w) + [b for b in ex if b not in w]
        self._deps(e, r, w)
        ins = fn(self.E[e])
        self.cnt[e] += 1
        ins.then_inc(self.sem[e], 1)
        self._commit((self.sem[e], self.cnt[e], e), e, r, w)
        self.ninst += 1
        return ins

    def dma(self, q, out, in_, r=(), w=()):
        slot = self.drr[q] % len(self.dsem[q])
        self.drr[q] += 1
        ent = self.dsem[q][slot]
        sem = ent[0]
        if ent[1] > 0:
            self._wait(q, (sem, ent[1], "dma"))
        self._deps(q, r, w)
        ins = self.E[q].dma_start(out=out, in_=in_)
        ent[1] += 16
        ins.then_inc(sem, 16)
        self._commit((sem, ent[1], "dma"), (q, slot), r, w)
        self.ninst += 1
        return ins

    def mark(self, name):
        if not hasattr(self, "marks"):
            self.marks = []
        d = dict(self.cnt)
        d["pe_slices"] = getattr(self, "pe_slices", 0)
        self.marks.append((name, d))

    def barrier(self):
        for e in self.E:
            for e2 in self.E:
                if self.cnt[e2] > 0 and (e2 != e or self.SAME[e]):
                    self._wait(e, (self.sem[e2], self.cnt[e2], e2))
            for q in self.dsem:
                for ent in self.dsem[q]:
                    if ent[1] > 0:
                        self._wait(e, (ent[0], ent[1], "dma"))

    def final_wait(self):
        e = "sp"
        for e2 in self.E:
            if e2 != e and self.cnt[e2] > 0:
                self._wait(e, (self.sem[e2], self.cnt[e2], e2))
        for q in self.dsem:
            for ent in self.dsem[q]:
                if ent[1] > 0:
                    self._wait(e, (ent[0], ent[1], "dma"))


class K:
    def __init__(self, nc, es, S):
        self.nc = nc
        self.es = es
        self.S = S
        self.ps = []
        for i in range(8):
            t = es.enter_context(nc.psum_tensor("psb%d" % i, [128, 512], F32))
            self.ps.append((t, Buf("ps%d" % i, excl=True)))
        self.prr = 0

    def psum(self):
        p = self.ps[self.prr % 8]
        self.prr += 1
        return p

    def sb(self, es, name, shape, dt):
        self.uid = getattr(self, "uid", 0) + 1
        name = "%s_%d" % (name, self.uid)
        t = es.enter_context(self.nc.sbuf_tensor(name, shape, dt))
        return t, Buf(name)

    def mm(self, out, lhsT, rhs, start, stop, r, w):
        self.S.pe_slices = getattr(self.S, "pe_slices", 0) + (2 if lhsT.dtype == F32 else 1)
        return self.S.op("pe", lambda e: e.matmul(out, lhsT, rhs, start=start, stop=stop), r=r, w=w)

    def tr(self, out, in_, ident, r, w):
        self.S.pe_slices = getattr(self.S, "pe_slices", 0) + 1
        return self.S.op("pe", lambda e: e.transpose(out, in_, ident), r=r, w=w)

    def ts(self, eng, out, in0, s1, s2, op0, op1, r, w):
        return self.S.op(eng, lambda e: e.tensor_scalar(out, in0, s1, s2, op0, op1), r=r, w=w)

    def stt(self, eng, out, in0, scalar, in1, op0, op1, r, w):
        eng = "dve"
        return self.S.op(eng, lambda e: e.scalar_tensor_tensor(out, in0, scalar, in1, op0, op1), r=r, w=w)

    def tt(self, eng, out, in0, in1, op, r, w):
        return self.S.op(eng, lambda e: e.tensor_tensor(out, in0, in1, op), r=r, w=w)

    def cp(self, eng, out, in_, r, w):
        if eng == "act":
            return self.S.op("act", lambda e: e.copy(out, in_), r=r, w=w)
        return self.S.op(eng, lambda e: e.tensor_copy(out, in_), r=r, w=w)

    def act(self, out, in_, func, r, w, bias=None, scale=None):
        kw = {}
        if bias is not None:
            kw["bias"] = bias
        if scale is not None:
            kw["scale"] = scale
        return self.S.op("act", lambda e: e.activation(out, in_, func, **kw), r=r, w=w)

    def memset(self, eng, ap, val, w):
        return self.S.op(eng, lambda e: e.memset(ap, val), w=w)


def emit_load_cols(k, dst_ap, dst_buf, src2d, R, C, Wd=128):
    S = k.S
    with ExitStack() as es:
        st, stb = k.sb(es, "lc_st", [128, 128], F32)
        S.dma("sp", st[0:R, 0:Wd], src2d, w=[stb])
        pt, pb = k.psum()
        k.tr(pt[0:Wd, 0:R], st[0:R, 0:Wd], C["ident"][0:R, 0:R], r=[stb, C["b"]], w=[pb])
        k.cp("dve", dst_ap, pt[0:Wd, 0:R], r=[pb], w=[dst_buf])
        S.barrier()


def emit_adaln(k, C, W, condT, condb, modT, modb, l):
    S = k.S
    with ExitStack() as es:
        bm, bmb = k.sb(es, "ad_bm", [128, 72], F32)
        emit_load_cols(k, bm[:, :], bmb, W["b_mod"][l].rearrange("(r p) -> r p", p=128), 72, C)
        wm = [k.sb(es, "ad_w%d" % i, [128, KC, 512], BF16) for i in range(2)]
        pm, pmb = k.psum()
        pmv = pm[:, 0:144].rearrange("p (j g) -> p j g", g=2)
        for blk in range(18):
            wt, wb = wm[blk % 2]
            S.dma("pool", wt[:, :, :],
                  W["w_mod"][l][:, blk * 512:(blk + 1) * 512].rearrange("(kc p) o -> p kc o", p=128), w=[wb])
            for j4 in range(4):
                j = blk * 4 + j4
                for kc in range(KC):
                    k.mm(pmv[:, j, :], wt[:, kc, j4 * 128:(j4 + 1) * 128], condT[:, kc, :],
                         start=(kc == 0), stop=(kc == KC - 1), r=[wb, condb], w=[pmb])
        for g in range(2):
            k.tt("dve", modT[:, l, g, :], pmv[:, :, g], bm[:, :], ALU.add, r=[pmb, bmb], w=[modb])
        S.barrier()


def emit_ln(k, C, yT, yb, t0, T, gcol, bcol, cb):
    S = k.S
    with ExitStack() as es:
        sq = [k.sb(es, "ln_sq%d" % i, [128, 512], F32) for i in range(2)]
        hl = [[k.sb(es, "ln_hl%d_%d" % (j, i), [128, 512], BF16) for i in range(2)] for j in range(4)]
        mean, meanb = k.sb(es, "ln_mean", [128, 512], F32)
        rstd, rstdb = k.sb(es, "ln_rstd", [128, 512], F32)
        tmp = [k.sb(es, "ln_tmp%d" % i, [128, 512], F32) for i in range(2)]
        ybs = [Buf("ln_y%d" % i) for i in range(KC)]
        for tg in range(T // 512):
            sl = slice(t0 + tg * 512, t0 + (tg + 1) * 512)
            p1, p1b = k.psum()
            p2, p2b = k.psum()
            for kc in range(KC):
                q, qb = sq[kc % 2]
                hi, hib = hl[0][kc % 2]
                lo, lob = hl[1][kc % 2]
                qhi, qhib = hl[2][kc % 2]
                qlo, qlob = hl[3][kc % 2]
                zc = yT[:, kc, sl]
                k.cp("act", hi[:, :], zc, r=[yb], w=[hib])
                k.act(q[:, :], zc, AF.Square, r=[yb], w=[qb])
                k.tt("dve", lo[:, :], zc, hi[:, :], ALU.subtract, r=[yb, hib], w=[lob])
                k.cp("pool", qhi[:, :], q[:, :], r=[qb], w=[qhib])
                k.tt("dve", qlo[:, :], q[:, :], qhi[:, :], ALU.subtract, r=[qb, qhib], w=[qlob])
                k.mm(p1[:, :], C["onesb"][:, :], hi[:, :], start=(kc == 0), stop=False, r=[hib, C["b"]], w=[p1b])
                k.mm(p1[:, :], C["onesb"][:, :], lo[:, :], start=False, stop=(kc == KC - 1), r=[lob, C["b"]], w=[p1b])
                k.mm(p2[:, :], C["onesb"][:, :], qhi[:, :], start=(kc == 0), stop=False, r=[qhib, C["b"]], w=[p2b])
                k.mm(p2[:, :], C["onesb"][:, :], qlo[:, :], start=False, stop=(kc == KC - 1), r=[qlob, C["b"]], w=[p2b])
            k.ts("dve", mean[:, :], p1[:, :], 1.0 / D, None, ALU.mult, ALU.bypass, r=[p1b], w=[meanb])
            t, tb = tmp[0]
            k.tt("dve", t[:, :], mean[:, :], mean[:, :], ALU.mult, r=[meanb], w=[tb])
            k.stt("dve", rstd[:, :], p2[:, :], 1.0 / D, t[:, :], ALU.mult, ALU.subtract, r=[p2b, tb], w=[rstdb])
            k.ts("dve", rstd[:, :], rstd[:, :], LN_EPS, None, ALU.add, ALU.bypass, r=[rstdb], w=[rstdb])
            k.act(rstd[:, :], rstd[:, :], AF.Sqrt, r=[rstdb], w=[rstdb])
            k.S.op("dve", lambda e: e.reciprocal(rstd[:, :], rstd[:, :]), r=[rstdb], w=[rstdb])
            for kc in range(KC):
                t, tb = tmp[kc % 2]
                k.tt("dve", t[:, :], yT[:, kc, sl], mean[:, :], ALU.subtract, r=[yb, meanb], w=[tb])
                k.tt("pool" if kc % 2 else "dve", t[:, :], t[:, :], rstd[:, :], ALU.mult, r=[tb, rstdb], w=[tb])
                k.act(yT[:, kc, sl], t[:, :], AF.Identity, r=[tb, cb], w=[ybs[kc]],
                      bias=bcol[:, kc:kc + 1], scale=gcol[:, kc:kc + 1])
        S.barrier()


def emit_ffn(k, C, W, yT, yb, t0, l, s, mod, modb, lncols, lnb):
    S = k.S
    T = 1024
    with ExitStack() as es:
        hT, hb = k.sb(es, "f_hT", [128, KC, T], BF16)
        aT, ab = k.sb(es, "f_aT", [128, FC, T], BF16)
        for kc in range(KC):
            if kc % 2 == 0:
                k.ts("dve", hT[:, kc, :], yT[:, kc, t0:t0 + T],
                     mod["sc1"][:, kc:kc + 1], mod["shift"][:, kc:kc + 1], ALU.mult, ALU.add, r=[yb, modb], w=[hb])
            else:
                k.act(hT[:, kc, :], yT[:, kc, t0:t0 + T], AF.Identity, r=[yb, modb], w=[hb],
                      bias=mod["shift"][:, kc:kc + 1], scale=mod["sc1"][:, kc:kc + 1])
        with ExitStack() as es2:
            NB = 2
            wg = [k.sb(es2, "f_wg%d" % i, [128, KC, 256], BF16) for i in range(NB)]
            wu = [k.sb(es2, "f_wu%d" % i, [128, KC, 256], BF16) for i in range(NB)]
            sg = [k.sb(es2, "f_sg%d" % i, [128, 512], F32) for i in range(2)]

            def load(gi):
                c0 = gi * 256
                S.dma("pool", wg[gi % NB][0][:, :, :],
                      W["ffn_w_gate"][l, s][:, c0:c0 + 256].rearrange("(kc p) o -> p kc o", p=128),
                      w=[wg[gi % NB][1]])
                S.dma("pool", wu[gi % NB][0][:, :, :],
                      W["ffn_w_up"][l, s][:, c0:c0 + 256].rearrange("(kc p) o -> p kc o", p=128),
                      w=[wu[gi % NB][1]])

            wd = [k.sb(es2, "f_wd%d" % i, [128, FC, 256], BF16) for i in range(2)]

            def loadd(gi):
                c0 = gi * 256
                S.dma("pool", wd[gi % 2][0][:, :, :],
                      W["ffn_w_down"][l, s][:, c0:c0 + 256].rearrange("(kc p) o -> p kc o", p=128),
                      w=[wd[gi % 2][1]])

            load(0)
            n = 0
            for gi in range(FC // 2):
                if gi + 1 < FC // 2:
                    load(gi + 1)
                else:
                    loadd(0)
                    loadd(1)
                wgt, wgb = wg[gi % NB]
                wut, wub = wu[gi % NB]
                for o2 in range(2):
                    oc = gi * 2 + o2
                    for tg in range(T // 512):
                        sl = slice(tg * 512, (tg + 1) * 512)
                        pg, pgb = k.psum()
                        pu, pub = k.psum()
                        for kc in range(KC):
                            k.mm(pg[:, :], wgt[:, kc, o2 * 128:(o2 + 1) * 128], hT[:, kc, sl],
                                 start=(kc == 0), stop=(kc == KC - 1), r=[wgb, hb], w=[pgb])
                        for kc in range(KC):
                            k.mm(pu[:, :], wut[:, kc, o2 * 128:(o2 + 1) * 128], hT[:, kc, sl],
                                 start=(kc == 0), stop=(kc == KC - 1), r=[wub, hb], w=[pub])
                        st, stb = sg[n % 2]
                        n += 1
                        k.act(st[:, :], pg[:, :], AF.Silu, r=[pgb], w=[stb])
                        k.tt("dve", aT[:, oc, sl], st[:, :], pu[:, :], ALU.mult, r=[stb, pub], w=[ab])
            for gi in range(4):
                if 1 <= gi < 3:
                    loadd(gi + 1)
                wdt, wdb = wd[gi % 2]
                for o2 in range(2):
                    oc = gi * 2 + o2
                    for tg in range(T // 512):
                        sl = slice(tg * 512, (tg + 1) * 512)
                        ysl = slice(t0 + tg * 512, t0 + (tg + 1) * 512)
                        pf, pfb = k.psum()
                        for kc in range(FC):
                            k.mm(pf[:, :], wdt[:, kc, o2 * 128:(o2 + 1) * 128], aT[:, kc, sl],
                                 start=(kc == 0), stop=(kc == FC - 1), r=[wdb, ab], w=[pfb])
                        k.stt("dve", yT[:, oc, ysl], pf[:, :], mod["gate"][:, oc:oc + 1], yT[:, oc, ysl],
                              ALU.mult, ALU.add, r=[pfb, yb, modb], w=[yb])
            S.barrier()
    emit_ln(k, C, yT, yb, t0, T, lncols[0], lncols[1], lnb)


def emit_load_x(k, C, yT, yb, xsrc, T):
    S = k.S
    with ExitStack() as es:
        xt = [k.sb(es, "lx%d" % i, [128, D], F32) for i in range(2)]
        for tt in range(T // 128):
            x, xb = xt[tt % 2]
            S.dma("sp", x[:, :], xsrc[tt * 128:(tt + 1) * 128, :], w=[xb])
            for h2 in range(2):
                pt, pb = k.psum()
                for c4 in range(4):
                    kc = h2 * 4 + c4
                    k.tr(pt[:, c4 * 128:(c4 + 1) * 128], x[:, kc * 128:(kc + 1) * 128], C["ident"][:, :],
                         r=[xb, C["b"]], w=[pb])
                k.cp("act" if h2 else "dve",
                     yT[:, h2 * 4:(h2 + 1) * 4, tt * 128:(tt + 1) * 128],
                     pt[:, :].rearrange("p (c t) -> p c t", t=128), r=[pb], w=[yb])
        S.barrier()


def emit_store_y(k, C, yT, yb, ydst, T):
    S = k.S
    with ExitStack() as es:
        ot = [k.sb(es, "sy%d" % i, [128, D], F32) for i in range(2)]
        for tt in range(T // 128):
            o, ob = ot[tt % 2]
            for h2 in range(2):
                pt, pb = k.psum()
                for c4 in range(4):
                    kc = h2 * 4 + c4
                    k.tr(pt[:, c4 * 128:(c4 + 1) * 128], yT[:, kc, tt * 128:(tt + 1) * 128], C["ident"][:, :],
                         r=[yb, C["b"]], w=[pb])
                k.cp("act" if h2 else "dve", o[:, h2 * 512:(h2 + 1) * 512], pt[:, :], r=[pb], w=[ob])
            S.dma("sp", ydst[tt * 128:(tt + 1) * 128, :], o[:, :], r=[ob])
        S.barrier()


NEG = -30000.0
LAST_MARKS = None
DBG = {"attn_stop": 9}
C_SCALE = 64 ** -0.5


def host_consts():
    c = {}
    c["ident"] = np.eye(128, dtype=np.float32)
    c["ones"] = np.ones((128, 128), dtype=np.float32)
    ii = np.arange(128)[:, None]
    jj = np.arange(128)[None, :]
    c["mprev"] = np.where(jj >= ii, 0.0, NEG).astype(np.float32)
    c["mnext"] = np.where(jj <= ii, 0.0, NEG).astype(np.float32)
    t = np.arange(TS)
    row = (t // 64).astype(np.float32)
    col = (t % 64).astype(np.float32)
    inv = (10000.0 ** (-np.arange(16, dtype=np.float32) / 16)).astype(np.float32)
    cos = np.zeros((64, TS), np.float32)
    sin = np.zeros((64, TS), np.float32)
    perm = np.zeros((128, 128), np.float32)
    for d in range(64):
        pos = row if d < 32 else col
        f = inv[d % 16]
        ang = (pos * f).astype(np.float32)
        cos[d] = np.cos(ang)
        first = (d % 32) < 16
        sin[d] = -np.sin(ang) if first else np.sin(ang)
        partner = d + 16 if first else d - 16
        perm[partner, d] = 1.0
    c["tri_f"] = (ii <= jj).astype(np.float32)
    c["tri_b"] = (ii >= jj).astype(np.float32)
    sel = np.zeros((1, 256), np.float32)
    sel[0, 0:64] = 1.0
    sel[0, 128 + 64:256] = 1.0
    c["sel"] = sel
    same = (ii // 64) == (jj // 64)
    c["gtri_f"] = (same & (ii <= jj)).astype(np.float32)
    c["gtri_b"] = (same & (ii >= jj)).astype(np.float32)
    c["gnmsl_f"] = -(same & (jj < ii)).astype(np.float32)
    c["gnmsl_b"] = -(same & (jj > ii)).astype(np.float32)
    c["gblk"] = same.astype(np.float32)
    gch = np.zeros((128, 128), np.float32)
    gch[0:64, 0] = 1.0
    gch[64:128, 1] = 1.0
    c["gch"] = gch
    c["cos"] = cos
    c["sin"] = sin
    c["perm"] = perm
    return c


def emit_mixer_tail(k, C, wout, KCI, inT, inb, yT, yb, T, gate5, modb, lncols, lnb):
    S = k.S
    S.mark("  mixer tail")
    with ExitStack() as es:
        wo = [k.sb(es, "mt_w%d" % i, [128, KCI, 256], BF16) for i in range(2)]

        def load(gi):
            S.dma("pool", wo[gi % 2][0][:, :, :],
                  wout[:, gi * 256:(gi + 1) * 256].rearrange("(kc p) o -> p kc o", p=128), w=[wo[gi % 2][1]])

        load(0)
        for gi in range(4):
            if gi + 1 < 4:
                load(gi + 1)
            wt, wb = wo[gi % 2]
            for o2 in range(2):
                oc = gi * 2 + o2
                for tg in range(T // 512):
                    sl = slice(tg * 512, (tg + 1) * 512)
                    pf, pfb = k.psum()
                    for kc in range(KCI):
                        k.mm(pf[:, :], wt[:, kc, o2 * 128:(o2 + 1) * 128], inT[:, kc, sl],
                             start=(kc == 0), stop=(kc == KCI - 1), r=[wb, inb], w=[pfb])
                    k.stt("dve", yT[:, oc, sl], pf[:, :], gate5[:, oc:oc + 1], yT[:, oc, sl],
                          ALU.mult, ALU.add, r=[pfb, yb, modb], w=[yb])
        S.barrier()
    for t0 in range(0, T, 1024):
        emit_ln(k, C, yT, yb, t0, 1024, lncols[0], lncols[1], lnb)


def emit_modulate(k, hT, hb, yT, yb, T, mod, modb):
    for kc in range(KC):
        if kc % 2 == 0:
            k.ts("dve", hT[:, kc, 0:T], yT[:, kc, 0:T],
                 mod["sc1"][:, kc:kc + 1], mod["shift"][:, kc:kc + 1], ALU.mult, ALU.add, r=[yb, modb], w=[hb])
        else:
            k.act(hT[:, kc, 0:T], yT[:, kc, 0:T], AF.Identity, r=[yb, modb], w=[hb],
                  bias=mod["shift"][:, kc:kc + 1], scale=mod["sc1"][:, kc:kc + 1])


def emit_attn(k, C, W, O, yT, yb, T, grp, mod, modb, lncols, lnb):
    S = k.S
    w_in = W["attn_w_in"][0]
    NT = T // 128
    with ExitStack() as es:
        if grp == 1:
            for nm in ("cos", "sin"):
                t_, _ = k.sb(es, "k_" + nm, [64, TS], F32)
                C[nm] = t_
                S.dma("sp", t_[:, :], W["c_" + nm][:, :], w=[C["b"]])
        OT, OTb = k.sb(es, "at_OT", [128, 8, T], BF16)
        sinkbc, sinkb = k.sb(es, "at_sink", [128, 16], F32)
        with ExitStack() as es1:
            s1, s1b = k.sb(es1, "at_s1", [1, 16], F32)
            S.dma("sp", s1[0:1, :], W["attn_sink"][0:1, :], w=[s1b])
            pt, pb = k.psum()
            k.mm(pt[:, 0:16], C["ones"][0:1, :], s1[0:1, :], start=True, stop=True, r=[s1b, C["b"]], w=[pb])
            k.cp("dve", sinkbc[:, :], pt[:, 0:16], r=[pb], w=[sinkb])
            S.barrier()
        with ExitStack() as es1:
            hts = [k.sb(es1, "at_hT%d" % i, [128, KC, 512], BF16) for i in range(2)]
            hcnt = [0]

            def mod_tile(tg):
                hT_, hb_ = hts[hcnt[0] % 2]
                hcnt[0] += 1
                for kc in range(KC):
                    if kc % 2 == 0:
                        k.ts("dve", hT_[:, kc, :], yT[:, kc, tg * 512:(tg + 1) * 512],
                             mod["sc1"][:, kc:kc + 1], mod["shift"][:, kc:kc + 1], ALU.mult, ALU.add,
                             r=[yb, modb], w=[hb_])
                    else:
                        k.act(hT_[:, kc, :], yT[:, kc, tg * 512:(tg + 1) * 512], AF.Identity, r=[yb, modb], w=[hb_],
                              bias=mod["shift"][:, kc:kc + 1], scale=mod["sc1"][:, kc:kc + 1])
                return hT_, hb_
            vtok, vtb = k.sb(es1, "at_vtok", [128, NT, 256], BF16)
            with ExitStack() as es2:
                wkv, wkvb = k.sb(es2, "at_wkv", [128, KC, 512], BF16)
                S.dma("pool", wkv[:, :, :], w_in[:, 1024:1536].rearrange("(kc p) o -> p kc o", p=128), w=[wkvb])
                st = [k.sb(es2, "at_kvst%d" % i, [128, 512], F32) for i in range(2)]
                for tt in range(NT):
                    if tt % 4 == 0:
                        hT, hb = mod_tile(tt // 4)
                    pk, pkb = k.psum()
                    for kc in range(KC):
                        k.mm(pk[:, :], hT[:, kc, (tt % 4) * 128:(tt % 4 + 1) * 128], wkv[:, kc, :],
                             start=(kc == 0), stop=(kc == KC - 1), r=[hb, wkvb], w=[pkb])
                    k.cp("act", vtok[:, tt, :], pk[:, 256:512], r=[pkb], w=[vtb])
                    if grp == 0:
                        s_, sb_ = st[tt % 2]
                        k.cp("dve", s_[:, :], pk[:, :], r=[pkb], w=[sb_])
                        S.dma("sp", O["new_cache_k"][tt * 128:(tt + 1) * 128, :], s_[:, 0:256], r=[sb_])
                        S.dma("sp", O["new_cache_v"][tt * 128:(tt + 1) * 128, :], s_[:, 256:512], r=[sb_])
                S.barrier()
            NCTX = 0
            if grp == 1:
                NCTX = 2
                ckT, ckb = k.sb(es1, "at_ckT", [64, 4, 256], BF16)
                cvp, cvb = k.sb(es1, "at_cvp", [128, 2, 4, 2, 128], BF16)
                k.memset("pool", cvp[:, :, :, :, :], 0.0, w=[cvb])
                with ExitStack() as es2:
                    ck, ckfb = k.sb(es2, "at_ck", [128, 2, 256], F32)
                    cv, cvfb = k.sb(es2, "at_cv", [128, 2, 256], F32)
                    S.dma("sp", ck[:, :, :], W["cache_k"].rearrange("(t p) f -> p t f", p=128), w=[ckfb])
                    S.dma("sp", cv[:, :, :], W["cache_v"].rearrange("(t p) f -> p t f", p=128), w=[cvfb])
                    for tt in range(2):
                        for g in range(4):
                            pt, pb = k.psum()
                            k.tr(pt[0:64, 0:128], ck[:, tt, g * 64:(g + 1) * 64], C["ident"][:, :],
                                 r=[ckfb, C["b"]], w=[pb])
                            k.cp("dve", ckT[:, g, tt * 128:(tt + 1) * 128], pt[0:64, 0:128], r=[pb], w=[ckb])
                            k.cp("act", cvp[:, tt, g, 0, 0:64], cv[:, tt, g * 64:(g + 1) * 64], r=[cvfb], w=[cvb])
                            k.cp("pool", cvp[:, tt, g, 1, 64:128], cv[:, tt, g * 64:(g + 1) * 64], r=[cvfb], w=[cvb])
                    S.barrier()
            for g in range(4 if DBG["attn_stop"] > 1 else 0):
                with ExitStack() as es2:
                    qT, qb = k.sb(es2, "at_qT", [64, 4, T], BF16)
                    kT, kb = k.sb(es2, "at_kT", [64, T], BF16)
                    vp, vpb = k.sb(es2, "at_vp", [128, NT, 2, 128], BF16)
                    k.memset("pool", vp[:, :, :, :], 0.0, w=[vpb])
                    for tt in range(NT):
                        k.cp("act", vp[:, tt, 0, 0:64], vtok[:, tt, g * 64:(g + 1) * 64], r=[vtb], w=[vpb])
                        k.cp("pool", vp[:, tt, 1, 64:128], vtok[:, tt, g * 64:(g + 1) * 64], r=[vtb], w=[vpb])
                    with ExitStack() as es3:
                        wq, wqb = k.sb(es3, "at_wq", [128, KC, 256], BF16)
                        wk, wkb = k.sb(es3, "at_wk", [128, KC, 64], BF16)
                        S.dma("pool", wq[:, :, :], w_in[:, g * 256:(g + 1) * 256].rearrange("(kc p) o -> p kc o", p=128),
                              w=[wqb])
                        S.dma("pool", wk[:, :, :],
                              w_in[:, 1024 + g * 64:1024 + (g + 1) * 64].rearrange("(kc p) o -> p kc o", p=128), w=[wkb])
                        raw = [k.sb(es3, "at_raw%d" % i, [64, 512], F32) for i in range(2)]
                        tm = [k.sb(es3, "at_tm%d" % i, [64, 512], F32) for i in range(2)]
                        n = 0
                        for tg in range(T // 512):
                            hT, hb = mod_tile(tg)
                            for hh in range(5):
                                sl = slice(tg * 512, (tg + 1) * 512)
                                pq, pqb = k.psum()
                                for kc in range(KC):
                                    lh = wq[:, kc, hh * 64:(hh + 1) * 64] if hh < 4 else wk[:, kc, :]
                                    k.mm(pq[0:64, :], lh, hT[:, kc, :], start=(kc == 0), stop=(kc == KC - 1),
                                         r=[wqb, wkb, hb], w=[pqb])
                                dst = qT[:, hh, sl] if hh < 4 else kT[:, sl]
                                dstb = qb if hh < 4 else kb
                                if grp == 0:
                                    k.cp("act" if n % 2 else "dve", dst, pq[0:64, :], r=[pqb], w=[dstb])
                                else:
                                    rw, rwb = raw[n % 2]
                                    t_, tb_ = tm[n % 2]
                                    k.cp("act", rw[:, :], pq[0:64, :], r=[pqb], w=[rwb])
                                    ps2, ps2b = k.psum()
                                    k.mm(ps2[0:64, :], C["perm"][0:64, 0:64], rw[:, :], start=True, stop=True,
                                         r=[rwb, C["b"]], w=[ps2b])
                                    k.tt("dve", t_[:, :], ps2[0:64, :], C["sin"][:, sl], ALU.mult, r=[ps2b, C["b"]], w=[tb_])
                                    k.tt("pool", rw[:, :], rw[:, :], C["cos"][:, sl], ALU.mult, r=[rwb, C["b"]], w=[rwb])
                                    k.tt("dve", dst, rw[:, :], t_[:, :], ALU.add, r=[rwb, tb_], w=[dstb])
                                n += 1
                        S.barrier()
                    with ExitStack() as es3:
                        NKMAX = 640 if grp == 1 else 256
                        WU = 2
                        sall = [k.sb(es3, "at_sall%d" % i, [128, NKMAX], F32) for i in range(2 * WU)]
                        pn = [k.sb(es3, "at_pn%d" % i, [128, NKMAX], BF16) for i in range(2 * WU)]
                        pT = [k.sb(es3, "at_pT%d" % i, [128, 2, 5, 128], BF16) for i in range(WU)]
                        sm = [k.sb(es3, "at_sm%d" % i, [128, 8], F32) for i in range(2 * WU)]
                        OTbs = [Buf("OT%d" % i) for i in range(WU)]

                        def unit_gen(uid, qt, cpair):
                            qsl = slice(qt * 128, (qt + 1) * 128)
                            if grp == 0:
                                seq = qt // 2
                                kblocks = [("full", seq * 2), ("full", seq * 2 + 1)]
                            else:
                                kblocks = []
                                if qt > 0:
                                    kblocks.append(("prev", qt - 1))
                                kblocks.append(("full", qt))
                                if qt < NT - 1:
                                    kblocks.append(("next", qt + 1))
                                kblocks += [("ctx", 0), ("ctx", 1)]
                            nkb = len(kblocks)
                            nk = nkb * 128
                            ptile, ptb_ = pT[uid % WU]
                            for par in range(2):
                                hh = cpair * 2 + par
                                h = g * 4 + hh
                                slot = (uid % WU) * 2 + par
                                sa, sab = sall[slot]
                                pn_, pnb = pn[slot]
                                sm_, smb = sm[slot]
                                banks = []
                                for b0 in range(0, nkb, 4):
                                    ps, psb = k.psum()
                                    banks.append((ps, psb))
                                    for bi in range(b0, min(nkb, b0 + 4)):
                                        kind, kt = kblocks[bi]
                                        if kind == "ctx":
                                            rhs = ckT[:, g, kt * 128:(kt + 1) * 128]
                                            rb = ckb
                                        else:
                                            rhs = kT[:, kt * 128:(kt + 1) * 128]
                                            rb = kb
                                        k.mm(ps[:, (bi - b0) * 128:(bi - b0 + 1) * 128], qT[:, hh, qsl], rhs,
                                             start=True, stop=True, r=[qb, rb], w=[psb])
                                yield
                                for bi, (kind, kt) in enumerate(kblocks):
                                    ps, psb = banks[bi // 4]
                                    src = ps[:, (bi % 4) * 128:(bi % 4 + 1) * 128]
                                    dsts = sa[:, bi * 128:(bi + 1) * 128]
                                    if kind == "prev":
                                        k.tt("dve", dsts, src, C["mprev"][:, :], ALU.add, r=[psb, C["b"]], w=[sab])
                                    elif kind == "next":
                                        k.tt("dve", dsts, src, C["mnext"][:, :], ALU.add, r=[psb, C["b"]], w=[sab])
                                    else:
                                        k.cp("act", dsts, src, r=[psb], w=[sab])
                                yield
                                k.S.op("dve", lambda e: e.reduce_max(sm_[:, 0:1], sa[:, 0:nk], mybir.AxisListType.X),
                                       r=[sab], w=[smb])
                                k.ts("dve", sm_[:, 1:2], sm_[:, 0:1], C_SCALE, sinkbc[:, h:h + 1], ALU.mult, ALU.max,
                                     r=[smb, sinkb], w=[smb])
                                k.ts("dve", sm_[:, 2:3], sm_[:, 1:2], -1.0, None, ALU.mult, ALU.bypass, r=[smb], w=[smb])
                                yield
                                k.act(sa[:, 0:nk], sa[:, 0:nk], AF.Exp, r=[sab, smb], w=[sab],
                                      bias=sm_[:, 2:3], scale=C_SCALE)
                                k.act(sm_[:, 3:4], sinkbc[:, h:h + 1], AF.Exp, r=[sinkb, smb], w=[smb],
                                      bias=sm_[:, 2:3], scale=1.0)
                                yield
                                k.S.op("dve", lambda e: e.reduce_sum(sm_[:, 4:5], sa[:, 0:nk], mybir.AxisListType.X),
                                       r=[sab], w=[smb])
                                k.tt("dve", sm_[:, 5:6], sm_[:, 4:5], sm_[:, 3:4], ALU.add, r=[smb], w=[smb])
                                k.S.op("dve", lambda e: e.reciprocal(sm_[:, 6:7], sm_[:, 5:6]), r=[smb], w=[smb])
                                yield
                                k.act(pn_[:, 0:nk], sa[:, 0:nk], AF.Identity, r=[sab, smb], w=[pnb], scale=sm_[:, 6:7])
                                yield
                                for b0 in range(0, nkb, 4):
                                    pt_, ptb2 = k.psum()
                                    ptv = pt_[:, :].bitcast(BF16)
                                    nb = min(nkb, b0 + 4) - b0
                                    for bi in range(b0, b0 + nb):
                                        k.tr(ptv[:, (bi - b0) * 128:(bi - b0 + 1) * 128], pn_[:, bi * 128:(bi + 1) * 128],
                                             C["identb"][:, :], r=[pnb, C["b"]], w=[ptb2])
                                    k.cp("act" if b0 else "dve", ptile[:, par, b0:b0 + nb, :],
                                         ptv[:, 0:nb * 128].rearrange("p (b q) -> p b q", q=128), r=[ptb2], w=[ptb_])
                                yield
                            po, pob = k.psum()
                            nmm = 2 * nkb
                            i_ = 0
                            for par in range(2):
                                for bi, (kind, kt) in enumerate(kblocks):
                                    if kind == "ctx":
                                        lh = cvp[:, kt, g, par, :]
                                        lb = cvb
                                    else:
                                        lh = vp[:, kt, par, :]
                                        lb = vpb
                                    k.mm(po[:, 0:128], lh, ptile[:, par, bi, :], start=(i_ == 0), stop=(i_ == nmm - 1),
                                         r=[lb, ptb_], w=[pob])
                                    i_ += 1
                            yield
                            k.cp("act", OT[:, g * 2 + cpair, qsl], po[:, 0:128], r=[pob], w=[OTbs[uid % WU]])

                        if DBG.get("mem") and g == 0:
                            print("ATTN grp", grp, "sbuf free", k.nc.sbuf_bytes_remaining)
                        units = [(qt, cp_) for qt in range(NT) for cp_ in range(2)]
                        run_interleaved([unit_gen(i, qt, cp_) for i, (qt, cp_) in enumerate(units)], WU)
                        k.S.op("dve", lambda e: e.memset(sm[0][0][:, 7:8], 0.0), r=OTbs, w=[OTb, sm[0][1]])
                        S.barrier()
            S.barrier()
        emit_mixer_tail(k, C, W["attn_w_out"][0], 8, OT, OTb, yT, yb, T, mod["gate"], modb, lncols, lnb)


def run_interleaved(gens, width):
    gens = list(gens)
    active = []
    while gens or active:
        while gens and len(active) < width:
            active.append(gens.pop(0))
        for g_ in list(active):
            try:
                next(g_)
            except StopIteration:
                active.remove(g_)


def bc(ap, axis, n):
    shp = list(ap.shape)
    shp.insert(axis, n)
    return ap.unsqueeze(axis).to_broadcast(shp)


def emit_softplus(k, C, out_ap, outb, xin, xinb, P, N, es):
    a, ab = k.sb(es, "sp_a", [128, N], F32)
    k.stt("dve", a[0:P, :], xin, -1.0, xin, ALU.mult, ALU.max, r=[xinb], w=[ab])
    k.act(a[0:P, :], a[0:P, :], AF.Exp, r=[ab], w=[ab], scale=-1.0)
    k.act(a[0:P, :], a[0:P, :], AF.Ln, r=[ab, C["b"]], w=[ab], bias=C["ones"][0:P, 0:1], scale=1.0)
    k.stt("dve", out_ap, xin, 0.0, a[0:P, :], ALU.max, ALU.add, r=[xinb, ab], w=[outb])


def emit_ssd(k, C, W, O, SCR, yT, yb, T, grp, j, mod, modb, lncols, lnb):
    S = k.S
    w_in = W["ssd_w_in"][j]
    NT = T // 128
    nseq, L = (4, 256) if grp == 0 else (1, 2048)
    TPS = L // 128
    with ExitStack() as es:
        cw, cwb = k.sb(es, "sd_cw", [128, 120], F32)
        emit_load_cols(k, cw[:, :], cwb, W["ssd_conv_w"][j].rearrange("k (c p) -> (k c) p", p=128), 120, C)
        cbias, cbb = k.sb(es, "sd_cb", [128, 24], F32)
        emit_load_cols(k, cbias[:, :], cbb, W["ssd_conv_b"][j].rearrange("(c p) -> c p", p=128), 24, C)
        normg, ngb = k.sb(es, "sd_ng", [128, 16], F32)
        emit_load_cols(k, normg[:, :], ngb, W["ssd_norm"][j].rearrange("(c p) -> c p", p=128), 16, C)
        dtb, dtbb = k.sb(es, "sd_dtb", [64, 2], F32)
        emit_load_cols(k, dtb[:, 0:1], dtbb, W["ssd_dt_bias"][j:j + 1].rearrange("o d h -> o (d h)"), 1, C, Wd=64)
        emit_load_cols(k, dtb[:, 1:2], dtbb, W["ssd_a_log"][j:j + 1].rearrange("o d h -> o (d h)"), 1, C, Wd=64)
        k.act(dtb[:, 1:2], dtb[:, 1:2], AF.Exp, r=[dtbb], w=[dtbb])
        k.ts("dve", dtb[:, 1:2], dtb[:, 1:2], -1.0, None, ALU.mult, ALU.bypass, r=[dtbb], w=[dtbb])
        dcol, dcb = k.sb(es, "sd_dcol", [128, 16], F32)
        with ExitStack() as es1:
            drow, drb = k.sb(es1, "sd_drow", [1, 32], F32)
            S.dma("sp", drow[0:1, :], W["ssd_d"][j:j + 1, :], w=[drb])
            pt, pb = k.psum()
            for par in range(2):
                k.mm(pt[:, 0:16], C["sel"][0:1, par * 128:(par + 1) * 128], drow[0:1, par:32:2],
                     start=(par == 0), stop=(par == 1), r=[drb, C["b"]], w=[pb])
            k.cp("dve", dcol[:, :], pt[:, 0:16], r=[pb], w=[dcb])
            S.barrier()
        dtT, dtTb = k.sb(es, "sd_dtT", [64, T], F32)
        dtaT, dtaTb = k.sb(es, "sd_dtaT", [64, T], F32)

        with ExitStack() as es1:
            hT, hb = k.sb(es1, "sd_hT", [128, KC, T], BF16)
            emit_modulate(k, hT, hb, yT, yb, T, mod, modb)
            wbuf = [k.sb(es1, "sd_w%d" % i, [128, KC, 256], BF16) for i in range(2)]
            pad, padb = k.sb(es1, "sd_pad", [128, nseq, L + 4], F32)
            k.memset("pool", pad[:, :, :], 0.0, w=[padb])
            acc = [k.sb(es1, "sd_acc%d" % i, [128, nseq, L], F32) for i in range(2)]
            ob16 = [k.sb(es1, "sd_o%d" % i, [128, T], BF16) for i in range(2)]

            def loadw(gi):
                S.dma("pool", wbuf[gi % 2][0][:, :, :],
                      w_in[:, gi * 256:(gi + 1) * 256].rearrange("(kc p) o -> p kc o", p=128), w=[wbuf[gi % 2][1]])

            loadw(0)
            n = 0
            for gi in range(20):
                if gi + 1 < 20:
                    loadw(gi + 1)
                wt, wb = wbuf[gi % 2]
                for o2 in range(2):
                    oc = gi * 2 + o2
                    o16, o16b = ob16[n % 2]
                    ac, acb = acc[n % 2]
                    n += 1
                    for tg in range(T // 512):
                        ps, psb = k.psum()
                        for kc in range(KC):
                            k.mm(ps[:, :], wt[:, kc, o2 * 128:(o2 + 1) * 128], hT[:, kc, tg * 512:(tg + 1) * 512],
                                 start=(kc == 0), stop=(kc == KC - 1), r=[wb, hb], w=[psb])
                        if oc < 16:
                            k.act(o16[:, tg * 512:(tg + 1) * 512], ps[:, :], AF.Silu, r=[psb], w=[o16b])
                        else:
                            if grp == 0:
                                k.cp("act", pad[:, tg * 2:(tg + 1) * 2, 2:2 + L],
                                     ps[:, :].rearrange("p (s l) -> p s l", l=L), r=[psb], w=[padb])
                            else:
                                k.cp("act", pad[:, 0, 2 + tg * 512:2 + (tg + 1) * 512], ps[:, :], r=[psb], w=[padb])
                    if oc < 16:
                        S.dma("sp", SCR["zs"][oc, :, 0:T], o16[:, :], r=[o16b], w=[SCR["zsb"]])
                    else:
                        c = oc - 16
                        e1 = "dve" if c % 2 == 0 else "pool"
                        k.ts(e1, ac[:, :, :], pad[:, :, 0:L], cw[:, c:c + 1], cbias[:, c:c + 1], ALU.mult, ALU.add,
                             r=[padb, cwb, cbb], w=[acb])
                        for kk in range(1, 5):
                            k.stt(e1, ac[:, :, :], pad[:, :, kk:kk + L], cw[:, kk * 24 + c:kk * 24 + c + 1], ac[:, :, :],
                                  ALU.mult, ALU.add, r=[padb, cwb, acb], w=[acb])
                        k.act(o16[:, :], ac[:, :, :].rearrange("p s l -> p (s l)"), AF.Silu, r=[acb], w=[o16b])
                        S.dma("sp", SCR["xc"][c, :, 0:T], o16[:, :], r=[o16b], w=[SCR["xcb"]])
            wdt, wdtb = k.sb(es1, "sd_wdt", [128, KC, 64], BF16)
            S.dma("pool", wdt[:, :, :], w_in[:, 5120:5184].rearrange("(kc p) o -> p kc o", p=128), w=[wdtb])
            xs_, xsb_ = k.sb(es1, "sd_dtx", [64, 512], F32)
            for tg in range(T // 512):
                ps, psb = k.psum()
                for kc in range(KC):
                    k.mm(ps[0:64, :], wdt[:, kc, :], hT[:, kc, tg * 512:(tg + 1) * 512],
                         start=(kc == 0), stop=(kc == KC - 1), r=[wdtb, hb], w=[psb])
                k.act(xs_[:, :], ps[0:64, :], AF.Identity, r=[psb, dtbb], w=[xsb_], bias=dtb[:, 0:1], scale=1.0)
                with ExitStack() as es2:
                    emit_softplus(k, C, dtT[:, tg * 512:(tg + 1) * 512], dtTb, xs_[:, :], xsb_, 64, 512, es2)
                    S.barrier()
            k.ts("dve", dtaT[:, :], dtT[:, :], dtb[:, 1:2], None, ALU.mult, ALU.bypass, r=[dtTb, dtbb], w=[dtaTb])
            S.barrier()

        for dirn in range(2):
            S.mark("  ssd sweep%d" % dirn)
            final = dirn == 1
            TRI = C["tri_f"] if dirn == 0 else C["tri_b"]
            END = 127 if dirn == 0 else 0
            with ExitStack() as es1:
                xTt, xTb = k.sb(es1, "sw_xT", [128, 16, 128], BF16)
                bcT, bcb = k.sb(es1, "sw_bcT", [128, 8, 128], BF16)
                xz, xzb = k.sb(es1, "sw_xz", [128, 32, 128], BF16)
                hz, hzb = k.sb(es1, "sw_hz", [128, 32, 128], BF16)
                hT_, hTb = k.sb(es1, "sw_hT", [128, 2048], F32)
                k.memset("pool", xz[:, :, :], 0.0, w=[xzb])
                xzv = xz[:, :, :].rearrange("p (c r) f -> p c r f", r=2)
                hzv = hz[:, :, :].rearrange("p (c r) f -> p c r f", r=2)
                hTv = hT_[:, :].rearrange("p (c r f) -> p c r f", r=2, f=64)
                btok, btb = k.sb(es1, "sw_btok", [128, 512], BF16)
                dtk, dtkb = k.sb(es1, "sw_dtk", [128, 128], F32)
                cbm, cbmb = k.sb(es1, "sw_cbm", [128, 4, 128], F32)
                nacs, nacsb = k.sb(es1, "sw_nacs", [128, 32], F32)
                dhl, dhlb = k.sb(es1, "sw_dhl", [128, 2, 32], BF16)
                TRIb, tribb = k.sb(es1, "sw_trib", [128, 128], BF16)
                k.cp("dve", TRIb[:, :], TRI[:, :], r=[C["b"]], w=[tribb])
                nb, nbb = k.sb(es1, "sw_nb", [128, 32], F32)
                rhsA = [k.sb(es1, "sw_rhsA%d" % i, [128, 4, 128], F32) for i in range(2)]
                e4 = [k.sb(es1, "sw_e4%d" % i, [128, 4, 128], F32) for i in range(2)]
                ea4 = [k.sb(es1, "sw_ea4%d" % i, [128, 4, 128], F32) for i in range(2)]
                MT, MTb = k.sb(es1, "sw_MT", [128, 32, 128], BF16)
                CE, CEb = k.sb(es1, "sw_CE", [128, 32, 128], BF16)
                MTbs = [Buf("MT%d" % i) for i in range(8)]
                CEbs = [Buf("CE%d" % i) for i in range(8)]
                wraw, wrb = k.sb(es1, "sw_wraw", [128, 32], F32)
                craw, crb = k.sb(es1, "sw_craw", [128, 32], F32)
                gt, gtb = k.sb(es1, "sw_gt", [128, 16, 128], F32)
                xs, xsb = k.sb(es1, "sw_xs", [128, 32, 64], BF16)
                xsv = xs[:, :, :].rearrange("p (c r) f -> p c r f", r=2)
                stg, stgb = k.sb(es1, "sw_stg", [128, 16, 128], F32)
                if final:
                    zst, zsb_ = k.sb(es1, "sw_zs", [128, 16, 128], BF16)
                    sq = [k.sb(es1, "sw_sq%d" % i, [128, 128], F32) for i in range(2)]
                    rstd, rstdb = k.sb(es1, "sw_rstd", [128, 128], F32)
                    g16, g16b = k.sb(es1, "sw_g16", [128, 16, 128], BF16)
                    sqa, sqab = k.sb(es1, "sw_sqa", [128, 16, 128], F32)

                if DBG.get("mem"):
                    print("SSD sweep", dirn, "grp", grp, "sbuf free", k.nc.sbuf_bytes_remaining)

                def hz_refresh():
                    k.cp("act", hzv[:, :, 0, 0:64], hTv[:, :, 0, :], r=[hTb], w=[hzb])
                    k.cp("pool", hzv[:, :, 1, 64:128], hTv[:, :, 1, :], r=[hTb], w=[hzb])

                order = []
                for sq_i in range(nseq):
                    tl = list(range(sq_i * TPS, (sq_i + 1) * TPS))
                    order += tl[::-1] if dirn == 1 else tl
                NBF = 2 if k.nc.sbuf_bytes_remaining > 6144 + 4096 else 1
                NSL = 2
                if k.nc.sbuf_bytes_remaining > 6144 * (NBF - 1) + 6144 + 4096:
                    NSL = 3
                    rhsA.append(k.sb(es1, "sw_rhsA2", [128, 4, 128], F32))
                    e4.append(k.sb(es1, "sw_e42", [128, 4, 128], F32))
                    ea4.append(k.sb(es1, "sw_ea42", [128, 4, 128], F32))
                xTts = [(xTt, xTb)]
                bcTs = [(bcT, bcb)]
                if NBF == 2:
                    xTts.append(k.sb(es1, "sw_xT2", [128, 16, 128], BF16))
                    bcTs.append(k.sb(es1, "sw_bcT2", [128, 8, 128], BF16))

                def issue_loads(i):
                    tsl_ = slice(order[i] * 128, (order[i] + 1) * 128)
                    S.dma("sp", xTts[i % NBF][0][:, :, :], SCR["xc"][0:16, :, tsl_].rearrange("c p t -> p c t"),
                          r=[SCR["xcb"]], w=[xTts[i % NBF][1]])
                    S.dma("sp", bcTs[i % NBF][0][:, :, :], SCR["xc"][16:24, :, tsl_].rearrange("c p t -> p c t"),
                          r=[SCR["xcb"]], w=[bcTs[i % NBF][1]])

                issue_loads(0)
                gi = 0
                for sq_i in range(nseq):
                    k.memset("pool", hz[:, :, :], 0.0, w=[hzb])
                    if grp == 0:
                        k.memset("dve", hT_[:, :], 0.0, w=[hTb])
                    else:
                        S.dma("sp", stg[:, :, :], W["state_ssd"][j, dirn].rearrange("h p n -> (h p) n")
                              .rearrange("(c q) n -> q c n", q=128), w=[stgb])
                        for c4 in range(4):
                            pt, pb = k.psum()
                            for cc in range(4):
                                c = c4 * 4 + cc
                                k.tr(pt[:, cc * 128:(cc + 1) * 128], stg[:, c, :], C["ident"][:, :], r=[stgb, C["b"]], w=[pb])
                            k.cp("dve", hT_[:, c4 * 512:(c4 + 1) * 512], pt[:, :], r=[pb], w=[hTb])
                        hz_refresh()
                    tiles = list(range(sq_i * TPS, (sq_i + 1) * TPS))
                    if dirn == 1:
                        tiles = tiles[::-1]
                    for tt in tiles:
                        tsl = slice(tt * 128, (tt + 1) * 128)
                        if NBF == 1:
                            if gi > 0:
                                issue_loads(gi)
                        elif gi + 1 < len(order):
                            issue_loads(gi + 1)
                        xTt, xTb = xTts[gi % NBF]
                        bcT, bcb = bcTs[gi % NBF]
                        gi += 1
                        if final:
                            S.dma("sp", zst[:, :, :], SCR["zs"][:, :, tsl].rearrange("c p t -> p c t"), r=[SCR["zsb"]], w=[zsb_])
                            S.dma("sp", stg[:, :, :], SCR["yf"][:, :, tsl].rearrange("c p t -> p c t"), r=[SCR["yfb"]], w=[stgb])
                        for half in range(2):
                            pt, pb = k.psum()
                            ptv = pt[:, :].bitcast(BF16)
                            for c8 in range(8):
                                k.tr(ptv[:, c8 * 128:(c8 + 1) * 128], xTt[:, half * 8 + c8, :], C["identb"][:, :],
                                     r=[xTb, C["b"]], w=[pb])
                            src = ptv[:, 0:1024].rearrange("p (c r f) -> p c r f", r=2, f=64)
                            k.cp("dve", xzv[:, half * 8:(half + 1) * 8, 0, 0:64], src[:, :, 0, :], r=[pb], w=[xzb])
                            k.cp("act", xzv[:, half * 8:(half + 1) * 8, 1, 64:128], src[:, :, 1, :], r=[pb], w=[xzb])
                        pt, pb = k.psum()
                        ptv = pt[:, :].bitcast(BF16)
                        for g in range(4):
                            k.tr(ptv[:, g * 128:(g + 1) * 128], bcT[:, g, :], C["identb"][:, :], r=[bcb, C["b"]], w=[pb])
                        k.cp("dve", btok[:, :], ptv[:, 0:512], r=[pb], w=[btb])
                        pt, pb = k.psum()
                        k.tr(pt[:, 0:64], dtT[0:64, tsl], C["ident"][0:64, 0:64], r=[dtTb, C["b"]], w=[pb])
                        k.tr(pt[:, 64:128], dtaT[0:64, tsl], C["ident"][0:64, 0:64], r=[dtaTb, C["b"]], w=[pb])
                        k.cp("act", dtk[:, :], pt[:, 0:128], r=[pb], w=[dtkb])
                        dt_d = dtk[:, dirn * 32:(dirn + 1) * 32]
                        dta_d = dtk[:, 64 + dirn * 32:64 + (dirn + 1) * 32]
                        pt, pb = k.psum()
                        for g in range(4):
                            k.mm(pt[:, g * 128:(g + 1) * 128], bcT[:, g, :], bcT[:, 4 + g, :], start=True, stop=True,
                                 r=[bcb], w=[pb])
                        k.tt("dve", cbm[:, :, :], pt[:, :].rearrange("p (g q) -> p g q", q=128), bc(TRI[:, :], 1, 4),
                             ALU.mult, r=[pb, C["b"]], w=[cbmb])
                        pt, pb = k.psum()
                        k.mm(pt[:, 0:32], TRI[:, :], dta_d, start=True, stop=True, r=[dtkb, C["b"]], w=[pb])
                        k.ts("dve", nacs[:, :], pt[:, 0:32], -1.0, None, ALU.mult, ALU.bypass, r=[pb], w=[nacsb])
                        k.cp("dve", dhl[:, 0, :], dta_d, r=[dtkb], w=[dhlb])
                        k.tt("dve", dhl[:, 1, :], dta_d, dhl[:, 0, :], ALU.subtract, r=[dtkb, dhlb], w=[dhlb])
                        k.act(nb[:, :], dt_d, AF.Ln, r=[dtkb], w=[nbb])
                        k.tt("dve", nb[:, :], nb[:, :], nacs[:, :], ALU.add, r=[nbb, nacsb], w=[nbb])
                        def bank_gen(hb_):
                            h0 = hb_ * 4
                            g = hb_ // 2
                            ra, rab = rhsA[hb_ % NSL]
                            e_, eb = e4[hb_ % NSL]
                            a_, aeb = ea4[hb_ % NSL]
                            R, Rb = k.psum()
                            for i in range(4):
                                k.mm(R[:, i * 128:(i + 1) * 128], dhl[:, 0, h0 + i:h0 + i + 1].to_broadcast([128, 128]),
                                     TRIb[:, :], start=True, stop=False, r=[dhlb, tribb], w=[Rb])
                                k.mm(R[:, i * 128:(i + 1) * 128], dhl[:, 1, h0 + i:h0 + i + 1].to_broadcast([128, 128]),
                                     TRIb[:, :], start=False, stop=True, r=[dhlb, tribb], w=[Rb])
                            Rv = R[:, :].rearrange("p (h q) -> p h q", q=128)
                            yield
                            k.tt("dve", e_[:, :, :], Rv, bc(nb[:, h0:h0 + 4], 2, 128), ALU.add, r=[Rb, nbb], w=[eb])
                            k.ts("dve", e_[:, :, :], e_[:, :, :], 20.0, None, ALU.min, ALU.bypass, r=[eb], w=[eb])
                            k.tt("dve", wraw[:, h0:h0 + 4], Rv[:, :, END], nacs[:, h0:h0 + 4], ALU.add, r=[Rb, nacsb], w=[wrb])
                            k.cp("dve", craw[:, h0:h0 + 4], Rv[:, :, END], r=[Rb], w=[crb])
                            k.act(a_[:, :, :], Rv, AF.Exp, r=[Rb], w=[aeb])
                            yield
                            k.act(e_[:, :, :], e_[:, :, :], AF.Exp, r=[eb], w=[eb])
                            k.tt("pool", CE[:, h0:h0 + 4, :], a_[:, :, :], bc(bcT[:, 4 + g, :], 1, 4), ALU.mult,
                                 r=[aeb, bcb], w=[CEbs[hb_]])
                            yield
                            k.tt("dve", MT[:, h0:h0 + 4, :], e_[:, :, :], bc(cbm[:, g, :], 1, 4), ALU.mult,
                                 r=[eb, cbmb], w=[MTbs[hb_]])

                        run_interleaved([bank_gen(hb_) for hb_ in range(8)], NSL)
                        k.act(wraw[:, :], wraw[:, :], AF.Exp, r=[wrb], w=[wrb])
                        k.tt("dve", wraw[:, :], wraw[:, :], dt_d, ALU.mult, r=[wrb, dtkb], w=[wrb])
                        k.act(craw[:, :], craw[:, :], AF.Exp, r=[crb], w=[crb])
                        for c4 in range(4):
                            po, pob = k.psum()
                            for cc in range(4):
                                c = c4 * 4 + cc
                                o_ = po[:, cc * 128:(cc + 1) * 128]
                                k.mm(o_, xz[:, 2 * c, :], MT[:, 2 * c, :], start=True, stop=False, r=[xzb, MTbs[c // 2]], w=[pob])
                                k.mm(o_, xz[:, 2 * c + 1, :], MT[:, 2 * c + 1, :], start=False, stop=False, r=[xzb, MTbs[c // 2]], w=[pob])
                                k.mm(o_, hz[:, 2 * c, :], CE[:, 2 * c, :], start=False, stop=False, r=[hzb, CEbs[c // 2]], w=[pob])
                                k.mm(o_, hz[:, 2 * c + 1, :], CE[:, 2 * c + 1, :], start=False, stop=True, r=[hzb, CEbs[c // 2]], w=[pob])
                            pov = po[:, :].rearrange("p (c q) -> p c q", q=128)
                            if not final:
                                k.cp("act", gt[:, c4 * 4:(c4 + 1) * 4, :], pov, r=[pob], w=[gtb])
                            else:
                                k.tt("dve", gt[:, c4 * 4:(c4 + 1) * 4, :], pov, stg[:, c4 * 4:(c4 + 1) * 4, :], ALU.add,
                                     r=[pob, stgb], w=[gtb])
                        if not final:
                            S.dma("sp", SCR["yf"][:, :, tsl].rearrange("c p t -> p c t"), gt[:, :, :], r=[gtb], w=[SCR["yfb"]])
                        else:
                            k.tt("pool", g16[:, :, :], xTt[:, :, :], bc(dcol[:, :], 2, 128), ALU.mult, r=[xTb, dcb], w=[g16b])
                            k.tt("dve", gt[:, :, :], gt[:, :, :], g16[:, :, :], ALU.add, r=[gtb, g16b], w=[gtb])
                            k.tt("dve", gt[:, :, :], gt[:, :, :], zst[:, :, :], ALU.mult, r=[gtb, zsb_], w=[gtb])
                            k.act(sqa[:, :, :], gt[:, :, :], AF.Square, r=[gtb], w=[sqab])
                            S.op("dve", lambda e: e.reduce_sum(sq[0][0][:, :], sqa[:, :, :].rearrange("p c q -> p q c"),
                                                               mybir.AxisListType.X), r=[sqab], w=[sq[0][1]])
                            pss, pssb = k.psum()
                            k.mm(pss[:, 0:128], C["ones"][:, :], sq[0][0][:, :], start=True, stop=True,
                                 r=[sq[0][1], C["b"]], w=[pssb])
                            k.ts("dve", rstd[:, :], pss[:, 0:128], 1.0 / 2048, 1e-6, ALU.mult, ALU.add, r=[pssb], w=[rstdb])
                            k.act(rstd[:, :], rstd[:, :], AF.Sqrt, r=[rstdb], w=[rstdb])
                            S.op("dve", lambda e: e.reciprocal(rstd[:, :], rstd[:, :]), r=[rstdb], w=[rstdb])
                            k.tt("dve", gt[:, :, :], gt[:, :, :], bc(rstd[:, :], 1, 16), ALU.mult, r=[gtb, rstdb], w=[gtb])
                            k.tt("pool", g16[:, :, :], gt[:, :, :], bc(normg[:, :], 2, 128), ALU.mult, r=[gtb, ngb], w=[g16b])
                            S.dma("sp", SCR["gt"][:, :, tsl].rearrange("c p t -> p c t"), g16[:, :, :], r=[g16b], w=[SCR["gtb"]])
                        k.tt("dve", xsv[:, :, 0, :], xzv[:, :, 0, 0:64], bc(wraw[:, 0:32:2], 2, 64), ALU.mult,
                             r=[xzb, wrb], w=[xsb])
                        k.tt("pool", xsv[:, :, 1, :], xzv[:, :, 1, 64:128], bc(wraw[:, 1:32:2], 2, 64), ALU.mult,
                             r=[xzb, wrb], w=[xsb])
                        k.tt("dve", hT_[:, :].rearrange("p (h f) -> p h f", f=64), hT_[:, :].rearrange("p (h f) -> p h f", f=64),
                             bc(craw[:, :], 2, 64), ALU.mult, r=[hTb, crb], w=[hTb])
                        for g in range(4):
                            pst, pstb = k.psum()
                            k.mm(pst[:, :], btok[:, g * 128:(g + 1) * 128], xs[:, g * 8:(g + 1) * 8, :].rearrange("p h f -> p (h f)"),
                                 start=True, stop=True, r=[btb, xsb], w=[pstb])
                            k.tt("dve", hT_[:, g * 512:(g + 1) * 512], hT_[:, g * 512:(g + 1) * 512], pst[:, :], ALU.add,
                                 r=[hTb, pstb], w=[hTb])
                        hz_refresh()
                    if grp == 0:
                        for c4 in range(4):
                            pt, pb = k.psum()
                            for cc in range(4):
                                c = c4 * 4 + cc
                                k.tr(pt[:, cc * 128:(cc + 1) * 128], hT_[:, c * 128:(c + 1) * 128], C["ident"][:, :],
                                     r=[hTb, C["b"]], w=[pb])
                            k.cp("act", stg[:, c4 * 4:(c4 + 1) * 4, :], pt[:, :].rearrange("p (c n) -> p c n", n=128),
                                 r=[pb], w=[stgb])
                        S.dma("sp", O["new_state_ssd"][sq_i, j, dirn].rearrange("h p n -> (h p) n")
                              .rearrange("(c q) n -> q c n", q=128), stg[:, :, :], r=[stgb])
                S.barrier()
        S.barrier()
    with ExitStack() as es:
        GT, GTb = k.sb(es, "sd_GT", [128, 16, T], BF16)
        S.dma("sp", GT[:, :, :], SCR["gt"][:, :, 0:T].rearrange("c p t -> p c t"), r=[SCR["gtb"]], w=[GTb])
        emit_mixer_tail(k, C, W["ssd_w_out"][j], 16, GT, GTb, yT, yb, T, mod["gate"], modb, lncols, lnb)


def emit_gdn(k, C, W, O, SCR, yT, yb, T, grp, mod, modb, lncols, lnb):
    S = k.S
    w_in = W["gdn_w_in"][0]
    nseq, L = (4, 256) if grp == 0 else (1, 2048)
    TPS = L // 128
    with ExitStack() as es:
        cw, cwb = k.sb(es, "gd_cw", [128, 160], F32)
        for half in range(2):
            emit_load_cols(k, cw[:, half * 80:(half + 1) * 80], cwb,
                           W["gdn_conv_w"][0].rearrange("k (c p) -> (k c) p", p=128)[half * 80:(half + 1) * 80, :], 80, C)
        cbias, cbb = k.sb(es, "gd_cb", [128, 32], F32)
        emit_load_cols(k, cbias[:, :], cbb, W["gdn_conv_b"][0].rearrange("(c p) -> c p", p=128), 32, C)
        normg, ngb = k.sb(es, "gd_ng", [128, 2], F32)
        emit_load_cols(k, normg[:, :], ngb, W["gdn_norm"][0].rearrange("(c p) -> c p", p=128), 2, C)
        dtb, dtbb = k.sb(es, "gd_dtb", [16, 2], F32)
        emit_load_cols(k, dtb[:, 0:1], dtbb, W["gdn_dt_bias"][0:1].rearrange("o d h -> o (d h)"), 1, C, Wd=16)
        emit_load_cols(k, dtb[:, 1:2], dtbb, W["gdn_a_log"][0:1].rearrange("o d h -> o (d h)"), 1, C, Wd=16)
        k.act(dtb[:, 1:2], dtb[:, 1:2], AF.Exp, r=[dtbb], w=[dtbb])
        k.ts("dve", dtb[:, 1:2], dtb[:, 1:2], -1.0, None, ALU.mult, ALU.bypass, r=[dtbb], w=[dtbb])
        betaT, betaTb = k.sb(es, "gd_betaT", [16, T], F32)
        gT_, gTb_ = k.sb(es, "gd_gT", [16, T], F32)

        with ExitStack() as es1:
            hT, hb = k.sb(es1, "gd_hT", [128, KC, T], BF16)
            emit_modulate(k, hT, hb, yT, yb, T, mod, modb)
            wbuf = [k.sb(es1, "gd_w%d" % i, [128, KC, 256], BF16) for i in range(2)]
            pad, padb = k.sb(es1, "gd_pad", [128, nseq, L + 4], F32)
            k.memset("pool", pad[:, :, :], 0.0, w=[padb])
            acc = [k.sb(es1, "gd_acc%d" % i, [128, nseq, L], F32) for i in range(2)]
            o32, o32b = k.sb(es1, "gd_o32", [128, T], F32)
            sq, sqb = k.sb(es1, "gd_sq", [128, 512], F32)
            rn, rnb = k.sb(es1, "gd_rn", [128, 512], F32)
            ob16 = [k.sb(es1, "gd_o%d" % i, [128, T], BF16) for i in range(2)]

            def loadw(gi):
                S.dma("pool", wbuf[gi % 2][0][:, :, :],
                      w_in[:, gi * 256:(gi + 1) * 256].rearrange("(kc p) o -> p kc o", p=128), w=[wbuf[gi % 2][1]])

            loadw(0)
            n = 0
            for gi in range(24):
                if gi + 1 < 24:
                    loadw(gi + 1)
                wt, wb = wbuf[gi % 2]
                for o2 in range(2):
                    oc = gi * 2 + o2
                    o16, o16b = ob16[n % 2]
                    ac, acb = acc[n % 2]
                    n += 1
                    for tg in range(T // 512):
                        ps, psb = k.psum()
                        for kc in range(KC):
                            k.mm(ps[:, :], wt[:, kc, o2 * 128:(o2 + 1) * 128], hT[:, kc, tg * 512:(tg + 1) * 512],
                                 start=(kc == 0), stop=(kc == KC - 1), r=[wb, hb], w=[psb])
                        if oc >= 32:
                            k.act(o16[:, tg * 512:(tg + 1) * 512], ps[:, :], AF.Silu, r=[psb], w=[o16b])
                        elif grp == 0:
                            k.cp("act", pad[:, tg * 2:(tg + 1) * 2, 2:2 + L],
                                 ps[:, :].rearrange("p (s l) -> p s l", l=L), r=[psb], w=[padb])
                        else:
                            k.cp("act", pad[:, 0, 2 + tg * 512:2 + (tg + 1) * 512], ps[:, :], r=[psb], w=[padb])
                    if oc >= 32:
                        S.dma("sp", SCR["zs"][oc - 32, :, 0:T], o16[:, :], r=[o16b], w=[SCR["zsb"]])
                        continue
                    c = oc
                    e1 = "dve" if c % 2 == 0 else "pool"
                    k.ts(e1, ac[:, :, :], pad[:, :, 0:L], cw[:, c:c + 1], cbias[:, c:c + 1], ALU.mult, ALU.add,
                         r=[padb, cwb, cbb], w=[acb])
                    for kk in range(1, 5):
                        k.stt(e1, ac[:, :, :], pad[:, :, kk:kk + L], cw[:, kk * 32 + c:kk * 32 + c + 1], ac[:, :, :],
                              ALU.mult, ALU.add, r=[padb, cwb, acb], w=[acb])
                    acf = ac[:, :, :].rearrange("p s l -> p (s l)")
                    if c >= 16:
                        k.act(o16[:, :], acf, AF.Silu, r=[acb], w=[o16b])
                    else:
                        k.act(o32[:, :], acf, AF.Silu, r=[acb], w=[o32b])
                        for tg in range(T // 512):
                            sl = slice(tg * 512, (tg + 1) * 512)
                            k.act(sq[:, :], o32[:, sl], AF.Square, r=[o32b], w=[sqb])
                            ps, psb = k.psum()
                            k.mm(ps[:, :], C["ones"][:, :], sq[:, :], start=True, stop=True, r=[sqb, C["b"]], w=[psb])
                            k.ts("dve", rn[:, :], ps[:, :], 1e-6, None, ALU.add, ALU.bypass, r=[psb], w=[rnb])
                            k.act(rn[:, :], rn[:, :], AF.Sqrt, r=[rnb], w=[rnb])
                            S.op("dve", lambda e: e.reciprocal(rn[:, :], rn[:, :]), r=[rnb], w=[rnb])
                            k.stt("dve", o16[:, sl], o32[:, sl], (128 ** -0.5) if c < 8 else 1.0, rn[:, :],
                                  ALU.mult, ALU.mult, r=[o32b, rnb], w=[o16b])
                    S.dma("sp", SCR["xc"][c, :, 0:T], o16[:, :], r=[o16b], w=[SCR["xcb"]])
            wab, wabb = k.sb(es1, "gd_wab", [128, KC, 32], BF16)
            S.dma("pool", wab[:, :, :], w_in[:, 6144:6176].rearrange("(kc p) o -> p kc o", p=128), w=[wabb])
            xs_, xsb_ = k.sb(es1, "gd_abx", [16, 512], F32)
            for tg in range(T // 512):
                sl = slice(tg * 512, (tg + 1) * 512)
                ps, psb = k.psum()
                for kc in range(KC):
                    k.mm(ps[0:16, :], wab[:, kc, 0:16], hT[:, kc, sl], start=(kc == 0), stop=(kc == KC - 1),
                         r=[wabb, hb], w=[psb])
                k.act(betaT[:, sl], ps[0:16, :], AF.Sigmoid, r=[psb], w=[betaTb])
                ps, psb = k.psum()
                for kc in range(KC):
                    k.mm(ps[0:16, :], wab[:, kc, 16:32], hT[:, kc, sl], start=(kc == 0), stop=(kc == KC - 1),
                         r=[wabb, hb], w=[psb])
                k.act(xs_[:, :], ps[0:16, :], AF.Identity, r=[psb, dtbb], w=[xsb_], bias=dtb[:, 0:1], scale=1.0)
                with ExitStack() as es2:
                    emit_softplus(k, C, gT_[:, sl], gTb_, xs_[:, :], xsb_, 16, 512, es2)
                    S.barrier()
            k.ts("dve", gT_[:, :], gT_[:, :], dtb[:, 1:2], None, ALU.mult, ALU.bypass, r=[gTb_, dtbb], w=[gTb_])
            S.barrier()

        for dirn in range(2):
            S.mark("  gdn sweep%d" % dirn)
            final = dirn == 1
            sfx = "_f" if dirn == 0 else "_b"
            TRIc = C["gtri" + sfx]
            NMSL = C["gnmsl" + sfx]
            with ExitStack() as es1:
                qkT, qkb = k.sb(es1, "gs_qkT", [128, 16, 128], BF16)
                vT, vTb = k.sb(es1, "gs_vT", [128, 16, 128], BF16)
                ktok, ktb = k.sb(es1, "gs_ktok", [128, 8, 128], BF16)
                vtok, vtb = k.sb(es1, "gs_vtok", [128, 8, 256], BF16)
                vb_, vbb = k.sb(es1, "gs_vb", [128, 8, 256], BF16)
                kbg, kbgb = k.sb(es1, "gs_kbg", [128, 8, 128], BF16)
                kdec, kdb = k.sb(es1, "gs_kdec", [128, 8, 128], BF16)
                bg, bgb = k.sb(es1, "gs_bg", [128, 64], F32)
                sc, scb = k.sb(es1, "gs_sc", [128, 48], F32)
                rhsA = [k.sb(es1, "gs_rhsA%d" % i, [128, 4, 128], F32) for i in range(2)]
                d1 = [k.sb(es1, "gs_d1%d" % i, [128, 4, 128], F32) for i in range(2)]
                d2 = [k.sb(es1, "gs_d2%d" % i, [128, 4, 128], F32) for i in range(2)]
                kkms = [k.sb(es1, "gs_kkm%d" % q, [128, 4, 128], F32) for q in range(2)]
                CDT = BF16 if DBG.get("gdn_bf16", False) else F32
                Xs = [[k.sb(es1, "gs_X%d_%d" % (q, i), [128, 4, 128], CDT) for i in range(2)] for q in range(2)]
                XTs = [[k.sb(es1, "gs_XT%d_%d" % (q, i), [128, 4, 128], CDT) for i in range(2)] for q in range(2)]
                Paccs = [k.sb(es1, "gs_Pacc%d" % q, [128, 4, 128], F32) for q in range(2)]
                PaccBs = [k.sb(es1, "gs_PaccB%d" % q, [128, 4, 128], CDT) for q in range(2)] if CDT == BF16 else Paccs
                TTb, TTbb = k.sb(es1, "gs_TTb", [128, 8, 128], BF16)
                u, ub = k.sb(es1, "gs_u", [128, 8, 256], F32)
                wT, wTb = k.sb(es1, "gs_wT", [128, 8, 128], BF16)
                qkm, qkmb = k.sb(es1, "gs_qkm", [128, 8, 128], BF16)
                delta, dlb = k.sb(es1, "gs_delta", [128, 8, 256], BF16)
                o_, ob_ = k.sb(es1, "gs_o", [128, 8, 256], F32)
                dlbs = [Buf("dl%d" % i) for i in range(8)]
                obs = [Buf("o%d" % i) for i in range(8)]
                ubs = [Buf("u%d" % i) for i in range(2)]
                wTbs = [Buf("wT%d" % i) for i in range(2)]
                qkmbs = [Buf("qkm%d" % i) for i in range(2)]
                TTbbs = [Buf("TTb%d" % i) for i in range(2)]
                Sst = [k.sb(es1, "gs_S%d" % h, [128, 256], F32) for h in range(8)]
                Sbf = [k.sb(es1, "gs_Sb%d" % h, [128, 256], BF16) for h in range(8)]
                if final:
                    of_, ofb = k.sb(es1, "gs_of", [128, 8, 256], F32)
                    zst, zsb_ = k.sb(es1, "gs_zs", [128, 16, 128], BF16)
                    ss, ssb = k.sb(es1, "gs_ss", [128, 16], F32)
                    on16, onb = vb_, vbb
                    g16, g16b = k.sb(es1, "gs_g16", [128, 16, 128], BF16)

                if DBG.get("mem"):
                    print("GDN sweep", dirn, "grp", grp, "sbuf free", k.nc.sbuf_bytes_remaining)
                order = []
                for sq_i in range(nseq):
                    tl = list(range(sq_i * TPS, (sq_i + 1) * TPS))
                    order += tl[::-1] if dirn == 1 else tl
                NBF = 2 if (k.nc.sbuf_bytes_remaining > 8192 + 4096 and not DBG.get("nopf")) else 1
                qkTs = [(qkT, qkb)]
                vTs = [(vT, vTb)]
                if NBF == 2:
                    qkTs.append(k.sb(es1, "gs_qkT2", [128, 16, 128], BF16))
                    vTs.append(k.sb(es1, "gs_vT2", [128, 16, 128], BF16))

                def issue_loads(i):
                    tsl_ = slice(order[i] * 128, (order[i] + 1) * 128)
                    S.dma("sp", qkTs[i % NBF][0][:, :, :], SCR["xc"][0:16, :, tsl_].rearrange("c p t -> p c t"),
                          r=[SCR["xcb"]], w=[qkTs[i % NBF][1]])
                    S.dma("sp", vTs[i % NBF][0][:, :, :], SCR["xc"][16:32, :, tsl_].rearrange("c p t -> p c t"),
                          r=[SCR["xcb"]], w=[vTs[i % NBF][1]])

                issue_loads(0)
                gi = 0
                for sq_i in range(nseq):
                    for h in range(8):
                        if grp == 0:
                            k.memset("dve" if h % 2 else "pool", Sst[h][0][:, :], 0.0, w=[Sst[h][1]])
                        else:
                            S.dma("sp", Sst[h][0][:, :], W["state_delta"][0, dirn, h], w=[Sst[h][1]])
                        k.cp("act", Sbf[h][0][:, :], Sst[h][0][:, :], r=[Sst[h][1]], w=[Sbf[h][1]])
                    tiles = list(range(sq_i * TPS, (sq_i + 1) * TPS))
                    if dirn == 1:
                        tiles = tiles[::-1]
                    for tt in tiles:
                        tsl = slice(tt * 128, (tt + 1) * 128)
                        if NBF == 1:
                            if gi > 0:
                                issue_loads(gi)
                        elif gi + 1 < len(order):
                            issue_loads(gi + 1)
                        qkT, qkb = qkTs[gi % NBF]
                        vT, vTb = vTs[gi % NBF]
                        gi += 1
                        if final:
                            S.dma("sp", zst[:, :, :], SCR["zs"][:, :, tsl].rearrange("c p t -> p c t"), r=[SCR["zsb"]], w=[zsb_])
                            S.dma("sp", of_[:, :, :], SCR["of"][tsl, :].rearrange("t (h v) -> t h v", v=256),
                                  r=[SCR["ofb"]], w=[ofb])
                        pt, pb = k.psum()
                        ptv = pt[:, :].bitcast(BF16)
                        for h in range(8):
                            k.tr(ptv[:, h * 128:(h + 1) * 128], qkT[:, 8 + h, :], C["identb"][:, :], r=[qkb, C["b"]], w=[pb])
                        k.cp("dve", ktok[:, :, :], ptv[:, 0:1024].rearrange("p (h f) -> p h f", f=128), r=[pb], w=[ktb])
                        for half in range(2):
                            pt, pb = k.psum()
                            ptv = pt[:, :].bitcast(BF16)
                            for c8 in range(8):
                                k.tr(ptv[:, c8 * 128:(c8 + 1) * 128], vT[:, half * 8 + c8, :], C["identb"][:, :],
                                     r=[vTb, C["b"]], w=[pb])
                            k.cp("act", vtok[:, half * 4:(half + 1) * 4, :],
                                 ptv[:, 0:1024].rearrange("p (h v) -> p h v", v=256), r=[pb], w=[vtb])
                        pt, pb = k.psum()
                        k.tr(pt[:, 0:16], betaT[0:16, tsl], C["ident"][0:16, 0:16], r=[betaTb, C["b"]], w=[pb])
                        k.tr(pt[:, 16:32], gT_[0:16, tsl], C["ident"][0:16, 0:16], r=[gTb_, C["b"]], w=[pb])
                        k.cp("act", bg[:, 0:32], pt[:, 0:32], r=[pb], w=[bgb])
                        beta_d = bg[:, dirn * 8:(dirn + 1) * 8]
                        g_d = bg[:, 16 + dirn * 8:16 + (dirn + 1) * 8]
                        pt, pb = k.psum()
                        k.mm(pt[:, 0:8], TRIc[:, :], g_d, start=True, stop=True, r=[bgb, C["b"]], w=[pb])
                        k.mm(pt[:, 8:16], C["gblk"][:, :], g_d, start=True, stop=True, r=[bgb, C["b"]], w=[pb])
                        k.cp("dve", bg[:, 32:40], pt[:, 0:8], r=[pb], w=[bgb])
                        k.ts("dve", bg[:, 40:48], pt[:, 0:8], -1.0, None, ALU.mult, ALU.bypass, r=[pb], w=[bgb])
                        k.tt("dve", sc[:, 16:24], pt[:, 8:16], bg[:, 40:48], ALU.add, r=[pb, bgb], w=[scb])
                        k.act(sc[:, 16:24], sc[:, 16:24], AF.Exp, r=[scb], w=[scb])
                        k.act(bg[:, 48:56], bg[:, 32:40], AF.Exp, r=[bgb], w=[bgb])
                        k.act(bg[:, 56:64], beta_d, AF.Ln, r=[bgb], w=[bgb])
                        k.tt("dve", bg[:, 56:64], bg[:, 56:64], bg[:, 32:40], ALU.add, r=[bgb], w=[bgb])
                        k.tt("dve", sc[:, 8:16], beta_d, bg[:, 48:56], ALU.mult, r=[bgb], w=[scb])
                        ra, rab = rhsA[0]
                        rav = ra[:, 0, 0:16].rearrange("p (h c) -> p h c", c=2)
                        k.tt("dve", rav, bc(g_d, 2, 2), bc(C["gch"][:, 0:2], 1, 8), ALU.mult, r=[bgb, C["b"]], w=[rab])
                        pt, pb = k.psum()
                        k.mm(pt[:, 0:16], C["ones"][:, :], ra[:, 0, 0:16], start=True, stop=True, r=[rab, C["b"]], w=[pb])
                        k.act(sc[:, 24:40], pt[:, 0:16], AF.Exp, r=[pb], w=[scb])
                        k.tt("pool", vb_[:, :, :], vtok[:, :, :], bc(beta_d, 2, 256), ALU.mult, r=[vtb, bgb], w=[vbb])
                        k.tt("dve", kbg[:, :, :], ktok[:, :, :], bc(sc[:, 8:16], 2, 128), ALU.mult, r=[ktb, scb], w=[kbgb])
                        k.tt("pool", kdec[:, :, :], ktok[:, :, :], bc(sc[:, 16:24], 2, 128), ALU.mult, r=[ktb, scb], w=[kdb])
                        def quad_gen(qd):
                            h0 = qd * 4
                            ra, rab = rhsA[qd]
                            d1_, d1b = d1[qd]
                            d2_, d2b = d2[qd]
                            kkm, kkmb = kkms[qd]
                            Pacc, Pab = Paccs[qd]
                            X = Xs[qd]
                            XT = XTs[qd]
                            for i in range(4):
                                k.act(ra[:, i, :], TRIc[:, :], AF.Identity, r=[C["b"], bgb], w=[rab],
                                      scale=g_d[:, h0 + i:h0 + i + 1])
                            R, Rb = k.psum()
                            k.mm(R[:, :], C["ones"][:, :], ra[:, :, :].rearrange("p h q -> p (h q)"), start=True, stop=True,
                                 r=[rab, C["b"]], w=[Rb])
                            Rv = R[:, :].rearrange("p (h q) -> p h q", q=128)
                            pk, pkb = k.psum()
                            for i in range(4):
                                k.mm(pk[:, i * 128:(i + 1) * 128], qkT[:, 8 + h0 + i, :], qkT[:, 8 + h0 + i, :],
                                     start=True, stop=True, r=[qkb], w=[pkb])
                            yield
                            k.tt("dve", d1_[:, :, :], Rv, bc(bg[:, 40 + h0:44 + h0], 2, 128), ALU.add, r=[Rb, bgb], w=[d1b])
                            k.ts("dve", d1_[:, :, :], d1_[:, :, :], 0.0, None, ALU.min, ALU.bypass, r=[d1b], w=[d1b])
                            k.stt("dve", d2_[:, :, :], Rv, -1.0, bc(bg[:, 56 + h0:60 + h0], 2, 128), ALU.mult, ALU.add,
                                  r=[Rb, bgb], w=[d2b])
                            k.ts("dve", d2_[:, :, :], d2_[:, :, :], 0.0, None, ALU.min, ALU.bypass, r=[d2b], w=[d2b])
                            k.tt("dve", kkm[:, :, :], pk[:, :].rearrange("p (h q) -> p h q", q=128), bc(NMSL[:, :], 1, 4),
                                 ALU.mult, r=[pkb, C["b"]], w=[kkmb])
                            yield
                            k.act(d2_[:, :, :], d2_[:, :, :], AF.Exp, r=[d2b], w=[d2b])
                            k.act(d1_[:, :, :], d1_[:, :, :], AF.Exp, r=[d1b], w=[d1b])
                            k.tt("pool", d1_[:, :, :], d1_[:, :, :], bc(TRIc[:, :], 1, 4), ALU.mult, r=[d1b, C["b"]], w=[d1b])
                            yield
                            xt, xtb = XT[0]
                            x_, xb_ = X[0]
                            k.tt("dve", xt[:, :, :], d2_[:, :, :], kkm[:, :, :], ALU.mult, r=[d2b, kkmb], w=[xtb])
                            pn, pnb = k.psum()
                            if CDT == BF16:
                                pnf = pn[:, :].bitcast(BF16)
                                for i in range(4):
                                    k.tr(pnf[:, i * 128:(i + 1) * 128], xt[:, i, :], C["identb"][:, :], r=[xtb, C["b"]], w=[pnb])
                                pnv = pnf[:, 0:512].rearrange("p (h q) -> p h q", q=128)
                            else:
                                for i in range(4):
                                    k.tr(pn[:, i * 128:(i + 1) * 128], xt[:, i, :], C["ident"][:, :], r=[xtb, C["b"]], w=[pnb])
                                pnv = pn[:, :].rearrange("p (h q) -> p h q", q=128)
                            PaccB, PaBb = PaccBs[qd]
                            yield
                            k.cp("act", x_[:, :, :], pnv, r=[pnb], w=[xb_])
                            k.tt("dve", Pacc[:, :, :], pnv, bc(C["ident"][:, :], 1, 4), ALU.add, r=[pnb, C["b"]], w=[Pab])
                            if CDT == BF16:
                                k.cp("act", PaccB[:, :, :], Pacc[:, :, :], r=[Pab], w=[PaBb])
                            cur = 0
                            for lvl in range(1, 6):
                                x_, xb_ = X[cur]
                                xt, xtb = XT[cur]
                                x2, x2b = X[1 - cur]
                                xt2, xt2b = XT[1 - cur]
                                p1, p1b = k.psum()
                                for i in range(4):
                                    k.mm(p1[:, i * 128:(i + 1) * 128], x_[:, i, :], xt[:, i, :], start=True, stop=True,
                                         r=[xb_, xtb], w=[p1b])
                                if lvl < 5:
                                    p2, p2b = k.psum()
                                    for i in range(4):
                                        k.mm(p2[:, i * 128:(i + 1) * 128], xt[:, i, :], x_[:, i, :], start=True, stop=True,
                                             r=[xb_, xtb], w=[p2b])
                                yield
                                k.cp("act", xt2[:, :, :], p1[:, :].rearrange("p (h q) -> p h q", q=128), r=[p1b], w=[xt2b])
                                if lvl < 5:
                                    k.cp("dve", x2[:, :, :], p2[:, :].rearrange("p (h q) -> p h q", q=128), r=[p2b], w=[x2b])
                                yield
                                p3, p3b = k.psum()
                                for i in range(4):
                                    k.mm(p3[:, i * 128:(i + 1) * 128], xt2[:, i, :], PaccB[:, i, :], start=True, stop=True,
                                         r=[xt2b, PaBb], w=[p3b])
                                yield
                                k.tt("dve", Pacc[:, :, :], Pacc[:, :, :], p3[:, :].rearrange("p (h q) -> p h q", q=128),
                                     ALU.add, r=[Pab, p3b], w=[Pab])
                                if lvl < 5 and CDT == BF16:
                                    k.cp("act", PaccB[:, :, :], Pacc[:, :, :], r=[Pab], w=[PaBb])
                                cur = 1 - cur
                            k.cp("act", TTb[:, h0:h0 + 4, :], Pacc[:, :, :], r=[Pab], w=[TTbbs[qd]])
                            pq, pqb = k.psum()
                            for i in range(4):
                                k.mm(pq[:, i * 128:(i + 1) * 128], qkT[:, 8 + h0 + i, :], qkT[:, h0 + i, :], start=True, stop=True,
                                     r=[qkb], w=[pqb])
                            yield
                            k.tt("dve", qkm[:, h0:h0 + 4, :], pq[:, :].rearrange("p (h q) -> p h q", q=128), d1_[:, :, :],
                                 ALU.mult, r=[pqb, d1b], w=[qkmbs[qd]])
                            for i2 in range(2):
                                pu, pub = k.psum()
                                for i in range(2):
                                    h = h0 + i2 * 2 + i
                                    k.mm(pu[:, i * 256:(i + 1) * 256], TTb[:, h, :], vb_[:, h, :], start=True, stop=True,
                                         r=[TTbbs[qd], vbb], w=[pub])
                                k.cp("act", u[:, h0 + i2 * 2:h0 + i2 * 2 + 2, :], pu[:, :].rearrange("p (h v) -> p h v", v=256),
                                     r=[pub], w=[ubs[qd]])
                            pw, pwb = k.psum()
                            for i in range(4):
                                k.mm(pw[:, i * 128:(i + 1) * 128], kbg[:, h0 + i, :], TTb[:, h0 + i, :], start=True, stop=True,
                                     r=[kbgb, TTbbs[qd]], w=[pwb])
                            yield
                            k.cp("act", wT[:, h0:h0 + 4, :], pw[:, :].rearrange("p (h q) -> p h q", q=128), r=[pwb], w=[wTbs[qd]])

                        run_interleaved([quad_gen(0), quad_gen(1)], 2)
                        chunks = [0, 1] if dirn == 0 else [1, 0]

                        def head_gen(h):
                            St, Stb = Sst[h]
                            Sb, Sbb = Sbf[h]
                            qd = h // 4
                            for ci in chunks:
                                rs = slice(ci * 64, (ci + 1) * 64)
                                pa, pab_ = k.psum()
                                k.mm(pa[:, 0:256], wT[:, h, :], Sb[:, :], start=True, stop=True, r=[wTbs[qd], Sbb], w=[pab_])
                                k.mm(pa[:, 256:512], qkT[:, h, :], Sb[:, :], start=True, stop=True, r=[qkb, Sbb], w=[pab_])
                                yield
                                k.tt("dve", delta[rs, h, :], u[rs, h, :], pa[rs, 0:256], ALU.subtract, r=[ubs[qd], pab_], w=[dlbs[h]])
                                k.act(o_[rs, h, :], pa[rs, 256:512], AF.Identity, r=[pab_, bgb], w=[obs[h]],
                                      scale=bg[rs, 48 + h:49 + h])
                                yield
                                pS, pSb = k.psum()
                                k.mm(pS[:, 0:256], kdec[rs, h, :], delta[rs, h, :], start=True, stop=True, r=[kdb, dlbs[h]], w=[pSb])
                                yield
                                k.stt("dve", St[:, :], St[:, :], sc[:, 24 + 2 * h + ci:25 + 2 * h + ci], pS[:, 0:256],
                                      ALU.mult, ALU.add, r=[Stb, scb, pSb], w=[Stb])
                                k.cp("act", Sb[:, :], St[:, :], r=[Stb], w=[Sbb])
                                yield
                            po, pob = k.psum()
                            k.mm(po[:, 0:256], qkm[:, h, :], delta[:, h, :], start=True, stop=True, r=[qkmbs[qd], dlbs[h]], w=[pob])
                            yield
                            k.tt("dve", o_[:, h, :], o_[:, h, :], po[:, 0:256], ALU.add, r=[obs[h], pob], w=[obs[h]])

                        run_interleaved([head_gen(h) for h in range(8)], 4)
                        if not final:
                            S.dma("sp", SCR["of"][tsl, :].rearrange("t (h v) -> t h v", v=256), o_[:, :, :],
                                  r=obs, w=[SCR["ofb"]])
                        else:
                            k.tt("dve", o_[:, :, :], o_[:, :, :], of_[:, :, :], ALU.add, r=obs + [ofb], w=obs)
                            k.tt("pool", of_[:, :, :], o_[:, :, :], o_[:, :, :], ALU.mult, r=obs, w=[ofb])
                            S.op("dve", lambda e: e.reduce_sum(ss[:, 0:8], of_[:, :, :], mybir.AxisListType.X), r=[ofb], w=[ssb])
                            k.ts("dve", ss[:, 8:16], ss[:, 0:8], 1.0 / 256, 1e-6, ALU.mult, ALU.add, r=[ssb], w=[ssb])
                            k.act(ss[:, 8:16], ss[:, 8:16], AF.Sqrt, r=[ssb], w=[ssb])
                            S.op("dve", lambda e: e.reciprocal(ss[:, 8:16], ss[:, 8:16]), r=[ssb], w=[ssb])
                            k.tt("dve", on16[:, :, :], o_[:, :, :], bc(ss[:, 8:16], 2, 256), ALU.mult, r=obs + [ssb], w=[onb])
                            for half in range(2):
                                pt, pb = k.psum()
                                ptv = pt[:, :].bitcast(BF16)
                                for c8 in range(8):
                                    c = half * 8 + c8
                                    k.tr(ptv[:, c8 * 128:(c8 + 1) * 128], on16[:, c // 2, (c % 2) * 128:(c % 2 + 1) * 128],
                                         C["identb"][:, :], r=[onb, C["b"]], w=[pb])
                                for c8 in range(8):
                                    c = half * 8 + c8
                                    k.stt("dve", g16[:, c, :], ptv[:, c8 * 128:(c8 + 1) * 128], normg[:, c % 2:c % 2 + 1],
                                          zst[:, c, :], ALU.mult, ALU.mult, r=[pb, ngb, zsb_], w=[g16b])
                            S.dma("sp", SCR["gt"][:, :, tsl].rearrange("c p t -> p c t"), g16[:, :, :], r=[g16b], w=[SCR["gtb"]])
                    if grp == 0:
                        for h in range(8):
                            S.dma("sp", O["new_state_delta"][sq_i, 0, dirn, h], Sst[h][0][:, :], r=[Sst[h][1]])
                S.barrier()
        S.barrier()
    with ExitStack() as es:
        GT, GTb = k.sb(es, "gd_GT", [128, 16, T], BF16)
        S.dma("sp", GT[:, :, :], SCR["gt"][:, :, 0:T].rearrange("c p t -> p c t"), r=[SCR["gtb"]], w=[GTb])
        emit_mixer_tail(k, C, W["gdn_w_out"][0], 16, GT, GTb, yT, yb, T, mod["gate"], modb, lncols, lnb)


IN_SPECS = {
    "x_prompt": (TP, D), "x_sample": (TS, D), "c": (1, D), "c_ctx": (D,),
    "state_ssd": (2, 2, 32, 64, 128), "state_delta": (1, 2, 8, 128, 256),
    "cache_k": (256, 256), "cache_v": (256, 256),
    "w_mod": (DEPTH, D, 9 * D), "b_mod": (DEPTH, 9 * D), "ln_g": (DEPTH, 3, D), "ln_b": (DEPTH, 3, D),
    "ffn_w_gate": (DEPTH, 2, D, DFF), "ffn_w_up": (DEPTH, 2, D, DFF), "ffn_w_down": (DEPTH, 2, DFF, D),
    "ssd_w_in": (2, D, 5184), "ssd_conv_w": (2, 5, 3072), "ssd_conv_b": (2, 3072), "ssd_dt_bias": (2, 2, 32),
    "ssd_a_log": (2, 2, 32), "ssd_d": (2, 32), "ssd_norm": (2, 2048), "ssd_w_out": (2, 2048, D),
    "gdn_w_in": (1, D, 6176), "gdn_conv_w": (1, 5, 4096), "gdn_conv_b": (1, 4096), "gdn_dt_bias": (1, 2, 8),
    "gdn_a_log": (1, 2, 8), "gdn_norm": (1, 256), "gdn_w_out": (1, 2048, D),
    "attn_w_in": (1, D, 1536), "attn_sink": (1, 16), "attn_w_out": (1, D, D),
}
OUT_SPECS = {
    "y_prompt": (TP, D), "y_sample": (TS, D),
    "new_state_ssd": (4, 2, 2, 32, 64, 128), "new_state_delta": (4, 1, 2, 8, 128, 256),
    "new_cache_k": (TP, 256), "new_cache_v": (TP, 256),
}
MIXER_OF_LAYER = {0: "ssd", 1: "gdn", 2: "attn", 3: "ssd"}


def build_program(cfg):
    nc = bass.Bass("TRN2", target_bir_lowering=False)
    names = cfg.get("inputs", list(IN_SPECS))
    W = {}
    for name in names:
        W[name] = nc.dram_tensor(name, list(IN_SPECS[name]), F32, kind="ExternalInput").ap()
    hc = host_consts()
    for name, arr in hc.items():
        W["c_" + name] = nc.dram_tensor("c_" + name, list(arr.shape), F32, kind="ExternalInput").ap()
    O = {}
    for name in cfg.get("outputs", list(OUT_SPECS)):
        O[name] = nc.dram_tensor(name, list(OUT_SPECS[name]), F32, kind="ExternalOutput").ap()
    SCR = {}
    for nm, shp, dt_ in (("xc", [32, 128, TS], BF16), ("zs", [16, 128, TS], BF16), ("yf", [16, 128, TS], F32),
                         ("gt", [16, 128, TS], BF16), ("of", [TS, 2048], F32)):
        SCR[nm] = nc.dram_tensor("scr_" + nm, shp, dt_).ap()
        SCR[nm + "b"] = Buf("scr_" + nm)
    layers = cfg.get("layers", list(range(DEPTH)))
    stages = cfg.get("stages", (0, 1, 2))
    mixer = dict(MIXER_OF_LAYER)
    mixer.update(cfg.get("mixer", {}))

    with ExitStack() as es:
        S = Sy(nc, es)
        k = K(nc, es, S)
        C = {"b": Buf("consts")}
        for name, arr in hc.items():
            if name in ("cos", "sin"):
                continue
            t, _ = k.sb(es, "k_" + name, list(arr.shape), F32)
            C[name] = t
            S.dma("sp", t[:, :], W["c_" + name][:, :], w=[C["b"]])
        idb, _ = k.sb(es, "k_identb", [128, 128], BF16)
        C["identb"] = idb
        S.dma("pool", idb[:, :], W["c_ident"][:, :], w=[C["b"]])
        onb_, _ = k.sb(es, "k_onesb", [128, 128], BF16)
        C["onesb"] = onb_
        S.dma("pool", onb_[:, :], W["c_ones"][:, :], w=[C["b"]])
        S.barrier()

        condT, condb = k.sb(es, "condT", [128, KC, 2], BF16)
        with ExitStack() as es1:
            cf, cfb = k.sb(es1, "cond_f", [128, 16], F32)
            st, stb = k.sb(es1, "cond_st", [16, 128], F32)
            S.dma("sp", st[0:8, :], W["c_ctx"].rearrange("(r p) -> r p", p=128), w=[stb])
            S.dma("sp", st[8:16, :], W["c"][0].rearrange("(r p) -> r p", p=128), w=[stb])
            pt, pb = k.psum()
            k.tr(pt[:, 0:16], st[0:16, :], C["ident"][0:16, 0:16], r=[stb, C["b"]], w=[pb])
            k.act(cf[:, :], pt[:, 0:16], AF.Silu, r=[pb], w=[cfb])
            k.cp("dve", condT[:, :, :], cf[:, :].rearrange("p (g c) -> p c g", g=2), r=[cfb], w=[condb])
            S.barrier()

        modT, modb = k.sb(es, "modT", [128, DEPTH, 2, 72], F32)
        for l in layers:
            S.mark("adaln%d" % l)
            emit_adaln(k, C, W, condT, condb, modT, modb, l)
        for l in layers:
            for g in range(2):
                for srow in (1, 4, 7):
                    k.ts("dve", modT[:, l, g, srow * 8:(srow + 1) * 8], modT[:, l, g, srow * 8:(srow + 1) * 8],
                         1.0, None, ALU.add, ALU.bypass, r=[modb], w=[modb])
                for srow, f in ((2, 0.5 / ALPHA), (5, 1.0 / ALPHA), (8, 0.5 / ALPHA)):
                    k.ts("dve", modT[:, l, g, srow * 8:(srow + 1) * 8], modT[:, l, g, srow * 8:(srow + 1) * 8],
                         f, None, ALU.mult, ALU.bypass, r=[modb], w=[modb])
        lnT, lnb = k.sb(es, "lnT", [128, DEPTH * 3 * 2 * 8], F32)
        emit_load_cols(k, lnT[:, 0:96], lnb, W["ln_g"].rearrange("l s (r p) -> (l s r) p", p=128), 96, C)
        emit_load_cols(k, lnT[:, 96:192], lnb, W["ln_b"].rearrange("l s (r p) -> (l s r) p", p=128), 96, C)

        def lncols(l, i):
            o = (l * 3 + i) * 8
            return (lnT[:, o:o + 8], lnT[:, 96 + o:96 + o + 8])

        def modcols(l, g, s3):
            return {"shift": modT[:, l, g, (3 * s3) * 8:(3 * s3 + 1) * 8],
                    "sc1": modT[:, l, g, (3 * s3 + 1) * 8:(3 * s3 + 2) * 8],
                    "gate": modT[:, l, g, (3 * s3 + 2) * 8:(3 * s3 + 3) * 8]}

        yT, yb = k.sb(es, "yT", [128, KC, TS], F32)

        for g, (T, xname, oname) in enumerate(((TP, "x_prompt", "y_prompt"), (TS, "x_sample", "y_sample"))):
            if g not in cfg.get("groups", (0, 1)):
                continue
            S.mark("g%d loadx" % g)
            emit_load_x(k, C, yT, yb, W[xname], T)
            for l in layers:
                if 0 in stages:
                    for t0 in range(0, T, 1024):
                        S.mark("g%d l%d ffn0 t%d" % (g, l, t0))
                        emit_ffn(k, C, W, yT, yb, t0, l, 0, modcols(l, g, 0), modb, lncols(l, 0), lnb)
                if 1 in stages:
                    kind = mixer[l]
                    S.mark("g%d l%d mixer %s" % (g, l, kind))
                    if kind == "attn":
                        emit_attn(k, C, W, O, yT, yb, T, g, modcols(l, g, 1), modb, lncols(l, 1), lnb)
                    elif kind == "ssd":
                        emit_ssd(k, C, W, O, SCR, yT, yb, T, g, l // 3, modcols(l, g, 1), modb, lncols(l, 1), lnb)
                    elif kind == "gdn":
                        emit_gdn(k, C, W, O, SCR, yT, yb, T, g, modcols(l, g, 1), modb, lncols(l, 1), lnb)
                if 2 in stages:
                    for t0 in range(0, T, 1024):
                        S.mark("g%d l%d ffn1 t%d" % (g, l, t0))
                        emit_ffn(k, C, W, yT, yb, t0, l, 1, modcols(l, g, 2), modb, lncols(l, 2), lnb)
            S.mark("g%d store" % g)
            emit_store_y(k, C, yT, yb, O[oname], T)
        S.mark("end")
        S.final_wait()
        global LAST_MARKS
        LAST_MARKS = S.marks
        print("instructions:", S.ninst)
    return nc


def make_in_maps(inputs, names):
    hc = host_consts()
    maps = []
    for i in range(NCORES):
        m = {}
        for name in names:
            a = inputs[name]
            if name == "x_prompt":
                a = a[4 * i:4 * i + 4].reshape(TP, D)
            elif name == "x_sample":
                a = a[i].reshape(TS, D)
            elif name == "c":
                a = a[i:i + 1]
            elif name in ("state_ssd", "state_delta"):
                a = a[i]
            elif name in ("cache_k", "cache_v"):
                a = a[i, 0].reshape(256, 256)
            m[name] = np.ascontiguousarray(a, dtype=np.float32)
        for name, arr in hc.items():
            m["c_" + name] = arr
        maps.append(m)
    return maps


def run(inputs, cfg):
    nc = build_program(cfg)
    maps = make_in_maps(inputs, cfg.get("inputs", list(IN_SPECS)))
    res = run_bass_kernel_spmd(nc, maps, core_ids=list(range(NCORES)))
    return res.results


def kernel(**inputs):
    inputs = {k_: np.asarray(v) for k_, v in inputs.items()}
    res = run(inputs, {})
    y_prompt = np.concatenate([r["y_prompt"].reshape(4, 256, D) for r in res], axis=0)
    y_sample = np.stack([r["y_sample"].reshape(TS, D) for r in res], axis=0)
    nss = np.concatenate([r["new_state_ssd"] for r in res], axis=0)
    nsd = np.concatenate([r["new_state_delta"] for r in res], axis=0)
    nck = np.concatenate([r["new_cache_k"].reshape(4, 1, 256, 4, 64) for r in res], axis=0)
    ncv = np.concatenate([r["new_cache_v"].reshape(4, 1, 256, 4, 64) for r in res], axis=0)
    return (y_prompt, y_sample, nss, nsd, nck, ncv)
```

```python
import numpy as np
import concourse.bass as bass
import concourse.mybir as mybir
from concourse.bass_utils import run_bass_kernel_spmd
from contextlib import ExitStack

F32 = mybir.dt.float32
BF16 = mybir.dt.bfloat16
AF = mybir.ActivationFunctionType
ALU = mybir.AluOpType

D = 1024
KC = 8
DFF = 2816
FC = 22
DEPTH = 4
ALPHA = (2 * DEPTH) ** 0.25
LN_EPS = 1e-5 / (ALPHA * ALPHA)
NCORES = 8
TP = 1024
TS = 2048


class Buf:
    __slots__ = ("w", "r", "name", "excl")

    def __init__(self, name="", excl=False):
        self.w = None
        self.r = {}
        self.name = name
        self.excl = excl


class Sy:
    SAME = {"pe": False, "dve": True, "act": True, "pool": True, "sp": False}

    def __init__(self, nc, es, ndma=12):
        self.nc = nc
        self.E = {"pe": nc.tensor, "dve": nc.vector, "act": nc.scalar, "pool": nc.gpsimd, "sp": nc.sync}
        self.sem = {k: es.enter_context(nc.semaphore("s_" + k)) for k in self.E}
        self.cnt = {k: 0 for k in self.E}
        self.waited = {k: {} for k in self.E}
        self.dsem = {}
        for q in ("sp", "pool"):
            self.dsem[q] = [[es.enter_context(nc.semaphore("d_%s%d" % (q, i))), 0] for i in range(ndma)]
        self.drr = {q: 0 for q in self.dsem}
        self.ninst = 0

    def _wait(self, e, tok):
        sem, val, src = tok
        if src == e and not self.SAME[e]:
            return
        k = sem.num
        if self.waited[e].get(k, 0) >= val:
            return
        self.E[e].wait_ge(sem, val)
        self.waited[e][k] = val

    def _deps(self, e, reads, writes):
        for b in reads:
            if b.w is not None:
                self._wait(e, b.w)
        for b in writes:
            if b.w is not None:
                self._wait(e, b.w)
            for t in b.r.values():
                self._wait(e, t)

    def _commit(self, tok, key, reads, writes):
        for b in writes:
            b.w = tok
            b.r = {}
        for b in reads:
            if b not in writes:
                b.r[key] = tok

    def op(self, e, fn, r=(), w=()):
        ex = [b for b in r if b.excl]
        if ex:
            w = list(w) + [b for b in ex if b not in w]
        self._deps(e, r, w)
        ins = fn(self.E[e])
        self.cnt[e] += 1
        ins.then_inc(self.sem[e], 1)
        self._commit((self.sem[e], self.cnt[e], e), e, r, w)
        self.ninst += 1
        return ins

    def dma(self, q, out, in_, r=(), w=()):
        slot = self.drr[q] % len(self.dsem[q])
        self.drr[q] += 1
        ent = self.dsem[q][slot]
        sem = ent[0]
        if ent[1] > 0:
            self._wait(q, (sem, ent[1], "dma"))
        self._deps(q, r, w)
        ins = self.E[q].dma_start(out=out, in_=in_)
        ent[1] += 16
        ins.then_inc(sem, 16)
        self._commit((sem, ent[1], "dma"), (q, slot), r, w)
        self.ninst += 1
        return ins

    def mark(self, name):
        if not hasattr(self, "marks"):
            self.marks = []
        d = dict(self.cnt)
        d["pe_slices"] = getattr(self, "pe_slices", 0)
        self.marks.append((name, d))

    def barrier(self):
        for e in self.E:
            for e2 in self.E:
                if self.cnt[e2] > 0 and (e2 != e or self.SAME[e]):
                    self._wait(e, (self.sem[e2], self.cnt[e2], e2))
            for q in self.dsem:
                for ent in self.dsem[q]:
                    if ent[1] > 0:
                        self._wait(e, (ent[0], ent[1], "dma"))

    def final_wait(self):
        e = "sp"
        for e2 in self.E:
            if e2 != e and self.cnt[e2] > 0:
                self._wait(e, (self.sem[e2], self.cnt[e2], e2))
        for q in self.dsem:
            for ent in self.dsem[q]:
                if ent[1] > 0:
                    self._wait(e, (ent[0], ent[1], "dma"))


class K:
    def __init__(self, nc, es, S):
        self.nc = nc
        self.es = es
        self.S = S
        self.ps = []
        for i in range(8):
            t = es.enter_context(nc.psum_tensor("psb%d" % i, [128, 512], F32))
            self.ps.append((t, Buf("ps%d" % i, excl=True)))
        self.prr = 0

    def psum(self):
        p = self.ps[self.prr % 8]
        self.prr += 1
        return p

    def sb(self, es, name, shape, dt):
        self.uid = getattr(self, "uid", 0) + 1
        name = "%s_%d" % (name, self.uid)
        t = es.enter_context(self.nc.sbuf_tensor(name, shape, dt))
        return t, Buf(name)

    def mm(self, out, lhsT, rhs, start, stop, r, w):
        self.S.pe_slices = getattr(self.S, "pe_slices", 0) + (2 if lhsT.dtype == F32 else 1)
        return self.S.op("pe", lambda e: e.matmul(out, lhsT, rhs, start=start, stop=stop), r=r, w=w)

    def tr(self, out, in_, ident, r, w):
        self.S.pe_slices = getattr(self.S, "pe_slices", 0) + 1
        return self.S.op("pe", lambda e: e.transpose(out, in_, ident), r=r, w=w)

    def ts(self, eng, out, in0, s1, s2, op0, op1, r, w):
        return self.S.op(eng, lambda e: e.tensor_scalar(out, in0, s1, s2, op0, op1), r=r, w=w)

    def stt(self, eng, out, in0, scalar, in1, op0, op1, r, w):
        eng = "dve"
        return self.S.op(eng, lambda e: e.scalar_tensor_tensor(out, in0, scalar, in1, op0, op1), r=r, w=w)

    def tt(self, eng, out, in0, in1, op, r, w):
        return self.S.op(eng, lambda e: e.tensor_tensor(out, in0, in1, op), r=r, w=w)

    def cp(self, eng, out, in_, r, w):
        if eng == "act":
            return self.S.op("act", lambda e: e.copy(out, in_), r=r, w=w)
        return self.S.op(eng, lambda e: e.tensor_copy(out, in_), r=r, w=w)

    def act(self, out, in_, func, r, w, bias=None, scale=None):
        kw = {}
        if bias is not None:
            kw["bias"] = bias
        if scale is not None:
            kw["scale"] = scale
        return self.S.op("act", lambda e: e.activation(out, in_, func, **kw), r=r, w=w)

    def memset(self, eng, ap, val, w):
        return self.S.op(eng, lambda e: e.memset(ap, val), w=w)


def emit_load_cols(k, dst_ap, dst_buf, src2d, R, C, Wd=128):
    S = k.S
    with ExitStack() as es:
        st, stb = k.sb(es, "lc_st", [128, 128], F32)
        S.dma("sp", st[0:R, 0:Wd], src2d, w=[stb])
        pt, pb = k.psum()
        k.tr(pt[0:Wd, 0:R], st[0:R, 0:Wd], C["ident"][0:R, 0:R], r=[stb, C["b"]], w=[pb])
        k.cp("dve", dst_ap, pt[0:Wd, 0:R], r=[pb], w=[dst_buf])
        S.barrier()


def emit_adaln(k, C, W, condT, condb, modT, modb, l):
    S = k.S
    with ExitStack() as es:
        bm, bmb = k.sb(es, "ad_bm", [128, 72], F32)
        emit_load_cols(k, bm[:, :], bmb, W["b_mod"][l].rearrange("(r p) -> r p", p=128), 72, C)
        wm = [k.sb(es, "ad_w%d" % i, [128, KC, 512], BF16) for i in range(2)]
        pm, pmb = k.psum()
        pmv = pm[:, 0:144].rearrange("p (j g) -> p j g", g=2)
        for blk in range(18):
            wt, wb = wm[blk % 2]
            S.dma("pool", wt[:, :, :],
                  W["w_mod"][l][:, blk * 512:(blk + 1) * 512].rearrange("(kc p) o -> p kc o", p=128), w=[wb])
            for j4 in range(4):
                j = blk * 4 + j4
                for kc in range(KC):
                    k.mm(pmv[:, j, :], wt[:, kc, j4 * 128:(j4 + 1) * 128], condT[:, kc, :],
                         start=(kc == 0), stop=(kc == KC - 1), r=[wb, condb], w=[pmb])
        for g in range(2):
            k.tt("dve", modT[:, l, g, :], pmv[:, :, g], bm[:, :], ALU.add, r=[pmb, bmb], w=[modb])
        S.barrier()


def emit_ln(k, C, yT, yb, t0, T, gcol, bcol, cb):
    S = k.S
    with ExitStack() as es:
        sq = [k.sb(es, "ln_sq%d" % i, [128, 512], F32) for i in range(2)]
        hl = [[k.sb(es, "ln_hl%d_%d" % (j, i), [128, 512], BF16) for i in range(2)] for j in range(4)]
        mean, meanb = k.sb(es, "ln_mean", [128, 512], F32)
        rstd, rstdb = k.sb(es, "ln_rstd", [128, 512], F32)
        tmp = [k.sb(es, "ln_tmp%d" % i, [128, 512], F32) for i in range(2)]
        ybs = [Buf("ln_y%d" % i) for i in range(KC)]
        for tg in range(T // 512):
            sl = slice(t0 + tg * 512, t0 + (tg + 1) * 512)
            p1, p1b = k.psum()
            p2, p2b = k.psum()
            for kc in range(KC):
                q, qb = sq[kc % 2]
                hi, hib = hl[0][kc % 2]
                lo, lob = hl[1][kc % 2]
                qhi, qhib = hl[2][kc % 2]
                qlo, qlob = hl[3][kc % 2]
                zc = yT[:, kc, sl]
                k.cp("act", hi[:, :], zc, r=[yb], w=[hib])
                k.act(q[:, :], zc, AF.Square, r=[yb], w=[qb])
                k.tt("dve", lo[:, :], zc, hi[:, :], ALU.subtract, r=[yb, hib], w=[lob])
                k.cp("pool", qhi[:, :], q[:, :], r=[qb], w=[qhib])
                k.tt("dve", qlo[:, :], q[:, :], qhi[:, :], ALU.subtract, r=[qb, qhib], w=[qlob])
                k.mm(p1[:, :], C["onesb"][:, :], hi[:, :], start=(kc == 0), stop=False, r=[hib, C["b"]], w=[p1b])
                k.mm(p1[:, :], C["onesb"][:, :], lo[:, :], start=False, stop=(kc == KC - 1), r=[lob, C["b"]], w=[p1b])
                k.mm(p2[:, :], C["onesb"][:, :], qhi[:, :], start=(kc == 0), stop=False, r=[qhib, C["b"]], w=[p2b])
                k.mm(p2[:, :], C["onesb"][:, :], qlo[:, :], start=False, stop=(kc == KC - 1), r=[qlob, C["b"]], w=[p2b])
            k.ts("dve", mean[:, :], p1[:, :], 1.0 / D, None, ALU.mult, ALU.bypass, r=[p1b], w=[meanb])
            t, tb = tmp[0]
            k.tt("dve", t[:, :], mean[:, :], mean[:, :], ALU.mult, r=[meanb], w=[tb])
            k.stt("dve", rstd[:, :], p2[:, :], 1.0 / D, t[:, :], ALU.mult, ALU.subtract, r=[p2b, tb], w=[rstdb])
            k.ts("dve", rstd[:, :], rstd[:, :], LN_EPS, None, ALU.add, ALU.bypass, r=[rstdb], w=[rstdb])
            k.act(rstd[:, :], rstd[:, :], AF.Sqrt, r=[rstdb], w=[rstdb])
            k.S.op("dve", lambda e: e.reciprocal(rstd[:, :], rstd[:, :]), r=[rstdb], w=[rstdb])
            for kc in range(KC):
                t, tb = tmp[kc % 2]
                k.tt("dve", t[:, :], yT[:, kc, sl], mean[:, :], ALU.subtract, r=[yb, meanb], w=[tb])
                k.tt("pool" if kc % 2 else "dve", t[:, :], t[:, :], rstd[:, :], ALU.mult, r=[tb, rstdb], w=[tb])
                k.act(yT[:, kc, sl], t[:, :], AF.Identity, r=[tb, cb], w=[ybs[kc]],
                      bias=bcol[:, kc:kc + 1], scale=gcol[:, kc:kc + 1])
        S.barrier()


def emit_ffn(k, C, W, yT, yb, t0, l, s, mod, modb, lncols, lnb):
    S = k.S
    T = 1024
    with ExitStack() as es:
        hT, hb = k.sb(es, "f_hT", [128, KC, T], BF16)
        aT, ab = k.sb(es, "f_aT", [128, FC, T], BF16)
        for kc in range(KC):
            if kc % 2 == 0:
                k.ts("dve", hT[:, kc, :], yT[:, kc, t0:t0 + T],
                     mod["sc1"][:, kc:kc + 1], mod["shift"][:, kc:kc + 1], ALU.mult, ALU.add, r=[yb, modb], w=[hb])
            else:
                k.act(hT[:, kc, :], yT[:, kc, t0:t0 + T], AF.Identity, r=[yb, modb], w=[hb],
                      bias=mod["shift"][:, kc:kc + 1], scale=mod["sc1"][:, kc:kc + 1])
        with ExitStack() as es2:
            NB = 2
            wg = [k.sb(es2, "f_wg%d" % i, [128, KC, 256], BF16) for i in range(NB)]
            wu = [k.sb(es2, "f_wu%d" % i, [128, KC, 256], BF16) for i in range(NB)]
            sg = [k.sb(es2, "f_sg%d" % i, [128, 512], F32) for i in range(2)]

            def load(gi):
                c0 = gi * 256
                S.dma("pool", wg[gi % NB][0][:, :, :],
                      W["ffn_w_gate"][l, s][:, c0:c0 + 256].rearrange("(kc p) o -> p kc o", p=128),
                      w=[wg[gi % NB][1]])
                S.dma("pool", wu[gi % NB][0][:, :, :],
                      W["ffn_w_up"][l, s][:, c0:c0 + 256].rearrange("(kc p) o -> p kc o", p=128),
                      w=[wu[gi % NB][1]])

            wd = [k.sb(es2, "f_wd%d" % i, [128, FC, 256], BF16) for i in range(2)]

            def loadd(gi):
                c0 = gi * 256
                S.dma("pool", wd[gi % 2][0][:, :, :],
                      W["ffn_w_down"][l, s][:, c0:c0 + 256].rearrange("(kc p) o -> p kc o", p=128),
                      w=[wd[gi % 2][1]])

            load(0)
            n = 0
            for gi in range(FC // 2):
                if gi + 1 < FC // 2:
                    load(gi + 1)
                else:
                    loadd(0)
                    loadd(1)
                wgt, wgb = wg[gi % NB]
                wut, wub = wu[gi % NB]
                for o2 in range(2):
                    oc = gi * 2 + o2
                    for tg in range(T // 512):
                        sl = slice(tg * 512, (tg + 1) * 512)
                        pg, pgb = k.psum()
                        pu, pub = k.psum()
                        for kc in range(KC):
                            k.mm(pg[:, :], wgt[:, kc, o2 * 128:(o2 + 1) * 128], hT[:, kc, sl],
                                 start=(kc == 0), stop=(kc == KC - 1), r=[wgb, hb], w=[pgb])
                        for kc in range(KC):
                            k.mm(pu[:, :], wut[:, kc, o2 * 128:(o2 + 1) * 128], hT[:, kc, sl],
                                 start=(kc == 0), stop=(kc == KC - 1), r=[wub, hb], w=[pub])
                        st, stb = sg[n % 2]
                        n += 1
                        k.act(st[:, :], pg[:, :], AF.Silu, r=[pgb], w=[stb])
                        k.tt("dve", aT[:, oc, sl], st[:, :], pu[:, :], ALU.mult, r=[stb, pub], w=[ab])
            for gi in range(4):
                if 1 <= gi < 3:
                    loadd(gi + 1)
                wdt, wdb = wd[gi % 2]
                for o2 in range(2):
                    oc = gi * 2 + o2
                    for tg in range(T // 512):
                        sl = slice(tg * 512, (tg + 1) * 512)
                        ysl = slice(t0 + tg * 512, t0 + (tg + 1) * 512)
                        pf, pfb = k.psum()
                        for kc in range(FC):
                            k.mm(pf[:, :], wdt[:, kc, o2 * 128:(o2 + 1) * 128], aT[:, kc, sl],
                                 start=(kc == 0), stop=(kc == FC - 1), r=[wdb, ab], w=[pfb])
                        k.stt("dve", yT[:, oc, ysl], pf[:, :], mod["gate"][:, oc:oc + 1], yT[:, oc, ysl],
                              ALU.mult, ALU.add, r=[pfb, yb, modb], w=[yb])
            S.barrier()
    emit_ln(k, C, yT, yb, t0, T, lncols[0], lncols[1], lnb)


def emit_load_x(k, C, yT, yb, xsrc, T):
    S = k.S
    with ExitStack() as es:
        xt = [k.sb(es, "lx%d" % i, [128, D], F32) for i in range(2)]
        for tt in range(T // 128):
            x, xb = xt[tt % 2]
            S.dma("sp", x[:, :], xsrc[tt * 128:(tt + 1) * 128, :], w=[xb])
            for h2 in range(2):
                pt, pb = k.psum()
                for c4 in range(4):
                    kc = h2 * 4 + c4
                    k.tr(pt[:, c4 * 128:(c4 + 1) * 128], x[:, kc * 128:(kc + 1) * 128], C["ident"][:, :],
                         r=[xb, C["b"]], w=[pb])
                k.cp("act" if h2 else "dve",
                     yT[:, h2 * 4:(h2 + 1) * 4, tt * 128:(tt + 1) * 128],
                     pt[:, :].rearrange("p (c t) -> p c t", t=128), r=[pb], w=[yb])
        S.barrier()


def emit_store_y(k, C, yT, yb, ydst, T):
    S = k.S
    with ExitStack() as es:
        ot = [k.sb(es, "sy%d" % i, [128, D], F32) for i in range(2)]
        for tt in range(T // 128):
            o, ob = ot[tt % 2]
            for h2 in range(2):
                pt, pb = k.psum()
                for c4 in range(4):
                    kc = h2 * 4 + c4
                    k.tr(pt[:, c4 * 128:(c4 + 1) * 128], yT[:, kc, tt * 128:(tt + 1) * 128], C["ident"][:, :],
                         r=[yb, C["b"]], w=[pb])
                k.cp("act" if h2 else "dve", o[:, h2 * 512:(h2 + 1) * 512], pt[:, :], r=[pb], w=[ob])
            S.dma("sp", ydst[tt * 128:(tt + 1) * 128, :], o[:, :], r=[ob])
        S.barrier()


NEG = -30000.0
LAST_MARKS = None
DBG = {"attn_stop": 9}
C_SCALE = 64 ** -0.5


def host_consts():
    c = {}
    c["ident"] = np.eye(128, dtype=np.float32)
    c["ones"] = np.ones((128, 128), dtype=np.float32)
    ii = np.arange(128)[:, None]
    jj = np.arange(128)[None, :]
    c["mprev"] = np.where(jj >= ii, 0.0, NEG).astype(np.float32)
    c["mnext"] = np.where(jj <= ii, 0.0, NEG).astype(np.float32)
    t = np.arange(TS)
    row = (t // 64).astype(np.float32)
    col = (t % 64).astype(np.float32)
    inv = (10000.0 ** (-np.arange(16, dtype=np.float32) / 16)).astype(np.float32)
    cos = np.zeros((64, TS), np.float32)
    sin = np.zeros((64, TS), np.float32)
    perm = np.zeros((128, 128), np.float32)
    for d in range(64):
        pos = row if d < 32 else col
        f = inv[d % 16]
        ang = (pos * f).astype(np.float32)
        cos[d] = np.cos(ang)
        first = (d % 32) < 16
        sin[d] = -np.sin(ang) if first else np.sin(ang)
        partner = d + 16 if first else d - 16
        perm[partner, d] = 1.0
    c["tri_f"] = (ii <= jj).astype(np.float32)
    c["tri_b"] = (ii >= jj).astype(np.float32)
    sel = np.zeros((1, 256), np.float32)
    sel[0, 0:64] = 1.0
    sel[0, 128 + 64:256] = 1.0
    c["sel"] = sel
    same = (ii // 64) == (jj // 64)
    c["gtri_f"] = (same & (ii <= jj)).astype(np.float32)
    c["gtri_b"] = (same & (ii >= jj)).astype(np.float32)
    c["gnmsl_f"] = -(same & (jj < ii)).astype(np.float32)
    c["gnmsl_b"] = -(same & (jj > ii)).astype(np.float32)
    c["gblk"] = same.astype(np.float32)
    gch = np.zeros((128, 128), np.float32)
    gch[0:64, 0] = 1.0
    gch[64:128, 1] = 1.0
    c["gch"] = gch
    c["cos"] = cos
    c["sin"] = sin
    c["perm"] = perm
    return c


def emit_mixer_tail(k, C, wout, KCI, inT, inb, yT, yb, T, gate5, modb, lncols, lnb):
    S = k.S
    S.mark("  mixer tail")
    with ExitStack() as es:
        wo = [k.sb(es, "mt_w%d" % i, [128, KCI, 256], BF16) for i in range(2)]

        def load(gi):
            S.dma("pool", wo[gi % 2][0][:, :, :],
                  wout[:, gi * 256:(gi + 1) * 256].rearrange("(kc p) o -> p kc o", p=128), w=[wo[gi % 2][1]])

        load(0)
        for gi in range(4):
            if gi + 1 < 4:
                load(gi + 1)
            wt, wb = wo[gi % 2]
            for o2 in range(2):
                oc = gi * 2 + o2
                for tg in range(T // 512):
                    sl = slice(tg * 512, (tg + 1) * 512)
                    pf, pfb = k.psum()
                    for kc in range(KCI):
                        k.mm(pf[:, :], wt[:, kc, o2 * 128:(o2 + 1) * 128], inT[:, kc, sl],
                             start=(kc == 0), stop=(kc == KCI - 1), r=[wb, inb], w=[pfb])
                    k.stt("dve", yT[:, oc, sl], pf[:, :], gate5[:, oc:oc + 1], yT[:, oc, sl],
                          ALU.mult, ALU.add, r=[pfb, yb, modb], w=[yb])
        S.barrier()
    for t0 in range(0, T, 1024):
        emit_ln(k, C, yT, yb, t0, 1024, lncols[0], lncols[1], lnb)


def emit_modulate(k, hT, hb, yT, yb, T, mod, modb):
    for kc in range(KC):
        if kc % 2 == 0:
            k.ts("dve", hT[:, kc, 0:T], yT[:, kc, 0:T],
                 mod["sc1"][:, kc:kc + 1], mod["shift"][:, kc:kc + 1], ALU.mult, ALU.add, r=[yb, modb], w=[hb])
        else:
            k.act(hT[:, kc, 0:T], yT[:, kc, 0:T], AF.Identity, r=[yb, modb], w=[hb],
                  bias=mod["shift"][:, kc:kc + 1], scale=mod["sc1"][:, kc:kc + 1])


def emit_attn(k, C, W, O, yT, yb, T, grp, mod, modb, lncols, lnb):
    S = k.S
    w_in = W["attn_w_in"][0]
    NT = T // 128
    with ExitStack() as es:
        if grp == 1:
            for nm in ("cos", "sin"):
                t_, _ = k.sb(es, "k_" + nm, [64, TS], F32)
                C[nm] = t_
                S.dma("sp", t_[:, :], W["c_" + nm][:, :], w=[C["b"]])
        OT, OTb = k.sb(es, "at_OT", [128, 8, T], BF16)
        sinkbc, sinkb = k.sb(es, "at_sink", [128, 16], F32)
        with ExitStack() as es1:
            s1, s1b = k.sb(es1, "at_s1", [1, 16], F32)
            S.dma("sp", s1[0:1, :], W["attn_sink"][0:1, :], w=[s1b])
            pt, pb = k.psum()
            k.mm(pt[:, 0:16], C["ones"][0:1, :], s1[0:1, :], start=True, stop=True, r=[s1b, C["b"]], w=[pb])
            k.cp("dve", sinkbc[:, :], pt[:, 0:16], r=[pb], w=[sinkb])
            S.barrier()
        with ExitStack() as es1:
            hts = [k.sb(es1, "at_hT%d" % i, [128, KC, 512], BF16) for i in range(2)]
            hcnt = [0]

            def mod_tile(tg):
                hT_, hb_ = hts[hcnt[0] % 2]
                hcnt[0] += 1
                for kc in range(KC):
                    if kc % 2 == 0:
                        k.ts("dve", hT_[:, kc, :], yT[:, kc, tg * 512:(tg + 1) * 512],
                             mod["sc1"][:, kc:kc + 1], mod["shift"][:, kc:kc + 1], ALU.mult, ALU.add,
                             r=[yb, modb], w=[hb_])
                    else:
                        k.act(hT_[:, kc, :], yT[:, kc, tg * 512:(tg + 1) * 512], AF.Identity, r=[yb, modb], w=[hb_],
                              bias=mod["shift"][:, kc:kc + 1], scale=mod["sc1"][:, kc:kc + 1])
                return hT_, hb_
            vtok, vtb = k.sb(es1, "at_vtok", [128, NT, 256], BF16)
            with ExitStack() as es2:
                wkv, wkvb = k.sb(es2, "at_wkv", [128, KC, 512], BF16)
                S.dma("pool", wkv[:, :, :], w_in[:, 1024:1536].rearrange("(kc p) o -> p kc o", p=128), w=[wkvb])
                st = [k.sb(es2, "at_kvst%d" % i, [128, 512], F32) for i in range(2)]
                for tt in range(NT):
                    if tt % 4 == 0:
                        hT, hb = mod_tile(tt // 4)
                    pk, pkb = k.psum()
                    for kc in range(KC):
                        k.mm(pk[:, :], hT[:, kc, (tt % 4) * 128:(tt % 4 + 1) * 128], wkv[:, kc, :],
                             start=(kc == 0), stop=(kc == KC - 1), r=[hb, wkvb], w=[pkb])
                    k.cp("act", vtok[:, tt, :], pk[:, 256:512], r=[pkb], w=[vtb])
                    if grp == 0:
                        s_, sb_ = st[tt % 2]
                        k.cp("dve", s_[:, :], pk[:, :], r=[pkb], w=[sb_])
                        S.dma("sp", O["new_cache_k"][tt * 128:(tt + 1) * 128, :], s_[:, 0:256], r=[sb_])
                        S.dma("sp", O["new_cache_v"][tt * 128:(tt + 1) * 128, :], s_[:, 256:512], r=[sb_])
                S.barrier()
            NCTX = 0
            if grp == 1:
                NCTX = 2
                ckT, ckb = k.sb(es1, "at_ckT", [64, 4, 256], BF16)
                cvp, cvb = k.sb(es1, "at_cvp", [128, 2, 4, 2, 128], BF16)
                k.memset("pool", cvp[:, :, :, :, :], 0.0, w=[cvb])
                with ExitStack() as es2:
                    ck, ckfb = k.sb(es2, "at_ck", [128, 2, 256], F32)
                    cv, cvfb = k.sb(es2, "at_cv", [128, 2, 256], F32)
                    S.dma("sp", ck[:, :, :], W["cache_k"].rearrange("(t p) f -> p t f", p=128), w=[ckfb])
                    S.dma("sp", cv[:, :, :], W["cache_v"].rearrange("(t p) f -> p t f", p=128), w=[cvfb])
                    for tt in range(2):
                        for g in range(4):
                            pt, pb = k.psum()
                            k.tr(pt[0:64, 0:128], ck[:, tt, g * 64:(g + 1) * 64], C["ident"][:, :],
                                 r=[ckfb, C["b"]], w=[pb])
                            k.cp("dve", ckT[:, g, tt * 128:(tt + 1) * 128], pt[0:64, 0:128], r=[pb], w=[ckb])
                            k.cp("act", cvp[:, tt, g, 0, 0:64], cv[:, tt, g * 64:(g + 1) * 64], r=[cvfb], w=[cvb])
                            k.cp("pool", cvp[:, tt, g, 1, 64:128], cv[:, tt, g * 64:(g + 1) * 64], r=[cvfb], w=[cvb])
                    S.barrier()
            for g in range(4 if DBG["attn_stop"] > 1 else 0):
                with ExitStack() as es2:
                    qT, qb = k.sb(es2, "at_qT", [64, 4, T], BF16)
                    kT, kb = k.sb(es2, "at_kT", [64, T], BF16)
                    vp, vpb = k.sb(es2, "at_vp", [128, NT, 2, 128], BF16)
                    k.memset("pool", vp[:, :, :, :], 0.0, w=[vpb])
                    for tt in range(NT):
                        k.cp("act", vp[:, tt, 0, 0:64], vtok[:, tt, g * 64:(g + 1) * 64], r=[vtb], w=[vpb])
                        k.cp("pool", vp[:, tt, 1, 64:128], vtok[:, tt, g * 64:(g + 1) * 64], r=[vtb], w=[vpb])
                    with ExitStack() as es3:
                        wq, wqb = k.sb(es3, "at_wq", [128, KC, 256], BF16)
                        wk, wkb = k.sb(es3, "at_wk", [128, KC, 64], BF16)
                        S.dma("pool", wq[:, :, :], w_in[:, g * 256:(g + 1) * 256].rearrange("(kc p) o -> p kc o", p=128),
                              w=[wqb])
                        S.dma("pool", wk[:, :, :],
                              w_in[:, 1024 + g * 64:1024 + (g + 1) * 64].rearrange("(kc p) o -> p kc o", p=128), w=[wkb])
                        raw = [k.sb(es3, "at_raw%d" % i, [64, 512], F32) for i in range(2)]
                        tm = [k.sb(es3, "at_tm%d" % i, [64, 512], F32) for i in range(2)]
                        n = 0
                        for tg in range(T // 512):
                            hT, hb = mod_tile(tg)
                            for hh in range(5):
                                sl = slice(tg * 512, (tg + 1) * 512)
                                pq, pqb = k.psum()
                                for kc in range(KC):
                                    lh = wq[:, kc, hh * 64:(hh + 1) * 64] if hh < 4 else wk[:, kc, :]
                                    k.mm(pq[0:64, :], lh, hT[:, kc, :], start=(kc == 0), stop=(kc == KC - 1),
                                         r=[wqb, wkb, hb], w=[pqb])
                                dst = qT[:, hh, sl] if hh < 4 else kT[:, sl]
                                dstb = qb if hh < 4 else kb
                                if grp == 0:
                                    k.cp("act" if n % 2 else "dve", dst, pq[0:64, :], r=[pqb], w=[dstb])
                                else:
                                    rw, rwb = raw[n % 2]
                                    t_, tb_ = tm[n % 2]
                                    k.cp("act", rw[:, :], pq[0:64, :], r=[pqb], w=[rwb])
                                    ps2, ps2b = k.psum()
                                    k.mm(ps2[0:64, :], C["perm"][0:64, 0:64], rw[:, :], start=True, stop=True,
                                         r=[rwb, C["b"]], w=[ps2b])
                                    k.tt("dve", t_[:, :], ps2[0:64, :], C["sin"][:, sl], ALU.mult, r=[ps2b, C["b"]], w=[tb_])
                                    k.tt("pool", rw[:, :], rw[:, :], C["cos"][:, sl], ALU.mult, r=[rwb, C["b"]], w=[rwb])
                                    k.tt("dve", dst, rw[:, :], t_[:, :], ALU.add, r=[rwb, tb_], w=[dstb])
                                n += 1
                        S.barrier()
                    with ExitStack() as es3:
                        NKMAX = 640 if grp == 1 else 256
                        WU = 2
                        sall = [k.sb(es3, "at_sall%d" % i, [128, NKMAX], F32) for i in range(2 * WU)]
                        pn = [k.sb(es3, "at_pn%d" % i, [128, NKMAX], BF16) for i in range(2 * WU)]
                        pT = [k.sb(es3, "at_pT%d" % i, [128, 2, 5, 128], BF16) for i in range(WU)]
                        sm = [k.sb(es3, "at_sm%d" % i, [128, 8], F32) for i in range(2 * WU)]
                        OTbs = [Buf("OT%d" % i) for i in range(WU)]

                        def unit_gen(uid, qt, cpair):
                            qsl = slice(qt * 128, (qt + 1) * 128)
                            if grp == 0:
                                seq = qt // 2
                                kblocks = [("full", seq * 2), ("full", seq * 2 + 1)]
                            else:
                                kblocks = []
                                if qt > 0:
                                    kblocks.append(("prev", qt - 1))
                                kblocks.append(("full", qt))
                                if qt < NT - 1:
                                    kblocks.append(("next", qt + 1))
                                kblocks += [("ctx", 0), ("ctx", 1)]
                            nkb = len(kblocks)
                            nk = nkb * 128
                            ptile, ptb_ = pT[uid % WU]
                            for par in range(2):
                                hh = cpair * 2 + par
                                h = g * 4 + hh
                                slot = (uid % WU) * 2 + par
                                sa, sab = sall[slot]
                                pn_, pnb = pn[slot]
                                sm_, smb = sm[slot]
                                banks = []
                                for b0 in range(0, nkb, 4):
                                    ps, psb = k.psum()
                                    banks.append((ps, psb))
                                    for bi in range(b0, min(nkb, b0 + 4)):
                                        kind, kt = kblocks[bi]
                                        if kind == "ctx":
                                            rhs = ckT[:, g, kt * 128:(kt + 1) * 128]
                                            rb = ckb
                                        else:
                                            rhs = kT[:, kt * 128:(kt + 1) * 128]
                                            rb = kb
                                        k.mm(ps[:, (bi - b0) * 128:(bi - b0 + 1) * 128], qT[:, hh, qsl], rhs,
                                             start=True, stop=True, r=[qb, rb], w=[psb])
                                yield
                                for bi, (kind, kt) in enumerate(kblocks):
                                    ps, psb = banks[bi // 4]
                                    src = ps[:, (bi % 4) * 128:(bi % 4 + 1) * 128]
                                    dsts = sa[:, bi * 128:(bi + 1) * 128]
                                    if kind == "prev":
                                        k.tt("dve", dsts, src, C["mprev"][:, :], ALU.add, r=[psb, C["b"]], w=[sab])
                                    elif kind == "next":
                                        k.tt("dve", dsts, src, C["mnext"][:, :], ALU.add, r=[psb, C["b"]], w=[sab])
                                    else:
                                        k.cp("act", dsts, src, r=[psb], w=[sab])
                                yield
                                k.S.op("dve", lambda e: e.reduce_max(sm_[:, 0:1], sa[:, 0:nk], mybir.AxisListType.X),
                                       r=[sab], w=[smb])
                                k.ts("dve", sm_[:, 1:2], sm_[:, 0:1], C_SCALE, sinkbc[:, h:h + 1], ALU.mult, ALU.max,
                                     r=[smb, sinkb], w=[smb])
                                k.ts("dve", sm_[:, 2:3], sm_[:, 1:2], -1.0, None, ALU.mult, ALU.bypass, r=[smb], w=[smb])
                                yield
                                k.act(sa[:, 0:nk], sa[:, 0:nk], AF.Exp, r=[sab, smb], w=[sab],
                                      bias=sm_[:, 2:3], scale=C_SCALE)
                                k.act(sm_[:, 3:4], sinkbc[:, h:h + 1], AF.Exp, r=[sinkb, smb], w=[smb],
                                      bias=sm_[:, 2:3], scale=1.0)
                                yield
                                k.S.op("dve", lambda e: e.reduce_sum(sm_[:, 4:5], sa[:, 0:nk], mybir.AxisListType.X),
                                       r=[sab], w=[smb])
                                k.tt("dve", sm_[:, 5:6], sm_[:, 4:5], sm_[:, 3:4], ALU.add, r=[smb], w=[smb])
                                k.S.op("dve", lambda e: e.reciprocal(sm_[:, 6:7], sm_[:, 5:6]), r=[smb], w=[smb])
                                yield
                                k.act(pn_[:, 0:nk], sa[:, 0:nk], AF.Identity, r=[sab, smb], w=[pnb], scale=sm_[:, 6:7])
                                yield
                                for b0 in range(0, nkb, 4):
                                    pt_, ptb2 = k.psum()
                                    ptv = pt_[:, :].bitcast(BF16)
                                    nb = min(nkb, b0 + 4) - b0
                                    for bi in range(b0, b0 + nb):
                                        k.tr(ptv[:, (bi - b0) * 128:(bi - b0 + 1) * 128], pn_[:, bi * 128:(bi + 1) * 128],
                                             C["identb"][:, :], r=[pnb, C["b"]], w=[ptb2])
                                    k.cp("act" if b0 else "dve", ptile[:, par, b0:b0 + nb, :],
                                         ptv[:, 0:nb * 128].rearrange("p (b q) -> p b q", q=128), r=[ptb2], w=[ptb_])
                                yield
                            po, pob = k.psum()
                            nmm = 2 * nkb
                            i_ = 0
                            for par in range(2):
                                for bi, (kind, kt) in enumerate(kblocks):
                                    if kind == "ctx":
                                        lh = cvp[:, kt, g, par, :]
                                        lb = cvb
                                    else:
                                        lh = vp[:, kt, par, :]
                                        lb = vpb
                                    k.mm(po[:, 0:128], lh, ptile[:, par, bi, :], start=(i_ == 0), stop=(i_ == nmm - 1),
                                         r=[lb, ptb_], w=[pob])
                                    i_ += 1
                            yield
                            k.cp("act", OT[:, g * 2 + cpair, qsl], po[:, 0:128], r=[pob], w=[OTbs[uid % WU]])

                        if DBG.get("mem") and g == 0:
                            print("ATTN grp", grp, "sbuf free", k.nc.sbuf_bytes_remaining)
                        units = [(qt, cp_) for qt in range(NT) for cp_ in range(2)]
                        run_interleaved([unit_gen(i, qt, cp_) for i, (qt, cp_) in enumerate(units)], WU)
                        k.S.op("dve", lambda e: e.memset(sm[0][0][:, 7:8], 0.0), r=OTbs, w=[OTb, sm[0][1]])
                        S.barrier()
            S.barrier()
        emit_mixer_tail(k, C, W["attn_w_out"][0], 8, OT, OTb, yT, yb, T, mod["gate"], modb, lncols, lnb)


def run_interleaved(gens, width):
    gens = list(gens)
    active = []
    while gens or active:
        while gens and len(active) < width:
            active.append(gens.pop(0))
        for g_ in list(active):
            try:
                next(g_)
            except StopIteration:
                active.remove(g_)


def bc(ap, axis, n):
    shp = list(ap.shape)
    shp.insert(axis, n)
    return ap.unsqueeze(axis).to_broadcast(shp)


def emit_softplus(k, C, out_ap, outb, xin, xinb, P, N, es):
    a, ab = k.sb(es, "sp_a", [128, N], F32)
    k.stt("dve", a[0:P, :], xin, -1.0, xin, ALU.mult, ALU.max, r=[xinb], w=[ab])
    k.act(a[0:P, :], a[0:P, :], AF.Exp, r=[ab], w=[ab], scale=-1.0)
    k.act(a[0:P, :], a[0:P, :], AF.Ln, r=[ab, C["b"]], w=[ab], bias=C["ones"][0:P, 0:1], scale=1.0)
    k.stt("dve", out_ap, xin, 0.0, a[0:P, :], ALU.max, ALU.add, r=[xinb, ab], w=[outb])


def emit_ssd(k, C, W, O, SCR, yT, yb, T, grp, j, mod, modb, lncols, lnb):
    S = k.S
    w_in = W["ssd_w_in"][j]
    NT = T // 128
    nseq, L = (4, 256) if grp == 0 else (1, 2048)
    TPS = L // 128
    with ExitStack() as es:
        cw, cwb = k.sb(es, "sd_cw", [128, 120], F32)
        emit_load_cols(k, cw[:, :], cwb, W["ssd_conv_w"][j].rearrange("k (c p) -> (k c) p", p=128), 120, C)
        cbias, cbb = k.sb(es, "sd_cb", [128, 24], F32)
        emit_load_cols(k, cbias[:, :], cbb, W["ssd_conv_b"][j].rearrange("(c p) -> c p", p=128), 24, C)
        normg, ngb = k.sb(es, "sd_ng", [128, 16], F32)
        emit_load_cols(k, normg[:, :], ngb, W["ssd_norm"][j].rearrange("(c p) -> c p", p=128), 16, C)
        dtb, dtbb = k.sb(es, "sd_dtb", [64, 2], F32)
        emit_load_cols(k, dtb[:, 0:1], dtbb, W["ssd_dt_bias"][j:j + 1].rearrange("o d h -> o (d h)"), 1, C, Wd=64)
        emit_load_cols(k, dtb[:, 1:2], dtbb, W["ssd_a_log"][j:j + 1].rearrange("o d h -> o (d h)"), 1, C, Wd=64)
        k.act(dtb[:, 1:2], dtb[:, 1:2], AF.Exp, r=[dtbb], w=[dtbb])
        k.ts("dve", dtb[:, 1:2], dtb[:, 1:2], -1.0, None, ALU.mult, ALU.bypass, r=[dtbb], w=[dtbb])
        dcol, dcb = k.sb(es, "sd_dcol", [128, 16], F32)
        with ExitStack() as es1:
            drow, drb = k.sb(es1, "sd_drow", [1, 32], F32)
            S.dma("sp", drow[0:1, :], W["ssd_d"][j:j + 1, :], w=[drb])
            pt, pb = k.psum()
            for par in range(2):
                k.mm(pt[:, 0:16], C["sel"][0:1, par * 128:(par + 1) * 128], drow[0:1, par:32:2],
                     start=(par == 0), stop=(par == 1), r=[drb, C["b"]], w=[pb])
            k.cp("dve", dcol[:, :], pt[:, 0:16], r=[pb], w=[dcb])
            S.barrier()
        dtT, dtTb = k.sb(es, "sd_dtT", [64, T], F32)
        dtaT, dtaTb = k.sb(es, "sd_dtaT", [64, T], F32)

        with ExitStack() as es1:
            hT, hb = k.sb(es1, "sd_hT", [128, KC, T], BF16)
            emit_modulate(k, hT, hb, yT, yb, T, mod, modb)
            wbuf = [k.sb(es1, "sd_w%d" % i, [128, KC, 256], BF16) for i in range(2)]
            pad, padb = k.sb(es1, "sd_pad", [128, nseq, L + 4], F32)
            k.memset("pool", pad[:, :, :], 0.0, w=[padb])
            acc = [k.sb(es1, "sd_acc%d" % i, [128, nseq, L], F32) for i in range(2)]
            ob16 = [k.sb(es1, "sd_o%d" % i, [128, T], BF16) for i in range(2)]

            def loadw(gi):
                S.dma("pool", wbuf[gi % 2][0][:, :, :],
                      w_in[:, gi * 256:(gi + 1) * 256].rearrange("(kc p) o -> p kc o", p=128), w=[wbuf[gi % 2][1]])

            loadw(0)
            n = 0
            for gi in range(20):
                if gi + 1 < 20:
                    loadw(gi + 1)
                wt, wb = wbuf[gi % 2]
                for o2 in range(2):
                    oc = gi * 2 + o2
                    o16, o16b = ob16[n % 2]
                    ac, acb = acc[n % 2]
                    n += 1
                    for tg in range(T // 512):
                        ps, psb = k.psum()
                        for kc in range(KC):
                            k.mm(ps[:, :], wt[:, kc, o2 * 128:(o2 + 1) * 128], hT[:, kc, tg * 512:(tg + 1) * 512],
                                 start=(kc == 0), stop=(kc == KC - 1), r=[wb, hb], w=[psb])
                        if oc < 16:
                            k.act(o16[:, tg * 512:(tg + 1) * 512], ps[:, :], AF.Silu, r=[psb], w=[o16b])
                        else:
                            if grp == 0:
                                k.cp("act", pad[:, tg * 2:(tg + 1) * 2, 2:2 + L],
                                     ps[:, :].rearrange("p (s l) -> p s l", l=L), r=[psb], w=[padb])
                            else:
                                k.cp("act", pad[:, 0, 2 + tg * 512:2 + (tg + 1) * 512], ps[:, :], r=[psb], w=[padb])
                    if oc < 16:
                        S.dma("sp", SCR["zs"][oc, :, 0:T], o16[:, :], r=[o16b], w=[SCR["zsb"]])
                    else:
                        c = oc - 16
                        e1 = "dve" if c % 2 == 0 else "pool"
                        k.ts(e1, ac[:, :, :], pad[:, :, 0:L], cw[:, c:c + 1], cbias[:, c:c + 1], ALU.mult, ALU.add,
                             r=[padb, cwb, cbb], w=[acb])
                        for kk in range(1, 5):
                            k.stt(e1, ac[:, :, :], pad[:, :, kk:kk + L], cw[:, kk * 24 + c:kk * 24 + c + 1], ac[:, :, :],
                                  ALU.mult, ALU.add, r=[padb, cwb, acb], w=[acb])
                        k.act(o16[:, :], ac[:, :, :].rearrange("p s l -> p (s l)"), AF.Silu, r=[acb], w=[o16b])
                        S.dma("sp", SCR["xc"][c, :, 0:T], o16[:, :], r=[o16b], w=[SCR["xcb"]])
            wdt, wdtb = k.sb(es1, "sd_wdt", [128, KC, 64], BF16)
            S.dma("pool", wdt[:, :, :], w_in[:, 5120:5184].rearrange("(kc p) o -> p kc o", p=128), w=[wdtb])
            xs_, xsb_ = k.sb(es1, "sd_dtx", [64, 512], F32)
            for tg in range(T // 512):
                ps, psb = k.psum()
                for kc in range(KC):
                    k.mm(ps[0:64, :], wdt[:, kc, :], hT[:, kc, tg * 512:(tg + 1) * 512],
                         start=(kc == 0), stop=(kc == KC - 1), r=[wdtb, hb], w=[psb])
                k.act(xs_[:, :], ps[0:64, :], AF.Identity, r=[psb, dtbb], w=[xsb_], bias=dtb[:, 0:1], scale=1.0)
                with ExitStack() as es2:
                    emit_softplus(k, C, dtT[:, tg * 512:(tg + 1) * 512], dtTb, xs_[:, :], xsb_, 64, 512, es2)
                    S.barrier()
            k.ts("dve", dtaT[:, :], dtT[:, :], dtb[:, 1:2], None, ALU.mult, ALU.bypass, r=[dtTb, dtbb], w=[dtaTb])
            S.barrier()

        for dirn in range(2):
            S.mark("  ssd sweep%d" % dirn)
            final = dirn == 1
            TRI = C["tri_f"] if dirn == 0 else C["tri_b"]
            END = 127 if dirn == 0 else 0
            with ExitStack() as es1:
                xTt, xTb = k.sb(es1, "sw_xT", [128, 16, 128], BF16)
                bcT, bcb = k.sb(es1, "sw_bcT", [128, 8, 128], BF16)
                xz, xzb = k.sb(es1, "sw_xz", [128, 32, 128], BF16)
                hz, hzb = k.sb(es1, "sw_hz", [128, 32, 128], BF16)
                hT_, hTb = k.sb(es1, "sw_hT", [128, 2048], F32)
                k.memset("pool", xz[:, :, :], 0.0, w=[xzb])
                xzv = xz[:, :, :].rearrange("p (c r) f -> p c r f", r=2)
                hzv = hz[:, :, :].rearrange("p (c r) f -> p c r f", r=2)
                hTv = hT_[:, :].rearrange("p (c r f) -> p c r f", r=2, f=64)
                btok, btb = k.sb(es1, "sw_btok", [128, 512], BF16)
                dtk, dtkb = k.sb(es1, "sw_dtk", [128, 128], F32)
                cbm, cbmb = k.sb(es1, "sw_cbm", [128, 4, 128], F32)
                nacs, nacsb = k.sb(es1, "sw_nacs", [128, 32], F32)
                dhl, dhlb = k.sb(es1, "sw_dhl", [128, 2, 32], BF16)
                TRIb, tribb = k.sb(es1, "sw_trib", [128, 128], BF16)
                k.cp("dve", TRIb[:, :], TRI[:, :], r=[C["b"]], w=[tribb])
                nb, nbb = k.sb(es1, "sw_nb", [128, 32], F32)
                rhsA = [k.sb(es1, "sw_rhsA%d" % i, [128, 4, 128], F32) for i in range(2)]
                e4 = [k.sb(es1, "sw_e4%d" % i, [128, 4, 128], F32) for i in range(2)]
                ea4 = [k.sb(es1, "sw_ea4%d" % i, [128, 4, 128], F32) for i in range(2)]
                MT, MTb = k.sb(es1, "sw_MT", [128, 32, 128], BF16)
                CE, CEb = k.sb(es1, "sw_CE", [128, 32, 128], BF16)
                MTbs = [Buf("MT%d" % i) for i in range(8)]
                CEbs = [Buf("CE%d" % i) for i in range(8)]
                wraw, wrb = k.sb(es1, "sw_wraw", [128, 32], F32)
                craw, crb = k.sb(es1, "sw_craw", [128, 32], F32)
                gt, gtb = k.sb(es1, "sw_gt", [128, 16, 128], F32)
                xs, xsb = k.sb(es1, "sw_xs", [128, 32, 64], BF16)
                xsv = xs[:, :, :].rearrange("p (c r) f -> p c r f", r=2)
                stg, stgb = k.sb(es1, "sw_stg", [128, 16, 128], F32)
                if final:
                    zst, zsb_ = k.sb(es1, "sw_zs", [128, 16, 128], BF16)
                    sq = [k.sb(es1, "sw_sq%d" % i, [128, 128], F32) for i in range(2)]
                    rstd, rstdb = k.sb(es1, "sw_rstd", [128, 128], F32)
                    g16, g16b = k.sb(es1, "sw_g16", [128, 16, 128], BF16)
                    sqa, sqab = k.sb(es1, "sw_sqa", [128, 16, 128], F32)

                if DBG.get("mem"):
                    print("SSD sweep", dirn, "grp", grp, "sbuf free", k.nc.sbuf_bytes_remaining)

                def hz_refresh():
                    k.cp("act", hzv[:, :, 0, 0:64], hTv[:, :, 0, :], r=[hTb], w=[hzb])
                    k.cp("pool", hzv[:, :, 1, 64:128], hTv[:, :, 1, :], r=[hTb], w=[hzb])

                order = []
                for sq_i in range(nseq):
                    tl = list(range(sq_i * TPS, (sq_i + 1) * TPS))
                    order += tl[::-1] if dirn == 1 else tl
                NBF = 2 if k.nc.sbuf_bytes_remaining > 6144 + 4096 else 1
                NSL = 2
                if k.nc.sbuf_bytes_remaining > 6144 * (NBF - 1) + 6144 + 4096:
                    NSL = 3
                    rhsA.append(k.sb(es1, "sw_rhsA2", [128, 4, 128], F32))
                    e4.append(k.sb(es1, "sw_e42", [128, 4, 128], F32))
                    ea4.append(k.sb(es1, "sw_ea42", [128, 4, 128], F32))
                xTts = [(xTt, xTb)]
                bcTs = [(bcT, bcb)]
                if NBF == 2:
                    xTts.append(k.sb(es1, "sw_xT2", [128, 16, 128], BF16))
                    bcTs.append(k.sb(es1, "sw_bcT2", [128, 8, 128], BF16))

                def issue_loads(i):
                    tsl_ = slice(order[i] * 128, (order[i] + 1) * 128)
                    S.dma("sp", xTts[i % NBF][0][:, :, :], SCR["xc"][0:16, :, tsl_].rearrange("c p t -> p c t"),
                          r=[SCR["xcb"]], w=[xTts[i % NBF][1]])
                    S.dma("sp", bcTs[i % NBF][0][:, :, :], SCR["xc"][16:24, :, tsl_].rearrange("c p t -> p c t"),
                          r=[SCR["xcb"]], w=[bcTs[i % NBF][1]])

                issue_loads(0)
                gi = 0
                for sq_i in range(nseq):
                    k.memset("pool", hz[:, :, :], 0.0, w=[hzb])
                    if grp == 0:
                        k.memset("dve", hT_[:, :], 0.0, w=[hTb])
                    else:
                        S.dma("sp", stg[:, :, :], W["state_ssd"][j, dirn].rearrange("h p n -> (h p) n")
                              .rearrange("(c q) n -> q c n", q=128), w=[stgb])
                        for c4 in range(4):
                            pt, pb = k.psum()
                            for cc in range(4):
                                c = c4 * 4 + cc
                                k.tr(pt[:, cc * 128:(cc + 1) * 128], stg[:, c, :], C["ident"][:, :], r=[stgb, C["b"]], w=[pb])
                            k.cp("dve", hT_[:, c4 * 512:(c4 + 1) * 512], pt[:, :], r=[pb], w=[hTb])
                        hz_refresh()
                    tiles = list(range(sq_i * TPS, (sq_i + 1) * TPS))
                    if dirn == 1:
                        tiles = tiles[::-1]
                    for tt in tiles:
                        tsl = slice(tt * 128, (tt + 1) * 128)
                        if NBF == 1:
                            if gi > 0:
                                issue_loads(gi)
                        elif gi + 1 < len(order):
                            issue_loads(gi + 1)
                        xTt, xTb = xTts[gi % NBF]
                        bcT, bcb = bcTs[gi % NBF]
                        gi += 1
                        if final:
                            S.dma("sp", zst[:, :, :], SCR["zs"][:, :, tsl].rearrange("c p t -> p c t"), r=[SCR["zsb"]], w=[zsb_])
                            S.dma("sp", stg[:, :, :], SCR["yf"][:, :, tsl].rearrange("c p t -> p c t"), r=[SCR["yfb"]], w=[stgb])
                        for half in range(2):
                            pt, pb = k.psum()
                            ptv = pt[:, :].bitcast(BF16)
                            for c8 in range(8):
                                k.tr(ptv[:, c8 * 128:(c8 + 1) * 128], xTt[:, half * 8 + c8, :], C["identb"][:, :],
                                     r=[xTb, C["b"]], w=[pb])
                            src = ptv[:, 0:1024].rearrange("p (c r f) -> p c r f", r=2, f=64)
                            k.cp("dve", xzv[:, half * 8:(half + 1) * 8, 0, 0:64], src[:, :, 0, :], r=[pb], w=[xzb])
                            k.cp("act", xzv[:, half * 8:(half + 1) * 8, 1, 64:128], src[:, :, 1, :], r=[pb], w=[xzb])
                        pt, pb = k.psum()
                        ptv = pt[:, :].bitcast(BF16)
                        for g in range(4):
                            k.tr(ptv[:, g * 128:(g + 1) * 128], bcT[:, g, :], C["identb"][:, :], r=[bcb, C["b"]], w=[pb])
                        k.cp("dve", btok[:, :], ptv[:, 0:512], r=[pb], w=[btb])
                        pt, pb = k.psum()
                        k.tr(pt[:, 0:64], dtT[0:64, tsl], C["ident"][0:64, 0:64], r=[dtTb, C["b"]], w=[pb])
                        k.tr(pt[:, 64:128], dtaT[0:64, tsl], C["ident"][0:64, 0:64], r=[dtaTb, C["b"]], w=[pb])
                        k.cp("act", dtk[:, :], pt[:, 0:128], r=[pb], w=[dtkb])
                        dt_d = dtk[:, dirn * 32:(dirn + 1) * 32]
                        dta_d = dtk[:, 64 + dirn * 32:64 + (dirn + 1) * 32]
                        pt, pb = k.psum()
                        for g in range(4):
                            k.mm(pt[:, g * 128:(g + 1) * 128], bcT[:, g, :], bcT[:, 4 + g, :], start=True, stop=True,
                                 r=[bcb], w=[pb])
                        k.tt("dve", cbm[:, :, :], pt[:, :].rearrange("p (g q) -> p g q", q=128), bc(TRI[:, :], 1, 4),
                             ALU.mult, r=[pb, C["b"]], w=[cbmb])
                        pt, pb = k.psum()
                        k.mm(pt[:, 0:32], TRI[:, :], dta_d, start=True, stop=True, r=[dtkb, C["b"]], w=[pb])
                        k.ts("dve", nacs[:, :], pt[:, 0:32], -1.0, None, ALU.mult, ALU.bypass, r=[pb], w=[nacsb])
                        k.cp("dve", dhl[:, 0, :], dta_d, r=[dtkb], w=[dhlb])
                        k.tt("dve", dhl[:, 1, :], dta_d, dhl[:, 0, :], ALU.subtract, r=[dtkb, dhlb], w=[dhlb])
                        k.act(nb[:, :], dt_d, AF.Ln, r=[dtkb], w=[nbb])
                        k.tt("dve", nb[:, :], nb[:, :], nacs[:, :], ALU.add, r=[nbb, nacsb], w=[nbb])
                        def bank_gen(hb_):
                            h0 = hb_ * 4
                            g = hb_ // 2
                            ra, rab = rhsA[hb_ % NSL]
                            e_, eb = e4[hb_ % NSL]
                            a_, aeb = ea4[hb_ % NSL]
                            R, Rb = k.psum()
                            for i in range(4):
                                k.mm(R[:, i * 128:(i + 1) * 128], dhl[:, 0, h0 + i:h0 + i + 1].to_broadcast([128, 128]),
                                     TRIb[:, :], start=True, stop=False, r=[dhlb, tribb], w=[Rb])
                                k.mm(R[:, i * 128:(i + 1) * 128], dhl[:, 1, h0 + i:h0 + i + 1].to_broadcast([128, 128]),
                                     TRIb[:, :], start=False, stop=True, r=[dhlb, tribb], w=[Rb])
                            Rv = R[:, :].rearrange("p (h q) -> p h q", q=128)
                            yield
                            k.tt("dve", e_[:, :, :], Rv, bc(nb[:, h0:h0 + 4], 2, 128), ALU.add, r=[Rb, nbb], w=[eb])
                            k.ts("dve", e_[:, :, :], e_[:, :, :], 20.0, None, ALU.min, ALU.bypass, r=[eb], w=[eb])
                            k.tt("dve", wraw[:, h0:h0 + 4], Rv[:, :, END], nacs[:, h0:h0 + 4], ALU.add, r=[Rb, nacsb], w=[wrb])
                            k.cp("dve", craw[:, h0:h0 + 4], Rv[:, :, END], r=[Rb], w=[crb])
                            k.act(a_[:, :, :], Rv, AF.Exp, r=[Rb], w=[aeb])
                            yield
                            k.act(e_[:, :, :], e_[:, :, :], AF.Exp, r=[eb], w=[eb])
                            k.tt("pool", CE[:, h0:h0 + 4, :], a_[:, :, :], bc(bcT[:, 4 + g, :], 1, 4), ALU.mult,
                                 r=[aeb, bcb], w=[CEbs[hb_]])
                            yield
                            k.tt("dve", MT[:, h0:h0 + 4, :], e_[:, :, :], bc(cbm[:, g, :], 1, 4), ALU.mult,
                                 r=[eb, cbmb], w=[MTbs[hb_]])

                        run_interleaved([bank_gen(hb_) for hb_ in range(8)], NSL)
                        k.act(wraw[:, :], wraw[:, :], AF.Exp, r=[wrb], w=[wrb])
                        k.tt("dve", wraw[:, :], wraw[:, :], dt_d, ALU.mult, r=[wrb, dtkb], w=[wrb])
                        k.act(craw[:, :], craw[:, :], AF.Exp, r=[crb], w=[crb])
                        for c4 in range(4):
                            po, pob = k.psum()
                            for cc in range(4):
                                c = c4 * 4 + cc
                                o_ = po[:, cc * 128:(cc + 1) * 128]
                                k.mm(o_, xz[:, 2 * c, :], MT[:, 2 * c, :], start=True, stop=False, r=[xzb, MTbs[c // 2]], w=[pob])
                                k.mm(o_, xz[:, 2 * c + 1, :], MT[:, 2 * c + 1, :], start=False, stop=False, r=[xzb, MTbs[c // 2]], w=[pob])
                                k.mm(o_, hz[:, 2 * c, :], CE[:, 2 * c, :], start=False, stop=False, r=[hzb, CEbs[c // 2]], w=[pob])
                                k.mm(o_, hz[:, 2 * c + 1, :], CE[:, 2 * c + 1, :], start=False, stop=True, r=[hzb, CEbs[c // 2]], w=[pob])
                            pov = po[:, :].rearrange("p (c q) -> p c q", q=128)
                            if not final:
                                k.cp("act", gt[:, c4 * 4:(c4 + 1) * 4, :], pov, r=[pob], w=[gtb])
                            else:
                                k.tt("dve", gt[:, c4 * 4:(c4 + 1) * 4, :], pov, stg[:, c4 * 4:(c4 + 1) * 4, :], ALU.add,
                                     r=[pob, stgb], w=[gtb])
                        if not final:
                            S.dma("sp", SCR["yf"][:, :, tsl].rearrange("c p t -> p c t"), gt[:, :, :], r=[gtb], w=[SCR["yfb"]])
                        else:
                            k.tt("pool", g16[:, :, :], xTt[:, :, :], bc(dcol[:, :], 2, 128), ALU.mult, r=[xTb, dcb], w=[g16b])
                            k.tt("dve", gt[:, :, :], gt[:, :, :], g16[:, :, :], ALU.add, r=[gtb, g16b], w=[gtb])
                            k.tt("dve", gt[:, :, :], gt[:, :, :], zst[:, :, :], ALU.mult, r=[gtb, zsb_], w=[gtb])
                            k.act(sqa[:, :, :], gt[:, :, :], AF.Square, r=[gtb], w=[sqab])
                            S.op("dve", lambda e: e.reduce_sum(sq[0][0][:, :], sqa[:, :, :].rearrange("p c q -> p q c"),
                                                               mybir.AxisListType.X), r=[sqab], w=[sq[0][1]])
                            pss, pssb = k.psum()
                            k.mm(pss[:, 0:128], C["ones"][:, :], sq[0][0][:, :], start=True, stop=True,
                                 r=[sq[0][1], C["b"]], w=[pssb])
                            k.ts("dve", rstd[:, :], pss[:, 0:128], 1.0 / 2048, 1e-6, ALU.mult, ALU.add, r=[pssb], w=[rstdb])
                            k.act(rstd[:, :], rstd[:, :], AF.Sqrt, r=[rstdb], w=[rstdb])
                            S.op("dve", lambda e: e.reciprocal(rstd[:, :], rstd[:, :]), r=[rstdb], w=[rstdb])
                            k.tt("dve", gt[:, :, :], gt[:, :, :], bc(rstd[:, :], 1, 16), ALU.mult, r=[gtb, rstdb], w=[gtb])
                            k.tt("pool", g16[:, :, :], gt[:, :, :], bc(normg[:, :], 2, 128), ALU.mult, r=[gtb, ngb], w=[g16b])
                            S.dma("sp", SCR["gt"][:, :, tsl].rearrange("c p t -> p c t"), g16[:, :, :], r=[g16b], w=[SCR["gtb"]])
                        k.tt("dve", xsv[:, :, 0, :], xzv[:, :, 0, 0:64], bc(wraw[:, 0:32:2], 2, 64), ALU.mult,
                             r=[xzb, wrb], w=[xsb])
                        k.tt("pool", xsv[:, :, 1, :], xzv[:, :, 1, 64:128], bc(wraw[:, 1:32:2], 2, 64), ALU.mult,
                             r=[xzb, wrb], w=[xsb])
                        k.tt("dve", hT_[:, :].rearrange("p (h f) -> p h f", f=64), hT_[:, :].rearrange("p (h f) -> p h f", f=64),
                             bc(craw[:, :], 2, 64), ALU.mult, r=[hTb, crb], w=[hTb])
                        for g in range(4):
                            pst, pstb = k.psum()
                            k.mm(pst[:, :], btok[:, g * 128:(g + 1) * 128], xs[:, g * 8:(g + 1) * 8, :].rearrange("p h f -> p (h f)"),
                                 start=True, stop=True, r=[btb, xsb], w=[pstb])
                            k.tt("dve", hT_[:, g * 512:(g + 1) * 512], hT_[:, g * 512:(g + 1) * 512], pst[:, :], ALU.add,
                                 r=[hTb, pstb], w=[hTb])
                        hz_refresh()
                    if grp == 0:
                        for c4 in range(4):
                            pt, pb = k.psum()
                            for cc in range(4):
                                c = c4 * 4 + cc
                                k.tr(pt[:, cc * 128:(cc + 1) * 128], hT_[:, c * 128:(c + 1) * 128], C["ident"][:, :],
                                     r=[hTb, C["b"]], w=[pb])
                            k.cp("act", stg[:, c4 * 4:(c4 + 1) * 4, :], pt[:, :].rearrange("p (c n) -> p c n", n=128),
                                 r=[pb], w=[stgb])
                        S.dma("sp", O["new_state_ssd"][sq_i, j, dirn].rearrange("h p n -> (h p) n")
                              .rearrange("(c q) n -> q c n", q=128), stg[:, :, :], r=[stgb])
                S.barrier()
        S.barrier()
    with ExitStack() as es:
        GT, GTb = k.sb(es, "sd_GT", [128, 16, T], BF16)
        S.dma("sp", GT[:, :, :], SCR["gt"][:, :, 0:T].rearrange("c p t -> p c t"), r=[SCR["gtb"]], w=[GTb])
        emit_mixer_tail(k, C, W["ssd_w_out"][j], 16, GT, GTb, yT, yb, T, mod["gate"], modb, lncols, lnb)


def emit_gdn(k, C, W, O, SCR, yT, yb, T, grp, mod, modb, lncols, lnb):
    S = k.S
    w_in = W["gdn_w_in"][0]
    nseq, L = (4, 256) if grp == 0 else (1, 2048)
    TPS = L // 128
    with ExitStack() as es:
        cw, cwb = k.sb(es, "gd_cw", [128, 160], F32)
        for half in range(2):
            emit_load_cols(k, cw[:, half * 80:(half + 1) * 80], cwb,
                           W["gdn_conv_w"][0].rearrange("k (c p) -> (k c) p", p=128)[half * 80:(half + 1) * 80, :], 80, C)
        cbias, cbb = k.sb(es, "gd_cb", [128, 32], F32)
        emit_load_cols(k, cbias[:, :], cbb, W["gdn_conv_b"][0].rearrange("(c p) -> c p", p=128), 32, C)
        normg, ngb = k.sb(es, "gd_ng", [128, 2], F32)
        emit_load_cols(k, normg[:, :], ngb, W["gdn_norm"][0].rearrange("(c p) -> c p", p=128), 2, C)
        dtb, dtbb = k.sb(es, "gd_dtb", [16, 2], F32)
        emit_load_cols(k, dtb[:, 0:1], dtbb, W["gdn_dt_bias"][0:1].rearrange("o d h -> o (d h)"), 1, C, Wd=16)
        emit_load_cols(k, dtb[:, 1:2], dtbb, W["gdn_a_log"][0:1].rearrange("o d h -> o (d h)"), 1, C, Wd=16)
        k.act(dtb[:, 1:2], dtb[:, 1:2], AF.Exp, r=[dtbb], w=[dtbb])
        k.ts("dve", dtb[:, 1:2], dtb[:, 1:2], -1.0, None, ALU.mult, ALU.bypass, r=[dtbb], w=[dtbb])
        betaT, betaTb = k.sb(es, "gd_betaT", [16, T], F32)
        gT_, gTb_ = k.sb(es, "gd_gT", [16, T], F32)

        with ExitStack() as es1:
            hT, hb = k.sb(es1, "gd_hT", [128, KC, T], BF16)
            emit_modulate(k, hT, hb, yT, yb, T, mod, modb)
            wbuf = [k.sb(es1, "gd_w%d" % i, [128, KC, 256], BF16) for i in range(2)]
            pad, padb = k.sb(es1, "gd_pad", [128, nseq, L + 4], F32)
            k.memset("pool", pad[:, :, :], 0.0, w=[padb])
            acc = [k.sb(es1, "gd_acc%d" % i, [128, nseq, L], F32) for i in range(2)]
            o32, o32b = k.sb(es1, "gd_o32", [128, T], F32)
            sq, sqb = k.sb(es1, "gd_sq", [128, 512], F32)
            rn, rnb = k.sb(es1, "gd_rn", [128, 512], F32)
            ob16 = [k.sb(es1, "gd_o%d" % i, [128, T], BF16) for i in range(2)]

            def loadw(gi):
                S.dma("pool", wbuf[gi % 2][0][:, :, :],
                      w_in[:, gi * 256:(gi + 1) * 256].rearrange("(kc p) o -> p kc o", p=128), w=[wbuf[gi % 2][1]])

            loadw(0)
            n = 0
            for gi in range(24):
                if gi + 1 < 24:
                    loadw(gi + 1)
                wt, wb = wbuf[gi % 2]
                for o2 in range(2):
                    oc = gi * 2 + o2
                    o16, o16b = ob16[n % 2]
                    ac, acb = acc[n % 2]
                    n += 1
                    for tg in range(T // 512):
                        ps, psb = k.psum()
                        for kc in range(KC):
                            k.mm(ps[:, :], wt[:, kc, o2 * 128:(o2 + 1) * 128], hT[:, kc, tg * 512:(tg + 1) * 512],
                                 start=(kc == 0), stop=(kc == KC - 1), r=[wb, hb], w=[psb])
                        if oc >= 32:
                            k.act(o16[:, tg * 512:(tg + 1) * 512], ps[:, :], AF.Silu, r=[psb], w=[o16b])
                        elif grp == 0:
                            k.cp("act", pad[:, tg * 2:(tg + 1) * 2, 2:2 + L],
                                 ps[:, :].rearrange("p (s l) -> p s l", l=L), r=[psb], w=[padb])
                        else:
                            k.cp("act", pad[:, 0, 2 + tg * 512:2 + (tg + 1) * 512], ps[:, :], r=[psb], w=[padb])
                    if oc >= 32:
                        S.dma("sp", SCR["zs"][oc - 32, :, 0:T], o16[:, :], r=[o16b], w=[SCR["zsb"]])
                        continue
                    c = oc
                    e1 = "dve" if c % 2 == 0 else "pool"
                    k.ts(e1, ac[:, :, :], pad[:, :, 0:L], cw[:, c:c + 1], cbias[:, c:c + 1], ALU.mult, ALU.add,
                         r=[padb, cwb, cbb], w=[acb])
                    for kk in range(1, 5):
                        k.stt(e1, ac[:, :, :], pad[:, :, kk:kk + L], cw[:, kk * 32 + c:kk * 32 + c + 1], ac[:, :, :],
                              ALU.mult, ALU.add, r=[padb, cwb, acb], w=[acb])
                    acf = ac[:, :, :].rearrange("p s l -> p (s l)")
                    if c >= 16:
                        k.act(o16[:, :], acf, AF.Silu, r=[acb], w=[o16b])
                    else:
                        k.act(o32[:, :], acf, AF.Silu, r=[acb], w=[o32b])
                        for tg in range(T // 512):
                            sl = slice(tg * 512, (tg + 1) * 512)
                            k.act(sq[:, :], o32[:, sl], AF.Square, r=[o32b], w=[sqb])
                            ps, psb = k.psum()
                            k.mm(ps[:, :], C["ones"][:, :], sq[:, :], start=True, stop=True, r=[sqb, C["b"]], w=[psb])
                            k.ts("dve", rn[:, :], ps[:, :], 1e-6, None, ALU.add, ALU.bypass, r=[psb], w=[rnb])
                            k.act(rn[:, :], rn[:, :], AF.Sqrt, r=[rnb], w=[rnb])
                            S.op("dve", lambda e: e.reciprocal(rn[:, :], rn[:, :]), r=[rnb], w=[rnb])
                            k.stt("dve", o16[:, sl], o32[:, sl], (128 ** -0.5) if c < 8 else 1.0, rn[:, :],
                                  ALU.mult, ALU.mult, r=[o32b, rnb], w=[o16b])
                    S.dma("sp", SCR["xc"][c, :, 0:T], o16[:, :], r=[o16b], w=[SCR["xcb"]])
            wab, wabb = k.sb(es1, "gd_wab", [128, KC, 32], BF16)
            S.dma("pool", wab[:, :, :], w_in[:, 6144:6176].rearrange("(kc p) o -> p kc o", p=128), w=[wabb])
            xs_, xsb_ = k.sb(es1, "gd_abx", [16, 512], F32)
            for tg in range(T // 512):
                sl = slice(tg * 512, (tg + 1) * 512)
                ps, psb = k.psum()
                for kc in range(KC):
                    k.mm(ps[0:16, :], wab[:, kc, 0:16], hT[:, kc, sl], start=(kc == 0), stop=(kc == KC - 1),
                         r=[wabb, hb], w=[psb])
                k.act(betaT[:, sl], ps[0:16, :], AF.Sigmoid, r=[psb], w=[betaTb])
                ps, psb = k.psum()
                for kc in range(KC):
                    k.mm(ps[0:16, :], wab[:, kc, 16:32], hT[:, kc, sl], start=(kc == 0), stop=(kc == KC - 1),
                         r=[wabb, hb], w=[psb])
                k.act(xs_[:, :], ps[0:16, :], AF.Identity, r=[psb, dtbb], w=[xsb_], bias=dtb[:, 0:1], scale=1.0)
                with ExitStack() as es2:
                    emit_softplus(k, C, gT_[:, sl], gTb_, xs_[:, :], xsb_, 16, 512, es2)
                    S.barrier()
            k.ts("dve", gT_[:, :], gT_[:, :], dtb[:, 1:2], None, ALU.mult, ALU.bypass, r=[gTb_, dtbb], w=[gTb_])
            S.barrier()

        for dirn in range(2):
            S.mark("  gdn sweep%d" % dirn)
            final = dirn == 1
            sfx = "_f" if dirn == 0 else "_b"
            TRIc = C["gtri" + sfx]
            NMSL = C["gnmsl" + sfx]
            with ExitStack() as es1:
                qkT, qkb = k.sb(es1, "gs_qkT", [128, 16, 128], BF16)
                vT, vTb = k.sb(es1, "gs_vT", [128, 16, 128], BF16)
                ktok, ktb = k.sb(es1, "gs_ktok", [128, 8, 128], BF16)
                vtok, vtb = k.sb(es1, "gs_vtok", [128, 8, 256], BF16)
                vb_, vbb = k.sb(es1, "gs_vb", [128, 8, 256], BF16)
                kbg, kbgb = k.sb(es1, "gs_kbg", [128, 8, 128], BF16)
                kdec, kdb = k.sb(es1, "gs_kdec", [128, 8, 128], BF16)
                bg, bgb = k.sb(es1, "gs_bg", [128, 64], F32)
                ghl, ghlb = k.sb(es1, "gs_ghl", [128, 2, 8], BF16)
                TRIcb, tricbb = k.sb(es1, "gs_tricb", [128, 128], BF16)
                k.cp("dve", TRIcb[:, :], TRIc[:, :], r=[C["b"]], w=[tricbb])
                sc, scb = k.sb(es1, "gs_sc", [128, 48], F32)
                rhsA = [k.sb(es1, "gs_rhsA%d" % i, [128, 4, 128], F32) for i in range(2)]
                d1 = [k.sb(es1, "gs_d1%d" % i, [128, 4, 128], F32) for i in range(2)]
                d2 = [k.sb(es1, "gs_d2%d" % i, [128, 4, 128], F32) for i in range(2)]
                kkms = [k.sb(es1, "gs_kkm%d" % q, [128, 4, 128], F32) for q in range(2)]
                CDT = BF16 if DBG.get("gdn_bf16", False) else F32
                Xs = [[k.sb(es1, "gs_X%d_%d" % (q, i), [128, 4, 128], CDT) for i in range(2)] for q in range(2)]
                XTs = [[k.sb(es1, "gs_XT%d_%d" % (q, i), [128, 4, 128], CDT) for i in range(2)] for q in range(2)]
                Paccs = [k.sb(es1, "gs_Pacc%d" % q, [128, 4, 128], F32) for q in range(2)]
                PaccBs = [k.sb(es1, "gs_PaccB%d" % q, [128, 4, 128], CDT) for q in range(2)] if CDT == BF16 else Paccs
                TTb, TTbb = k.sb(es1, "gs_TTb", [128, 8, 128], BF16)
                u, ub = k.sb(es1, "gs_u", [128, 8, 256], F32)
                wT, wTb = k.sb(es1, "gs_wT", [128, 8, 128], BF16)
                qkm, qkmb = k.sb(es1, "gs_qkm", [128, 8, 128], BF16)
                delta, dlb = k.sb(es1, "gs_delta", [128, 8, 256], BF16)
                o_, ob_ = k.sb(es1, "gs_o", [128, 8, 256], F32)
                dlbs = [Buf("dl%d" % i) for i in range(8)]
                obs = [Buf("o%d" % i) for i in range(8)]
                ubs = [Buf("u%d" % i) for i in range(2)]
                wTbs = [Buf("wT%d" % i) for i in range(2)]
                qkmbs = [Buf("qkm%d" % i) for i in range(2)]
                TTbbs = [Buf("TTb%d" % i) for i in range(2)]
                Sst = [k.sb(es1, "gs_S%d" % h, [128, 256], F32) for h in range(8)]
                Sbf = [k.sb(es1, "gs_Sb%d" % h, [128, 256], BF16) for h in range(8)]
                if final:
                    of_, ofb = k.sb(es1, "gs_of", [128, 8, 256], F32)
                    zst, zsb_ = k.sb(es1, "gs_zs", [128, 16, 128], BF16)
                    ss, ssb = k.sb(es1, "gs_ss", [128, 16], F32)
                    on16, onb = vb_, vbb
                    g16, g16b = k.sb(es1, "gs_g16", [128, 16, 128], BF16)

                if DBG.get("mem"):
                    print("GDN sweep", dirn, "grp", grp, "sbuf free", k.nc.sbuf_bytes_remaining)
                order = []
                for sq_i in range(nseq):
                    tl = list(range(sq_i * TPS, (sq_i + 1) * TPS))
                    order += tl[::-1] if dirn == 1 else tl
                NBF = 2 if (k.nc.sbuf_bytes_remaining > 8192 + 4096 and not DBG.get("nopf")) else 1
                qkTs = [(qkT, qkb)]
                vTs = [(vT, vTb)]
                if NBF == 2:
                    qkTs.append(k.sb(es1, "gs_qkT2", [128, 16, 128], BF16))
                    vTs.append(k.sb(es1, "gs_vT2", [128, 16, 128], BF16))

                def issue_loads(i):
                    tsl_ = slice(order[i] * 128, (order[i] + 1) * 128)
                    S.dma("sp", qkTs[i % NBF][0][:, :, :], SCR["xc"][0:16, :, tsl_].rearrange("c p t -> p c t"),
                          r=[SCR["xcb"]], w=[qkTs[i % NBF][1]])
                    S.dma("sp", vTs[i % NBF][0][:, :, :], SCR["xc"][16:32, :, tsl_].rearrange("c p t -> p c t"),
                          r=[SCR["xcb"]], w=[vTs[i % NBF][1]])

                issue_loads(0)
                gi = 0
                for sq_i in range(nseq):
                    for h in range(8):
                        if grp == 0:
                            k.memset("dve" if h % 2 else "pool", Sst[h][0][:, :], 0.0, w=[Sst[h][1]])
                        else:
                            S.dma("sp", Sst[h][0][:, :], W["state_delta"][0, dirn, h], w=[Sst[h][1]])
                        k.cp("act", Sbf[h][0][:, :], Sst[h][0][:, :], r=[Sst[h][1]], w=[Sbf[h][1]])
                    tiles = list(range(sq_i * TPS, (sq_i + 1) * TPS))
                    if dirn == 1:
                        tiles = tiles[::-1]
                    for tt in tiles:
                        tsl = slice(tt * 128, (tt + 1) * 128)
                        if NBF == 1:
                            if gi > 0:
                                issue_loads(gi)
                        elif gi + 1 < len(order):
                            issue_loads(gi + 1)
                        qkT, qkb = qkTs[gi % NBF]
                        vT, vTb = vTs[gi % NBF]
                        gi += 1
                        if final:
                            S.dma("sp", zst[:, :, :], SCR["zs"][:, :, tsl].rearrange("c p t -> p c t"), r=[SCR["zsb"]], w=[zsb_])
                            S.dma("sp", of_[:, :, :], SCR["of"][tsl, :].rearrange("t (h v) -> t h v", v=256),
                                  r=[SCR["ofb"]], w=[ofb])
                        pt, pb = k.psum()
                        ptv = pt[:, :].bitcast(BF16)
                        for h in range(8):
                            k.tr(ptv[:, h * 128:(h + 1) * 128], qkT[:, 8 + h, :], C["identb"][:, :], r=[qkb, C["b"]], w=[pb])
                        k.cp("dve", ktok[:, :, :], ptv[:, 0:1024].rearrange("p (h f) -> p h f", f=128), r=[pb], w=[ktb])
                        for half in range(2):
                            pt, pb = k.psum()
                            ptv = pt[:, :].bitcast(BF16)
                            for c8 in range(8):
                                k.tr(ptv[:, c8 * 128:(c8 + 1) * 128], vT[:, half * 8 + c8, :], C["identb"][:, :],
                                     r=[vTb, C["b"]], w=[pb])
                            k.cp("act", vtok[:, half * 4:(half + 1) * 4, :],
                                 ptv[:, 0:1024].rearrange("p (h v) -> p h v", v=256), r=[pb], w=[vtb])
                        pt, pb = k.psum()
                        k.tr(pt[:, 0:16], betaT[0:16, tsl], C["ident"][0:16, 0:16], r=[betaTb, C["b"]], w=[pb])
                        k.tr(pt[:, 16:32], gT_[0:16, tsl], C["ident"][0:16, 0:16], r=[gTb_, C["b"]], w=[pb])
                        k.cp("act", bg[:, 0:32], pt[:, 0:32], r=[pb], w=[bgb])
                        beta_d = bg[:, dirn * 8:(dirn + 1) * 8]
                        g_d = bg[:, 16 + dirn * 8:16 + (dirn + 1) * 8]
                        k.cp("dve", ghl[:, 0, :], g_d, r=[bgb], w=[ghlb])
                        k.tt("dve", ghl[:, 1, :], g_d, ghl[:, 0, :], ALU.subtract, r=[bgb, ghlb], w=[ghlb])
                        pt, pb = k.psum()
                        k.mm(pt[:, 0:8], TRIc[:, :], g_d, start=True, stop=True, r=[bgb, C["b"]], w=[pb])
                        k.mm(pt[:, 8:16], C["gblk"][:, :], g_d, start=True, stop=True, r=[bgb, C["b"]], w=[pb])
                        k.cp("dve", bg[:, 32:40], pt[:, 0:8], r=[pb], w=[bgb])
                        k.ts("dve", bg[:, 40:48], pt[:, 0:8], -1.0, None, ALU.mult, ALU.bypass, r=[pb], w=[bgb])
                        k.tt("dve", sc[:, 16:24], pt[:, 8:16], bg[:, 40:48], ALU.add, r=[pb, bgb], w=[scb])
                        k.act(sc[:, 16:24], sc[:, 16:24], AF.Exp, r=[scb], w=[scb])
                        k.act(bg[:, 48:56], bg[:, 32:40], AF.Exp, r=[bgb], w=[bgb])
                        k.act(bg[:, 56:64], beta_d, AF.Ln, r=[bgb], w=[bgb])
                        k.tt("dve", bg[:, 56:64], bg[:, 56:64], bg[:, 32:40], ALU.add, r=[bgb], w=[bgb])
                        k.tt("dve", sc[:, 8:16], beta_d, bg[:, 48:56], ALU.mult, r=[bgb], w=[scb])
                        ra, rab = rhsA[0]
                        rav = ra[:, 0, 0:16].rearrange("p (h c) -> p h c", c=2)
                        k.tt("dve", rav, bc(g_d, 2, 2), bc(C["gch"][:, 0:2], 1, 8), ALU.mult, r=[bgb, C["b"]], w=[rab])
                        pt, pb = k.psum()
                        k.mm(pt[:, 0:16], C["ones"][:, :], ra[:, 0, 0:16], start=True, stop=True, r=[rab, C["b"]], w=[pb])
                        k.act(sc[:, 24:40], pt[:, 0:16], AF.Exp, r=[pb], w=[scb])
                        k.tt("pool", vb_[:, :, :], vtok[:, :, :], bc(beta_d, 2, 256), ALU.mult, r=[vtb, bgb], w=[vbb])
                        k.tt("dve", kbg[:, :, :], ktok[:, :, :], bc(sc[:, 8:16], 2, 128), ALU.mult, r=[ktb, scb], w=[kbgb])
                        k.tt("pool", kdec[:, :, :], ktok[:, :, :], bc(sc[:, 16:24], 2, 128), ALU.mult, r=[ktb, scb], w=[kdb])
                        def quad_gen(qd):
                            h0 = qd * 4
                            ra, rab = rhsA[qd]
                            d1_, d1b = d1[qd]
                            d2_, d2b = d2[qd]
                            kkm, kkmb = kkms[qd]
                            Pacc, Pab = Paccs[qd]
                            X = Xs[qd]
                            XT = XTs[qd]
                            R, Rb = k.psum()
                            for i in range(4):
                                k.mm(R[:, i * 128:(i + 1) * 128], ghl[:, 0, h0 + i:h0 + i + 1].to_broadcast([128, 128]),
                                     TRIcb[:, :], start=True, stop=False, r=[ghlb, tricbb], w=[Rb])
                                k.mm(R[:, i * 128:(i + 1) * 128], ghl[:, 1, h0 + i:h0 + i + 1].to_broadcast([128, 128]),
                                     TRIcb[:, :], start=False, stop=True, r=[ghlb, tricbb], w=[Rb])
                            Rv = R[:, :].rearrange("p (h q) -> p h q", q=128)
                            pk, pkb = k.psum()
                            for i in range(4):
                                k.mm(pk[:, i * 128:(i + 1) * 128], qkT[:, 8 + h0 + i, :], qkT[:, 8 + h0 + i, :],
                                     start=True, stop=True, r=[qkb], w=[pkb])
                            yield
                            k.tt("dve", d1_[:, :, :], Rv, bc(bg[:, 40 + h0:44 + h0], 2, 128), ALU.add, r=[Rb, bgb], w=[d1b])
                            k.ts("dve", d1_[:, :, :], d1_[:, :, :], 0.0, None, ALU.min, ALU.bypass, r=[d1b], w=[d1b])
                            k.stt("dve", d2_[:, :, :], Rv, -1.0, bc(bg[:, 56 + h0:60 + h0], 2, 128), ALU.mult, ALU.add,
                                  r=[Rb, bgb], w=[d2b])
                            k.ts("dve", d2_[:, :, :], d2_[:, :, :], 0.0, None, ALU.min, ALU.bypass, r=[d2b], w=[d2b])
                            k.tt("dve", kkm[:, :, :], pk[:, :].rearrange("p (h q) -> p h q", q=128), bc(NMSL[:, :], 1, 4),
                                 ALU.mult, r=[pkb, C["b"]], w=[kkmb])
                            yield
                            k.act(d2_[:, :, :], d2_[:, :, :], AF.Exp, r=[d2b], w=[d2b])
                            k.act(d1_[:, :, :], d1_[:, :, :], AF.Exp, r=[d1b], w=[d1b])
                            k.tt("pool", d1_[:, :, :], d1_[:, :, :], bc(TRIc[:, :], 1, 4), ALU.mult, r=[d1b, C["b"]], w=[d1b])
                            yield
                            xt, xtb = XT[0]
                            x_, xb_ = X[0]
                            k.tt("dve", xt[:, :, :], d2_[:, :, :], kkm[:, :, :], ALU.mult, r=[d2b, kkmb], w=[xtb])
                            pn, pnb = k.psum()
                            if CDT == BF16:
                                pnf = pn[:, :].bitcast(BF16)
                                for i in range(4):
                                    k.tr(pnf[:, i * 128:(i + 1) * 128], xt[:, i, :], C["identb"][:, :], r=[xtb, C["b"]], w=[pnb])
                                pnv = pnf[:, 0:512].rearrange("p (h q) -> p h q", q=128)
                            else:
                                for i in range(4):
                                    k.tr(pn[:, i * 128:(i + 1) * 128], xt[:, i, :], C["ident"][:, :], r=[xtb, C["b"]], w=[pnb])
                                pnv = pn[:, :].rearrange("p (h q) -> p h q", q=128)
                            PaccB, PaBb = PaccBs[qd]
                            yield
                            k.cp("act", x_[:, :, :], pnv, r=[pnb], w=[xb_])
                            k.tt("dve", Pacc[:, :, :], pnv, bc(C["ident"][:, :], 1, 4), ALU.add, r=[pnb, C["b"]], w=[Pab])
                            if CDT == BF16:
                                k.cp("act", PaccB[:, :, :], Pacc[:, :, :], r=[Pab], w=[PaBb])
                            cur = 0
                            for lvl in range(1, 6):
                                x_, xb_ = X[cur]
                                xt, xtb = XT[cur]
                                x2, x2b = X[1 - cur]
                                xt2, xt2b = XT[1 - cur]
                                p1, p1b = k.psum()
                                for i in range(4):
                                    k.mm(p1[:, i * 128:(i + 1) * 128], x_[:, i, :], xt[:, i, :], start=True, stop=True,
                                         r=[xb_, xtb], w=[p1b])
                                if lvl < 5:
                                    p2, p2b = k.psum()
                                    for i in range(4):
                                        k.mm(p2[:, i * 128:(i + 1) * 128], xt[:, i, :], x_[:, i, :], start=True, stop=True,
                                             r=[xb_, xtb], w=[p2b])
                                yield
                                k.cp("act", xt2[:, :, :], p1[:, :].rearrange("p (h q) -> p h q", q=128), r=[p1b], w=[xt2b])
                                if lvl < 5:
                                    k.cp("dve", x2[:, :, :], p2[:, :].rearrange("p (h q) -> p h q", q=128), r=[p2b], w=[x2b])
                                yield
                                p3, p3b = k.psum()
                                for i in range(4):
                                    k.mm(p3[:, i * 128:(i + 1) * 128], xt2[:, i, :], PaccB[:, i, :], start=True, stop=True,
                                         r=[xt2b, PaBb], w=[p3b])
                                yield
                                k.tt("dve", Pacc[:, :, :], Pacc[:, :, :], p3[:, :].rearrange("p (h q) -> p h q", q=128),
                                     ALU.add, r=[Pab, p3b], w=[Pab])
                                if lvl < 5 and CDT == BF16:
                                    k.cp("act", PaccB[:, :, :], Pacc[:, :, :], r=[Pab], w=[PaBb])
                                cur = 1 - cur
                            k.cp("act", TTb[:, h0:h0 + 4, :], Pacc[:, :, :], r=[Pab], w=[TTbbs[qd]])
                            pq, pqb = k.psum()
                            for i in range(4):
                                k.mm(pq[:, i * 128:(i + 1) * 128], qkT[:, 8 + h0 + i, :], qkT[:, h0 + i, :], start=True, stop=True,
                                     r=[qkb], w=[pqb])
                            yield
                            k.tt("dve", qkm[:, h0:h0 + 4, :], pq[:, :].rearrange("p (h q) -> p h q", q=128), d1_[:, :, :],
                                 ALU.mult, r=[pqb, d1b], w=[qkmbs[qd]])
                            for i2 in range(2):
                                pu, pub = k.psum()
                                for i in range(2):
                                    h = h0 + i2 * 2 + i
                                    k.mm(pu[:, i * 256:(i + 1) * 256], TTb[:, h, :], vb_[:, h, :], start=True, stop=True,
                                         r=[TTbbs[qd], vbb], w=[pub])
                                k.cp("act", u[:, h0 + i2 * 2:h0 + i2 * 2 + 2, :], pu[:, :].rearrange("p (h v) -> p h v", v=256),
                                     r=[pub], w=[ubs[qd]])
                            pw, pwb = k.psum()
                            for i in range(4):
                                k.mm(pw[:, i * 128:(i + 1) * 128], kbg[:, h0 + i, :], TTb[:, h0 + i, :], start=True, stop=True,
                                     r=[kbgb, TTbbs[qd]], w=[pwb])
                            yield
                            k.cp("act", wT[:, h0:h0 + 4, :], pw[:, :].rearrange("p (h q) -> p h q", q=128), r=[pwb], w=[wTbs[qd]])

                        run_interleaved([quad_gen(0), quad_gen(1)], 2)
                        chunks = [0, 1] if dirn == 0 else [1, 0]

                        def head_gen(h):
                            St, Stb = Sst[h]
                            Sb, Sbb = Sbf[h]
                            qd = h // 4
                            for ci in chunks:
                                rs = slice(ci * 64, (ci + 1) * 64)
                                pa, pab_ = k.psum()
                                k.mm(pa[:, 0:256], wT[:, h, :], Sb[:, :], start=True, stop=True, r=[wTbs[qd], Sbb], w=[pab_])
                                k.mm(pa[:, 256:512], qkT[:, h, :], Sb[:, :], start=True, stop=True, r=[qkb, Sbb], w=[pab_])
                                yield
                                k.tt("dve", delta[rs, h, :], u[rs, h, :], pa[rs, 0:256], ALU.subtract, r=[ubs[qd], pab_], w=[dlbs[h]])
                                k.act(o_[rs, h, :], pa[rs, 256:512], AF.Identity, r=[pab_, bgb], w=[obs[h]],
                                      scale=bg[rs, 48 + h:49 + h])
                                yield
                                pS, pSb = k.psum()
                                k.mm(pS[:, 0:256], kdec[rs, h, :], delta[rs, h, :], start=True, stop=True, r=[kdb, dlbs[h]], w=[pSb])
                                yield
                                k.stt("dve", St[:, :], St[:, :], sc[:, 24 + 2 * h + ci:25 + 2 * h + ci], pS[:, 0:256],
                                      ALU.mult, ALU.add, r=[Stb, scb, pSb], w=[Stb])
                                k.cp("act", Sb[:, :], St[:, :], r=[Stb], w=[Sbb])
                                yield
                            po, pob = k.psum()
                            k.mm(po[:, 0:256], qkm[:, h, :], delta[:, h, :], start=True, stop=True, r=[qkmbs[qd], dlbs[h]], w=[pob])
                            yield
                            k.tt("dve", o_[:, h, :], o_[:, h, :], po[:, 0:256], ALU.add, r=[obs[h], pob], w=[obs[h]])

                        run_interleaved([head_gen(h) for h in range(8)], 4)
                        if not final:
                            S.dma("sp", SCR["of"][tsl, :].rearrange("t (h v) -> t h v", v=256), o_[:, :, :],
                                  r=obs, w=[SCR["ofb"]])
                        else:
                            k.tt("dve", o_[:, :, :], o_[:, :, :], of_[:, :, :], ALU.add, r=obs + [ofb], w=obs)
                            k.tt("pool", of_[:, :, :], o_[:, :, :], o_[:, :, :], ALU.mult, r=obs, w=[ofb])
                            S.op("dve", lambda e: e.reduce_sum(ss[:, 0:8], of_[:, :, :], mybir.AxisListType.X), r=[ofb], w=[ssb])
                            k.ts("dve", ss[:, 8:16], ss[:, 0:8], 1.0 / 256, 1e-6, ALU.mult, ALU.add, r=[ssb], w=[ssb])
                            k.act(ss[:, 8:16], ss[:, 8:16], AF.Sqrt, r=[ssb], w=[ssb])
                            S.op("dve", lambda e: e.reciprocal(ss[:, 8:16], ss[:, 8:16]), r=[ssb], w=[ssb])
                            k.tt("dve", on16[:, :, :], o_[:, :, :], bc(ss[:, 8:16], 2, 256), ALU.mult, r=obs + [ssb], w=[onb])
                            for half in range(2):
                                pt, pb = k.psum()
                                ptv = pt[:, :].bitcast(BF16)
                                for c8 in range(8):
                                    c = half * 8 + c8
                                    k.tr(ptv[:, c8 * 128:(c8 + 1) * 128], on16[:, c // 2, (c % 2) * 128:(c % 2 + 1) * 128],
                                         C["identb"][:, :], r=[onb, C["b"]], w=[pb])
                                for c8 in range(8):
                                    c = half * 8 + c8
                                    k.stt("dve", g16[:, c, :], ptv[:, c8 * 128:(c8 + 1) * 128], normg[:, c % 2:c % 2 + 1],
                                          zst[:, c, :], ALU.mult, ALU.mult, r=[pb, ngb, zsb_], w=[g16b])
                            S.dma("sp", SCR["gt"][:, :, tsl].rearrange("c p t -> p c t"), g16[:, :, :], r=[g16b], w=[SCR["gtb"]])
                    if grp == 0:
                        for h in range(8):
                            S.dma("sp", O["new_state_delta"][sq_i, 0, dirn, h], Sst[h][0][:, :], r=[Sst[h][1]])
                S.barrier()
        S.barrier()
    with ExitStack() as es:
        GT, GTb = k.sb(es, "gd_GT", [128, 16, T], BF16)
        S.dma("sp", GT[:, :, :], SCR["gt"][:, :, 0:T].rearrange("c p t -> p c t"), r=[SCR["gtb"]], w=[GTb])
        emit_mixer_tail(k, C, W["gdn_w_out"][0], 16, GT, GTb, yT, yb, T, mod["gate"], modb, lncols, lnb)


IN_SPECS = {
    "x_prompt": (TP, D), "x_sample": (TS, D), "c": (1, D), "c_ctx": (D,),
    "state_ssd": (2, 2, 32, 64, 128), "state_delta": (1, 2, 8, 128, 256),
    "cache_k": (256, 256), "cache_v": (256, 256),
    "w_mod": (DEPTH, D, 9 * D), "b_mod": (DEPTH, 9 * D), "ln_g": (DEPTH, 3, D), "ln_b": (DEPTH, 3, D),
    "ffn_w_gate": (DEPTH, 2, D, DFF), "ffn_w_up": (DEPTH, 2, D, DFF), "ffn_w_down": (DEPTH, 2, DFF, D),
    "ssd_w_in": (2, D, 5184), "ssd_conv_w": (2, 5, 3072), "ssd_conv_b": (2, 3072), "ssd_dt_bias": (2, 2, 32),
    "ssd_a_log": (2, 2, 32), "ssd_d": (2, 32), "ssd_norm": (2, 2048), "ssd_w_out": (2, 2048, D),
    "gdn_w_in": (1, D, 6176), "gdn_conv_w": (1, 5, 4096), "gdn_conv_b": (1, 4096), "gdn_dt_bias": (1, 2, 8),
    "gdn_a_log": (1, 2, 8), "gdn_norm": (1, 256), "gdn_w_out": (1, 2048, D),
    "attn_w_in": (1, D, 1536), "attn_sink": (1, 16), "attn_w_out": (1, D, D),
}
OUT_SPECS = {
    "y_prompt": (TP, D), "y_sample": (TS, D),
    "new_state_ssd": (4, 2, 2, 32, 64, 128), "new_state_delta": (4, 1, 2, 8, 128, 256),
    "new_cache_k": (TP, 256), "new_cache_v": (TP, 256),
}
MIXER_OF_LAYER = {0: "ssd", 1: "gdn", 2: "attn", 3: "ssd"}


def build_program(cfg):
    nc = bass.Bass("TRN2", target_bir_lowering=False)
    names = cfg.get("inputs", list(IN_SPECS))
    W = {}
    for name in names:
        W[name] = nc.dram_tensor(name, list(IN_SPECS[name]), F32, kind="ExternalInput").ap()
    hc = host_consts()
    for name, arr in hc.items():
        W["c_" + name] = nc.dram_tensor("c_" + name, list(arr.shape), F32, kind="ExternalInput").ap()
    O = {}
    for name in cfg.get("outputs", list(OUT_SPECS)):
        O[name] = nc.dram_tensor(name, list(OUT_SPECS[name]), F32, kind="ExternalOutput").ap()
    SCR = {}
    for nm, shp, dt_ in (("xc", [32, 128, TS], BF16), ("zs", [16, 128, TS], BF16), ("yf", [16, 128, TS], F32),
                         ("gt", [16, 128, TS], BF16), ("of", [TS, 2048], F32)):
        SCR[nm] = nc.dram_tensor("scr_" + nm, shp, dt_).ap()
        SCR[nm + "b"] = Buf("scr_" + nm)
    layers = cfg.get("layers", list(range(DEPTH)))
    stages = cfg.get("stages", (0, 1, 2))
    mixer = dict(MIXER_OF_LAYER)
    mixer.update(cfg.get("mixer", {}))

    with ExitStack() as es:
        S = Sy(nc, es)
        k = K(nc, es, S)
        C = {"b": Buf("consts")}
        for name, arr in hc.items():
            if name in ("cos", "sin"):
                continue
            t, _ = k.sb(es, "k_" + name, list(arr.shape), F32)
            C[name] = t
            S.dma("sp", t[:, :], W["c_" + name][:, :], w=[C["b"]])
        idb, _ = k.sb(es, "k_identb", [128, 128], BF16)
        C["identb"] = idb
        S.dma("pool", idb[:, :], W["c_ident"][:, :], w=[C["b"]])
        onb_, _ = k.sb(es, "k_onesb", [128, 128], BF16)
        C["onesb"] = onb_
        S.dma("pool", onb_[:, :], W["c_ones"][:, :], w=[C["b"]])
        S.barrier()

        condT, condb = k.sb(es, "condT", [128, KC, 2], BF16)
        with ExitStack() as es1:
            cf, cfb = k.sb(es1, "cond_f", [128, 16], F32)
            st, stb = k.sb(es1, "cond_st", [16, 128], F32)
            S.dma("sp", st[0:8, :], W["c_ctx"].rearrange("(r p) -> r p", p=128), w=[stb])
            S.dma("sp", st[8:16, :], W["c"][0].rearrange("(r p) -> r p", p=128), w=[stb])
            pt, pb = k.psum()
            k.tr(pt[:, 0:16], st[0:16, :], C["ident"][0:16, 0:16], r=[stb, C["b"]], w=[pb])
            k.act(cf[:, :], pt[:, 0:16], AF.Silu, r=[pb], w=[cfb])
            k.cp("dve", condT[:, :, :], cf[:, :].rearrange("p (g c) -> p c g", g=2), r=[cfb], w=[condb])
            S.barrier()

        modT, modb = k.sb(es, "modT", [128, DEPTH, 2, 72], F32)
        for l in layers:
            S.mark("adaln%d" % l)
            emit_adaln(k, C, W, condT, condb, modT, modb, l)
        for l in layers:
            for g in range(2):
                for srow in (1, 4, 7):
                    k.ts("dve", modT[:, l, g, srow * 8:(srow + 1) * 8], modT[:, l, g, srow * 8:(srow + 1) * 8],
                         1.0, None, ALU.add, ALU.bypass, r=[modb], w=[modb])
                for srow, f in ((2, 0.5 / ALPHA), (5, 1.0 / ALPHA), (8, 0.5 / ALPHA)):
                    k.ts("dve", modT[:, l, g, srow * 8:(srow + 1) * 8], modT[:, l, g, srow * 8:(srow + 1) * 8],
                         f, None, ALU.mult, ALU.bypass, r=[modb], w=[modb])
        lnT, lnb = k.sb(es, "lnT", [128, DEPTH * 3 * 2 * 8], F32)
        emit_load_cols(k, lnT[:, 0:96], lnb, W["ln_g"].rearrange("l s (r p) -> (l s r) p", p=128), 96, C)
        emit_load_cols(k, lnT[:, 96:192], lnb, W["ln_b"].rearrange("l s (r p) -> (l s r) p", p=128), 96, C)

        def lncols(l, i):
            o = (l * 3 + i) * 8
            return (lnT[:, o:o + 8], lnT[:, 96 + o:96 + o + 8])

        def modcols(l, g, s3):
            return {"shift": modT[:, l, g, (3 * s3) * 8:(3 * s3 + 1) * 8],
                    "sc1": modT[:, l, g, (3 * s3 + 1) * 8:(3 * s3 + 2) * 8],
                    "gate": modT[:, l, g, (3 * s3 + 2) * 8:(3 * s3 + 3) * 8]}

        yT, yb = k.sb(es, "yT", [128, KC, TS], F32)

        for g, (T, xname, oname) in enumerate(((TP, "x_prompt", "y_prompt"), (TS, "x_sample", "y_sample"))):
            if g not in cfg.get("groups", (0, 1)):
                continue
            S.mark("g%d loadx" % g)
            emit_load_x(k, C, yT, yb, W[xname], T)
            for l in layers:
                if 0 in stages:
                    for t0 in range(0, T, 1024):
                        S.mark("g%d l%d ffn0 t%d" % (g, l, t0))
                        emit_ffn(k, C, W, yT, yb, t0, l, 0, modcols(l, g, 0), modb, lncols(l, 0), lnb)
                if 1 in stages:
                    kind = mixer[l]
                    S.mark("g%d l%d mixer %s" % (g, l, kind))
                    if kind == "attn":
                        emit_attn(k, C, W, O, yT, yb, T, g, modcols(l, g, 1), modb, lncols(l, 1), lnb)
                    elif kind == "ssd":
                        emit_ssd(k, C, W, O, SCR, yT, yb, T, g, l // 3, modcols(l, g, 1), modb, lncols(l, 1), lnb)
                    elif kind == "gdn":
                        emit_gdn(k, C, W, O, SCR, yT, yb, T, g, modcols(l, g, 1), modb, lncols(l, 1), lnb)
                if 2 in stages:
                    for t0 in range(0, T, 1024):
                        S.mark("g%d l%d ffn1 t%d" % (g, l, t0))
                        emit_ffn(k, C, W, yT, yb, t0, l, 1, modcols(l, g, 2), modb, lncols(l, 2), lnb)
            S.mark("g%d store" % g)
            emit_store_y(k, C, yT, yb, O[oname], T)
        S.mark("end")
        S.final_wait()
        global LAST_MARKS
        LAST_MARKS = S.marks
        print("instructions:", S.ninst)
    return nc


def make_in_maps(inputs, names):
    hc = host_consts()
    maps = []
    for i in range(NCORES):
        m = {}
        for name in names:
            a = inputs[name]
            if name == "x_prompt":
                a = a[4 * i:4 * i + 4].reshape(TP, D)
            elif name == "x_sample":
                a = a[i].reshape(TS, D)
            elif name == "c":
                a = a[i:i + 1]
            elif name in ("state_ssd", "state_delta"):
                a = a[i]
            elif name in ("cache_k", "cache_v"):
                a = a[i, 0].reshape(256, 256)
            m[name] = np.ascontiguousarray(a, dtype=np.float32)
        for name, arr in hc.items():
            m["c_" + name] = arr
        maps.append(m)
    return maps


def run(inputs, cfg):
    nc = build_program(cfg)
    maps = make_in_maps(inputs, cfg.get("inputs", list(IN_SPECS)))
    res = run_bass_kernel_spmd(nc, maps, core_ids=list(range(NCORES)))
    return res.results


def kernel(**inputs):
    inputs = {k_: np.asarray(v) for k_, v in inputs.items()}
    res = run(inputs, {})
    y_prompt = np.concatenate([r["y_prompt"].reshape(4, 256, D) for r in res], axis=0)
    y_sample = np.stack([r["y_sample"].reshape(TS, D) for r in res], axis=0)
    nss = np.concatenate([r["new_state_ssd"] for r in res], axis=0)
    nsd = np.concatenate([r["new_state_delta"] for r in res], axis=0)
    nck = np.concatenate([r["new_cache_k"].reshape(4, 1, 256, 4, 64) for r in res], axis=0)
    ncv = np.concatenate([r["new_cache_v"].reshape(4, 1, 256, 4, 64) for r in res], axis=0)
    return (y_prompt, y_sample, nss, nsd, nck, ncv)
```
